# Optimizing a Trainium2 kernel written in Bass

```python
import math
import jax, jax.numpy as jnp
from jax import lax
import numpy as np

D_MODEL = 1024
BATCH = 8
SEQ = 8192
DEPTH = 4

N_MIXERS = 2
BRANCH = D_MODEL
SSM_GROUP = 16
SSM_GROUPS = BRANCH // SSM_GROUP
SSM_STATE = 64
SSM_CHUNK = 128
DT_MIN = 1e-3
DT_MAX = 1e-1
HEAD_DIM = 64
N_Q_HEADS = BRANCH // HEAD_DIM
N_KV_HEADS = 2
GQA_GROUP = N_Q_HEADS // N_KV_HEADS
WINDOW = 128
ATTN_BLOCK = 128
Q_DIM = N_Q_HEADS * HEAD_DIM
KV_DIM = N_KV_HEADS * HEAD_DIM
ROPE_THETA = 10000.0
NORM_EPS = 1e-5
NEG_INF = -1e30

kernel_name = "hybrid_s5_swa_sink_trunk"


def _rmsnorm(x, g):
    xf = x.astype(jnp.float32)
    y = xf * lax.rsqrt(jnp.mean(xf * xf, axis=-1, keepdims=True) + NORM_EPS)
    return (y * g.astype(jnp.float32)).astype(x.dtype)


def _ssm_combine(e1, e2):
    a1, b1 = e1
    a2, b2 = e2
    return a1 * a2, a2 * b1 + b2


def _s5_scan(u, a_re, a_im, log_step, b_re, b_im, c_re, c_im, d):
    bsz, seq, _ = u.shape
    f32 = jnp.float32
    u = u.astype(f32)
    lam = lax.complex(a_re.astype(f32), a_im.astype(f32))
    step = jnp.exp(log_step.astype(f32))[:, None]
    a_bar = jnp.exp(lam * step)
    b = lax.complex(b_re.astype(f32), b_im.astype(f32))
    b_bar = ((a_bar - 1.0) / lam)[..., None] * b
    c = lax.complex(c_re.astype(f32), c_im.astype(f32))
    n_chunks = seq // SSM_CHUNK
    u_c = u.reshape(bsz, n_chunks, SSM_CHUNK, SSM_GROUPS, SSM_GROUP).transpose(1, 2, 0, 3, 4)

    def step_fn(h_prev, u_blk):
        bu = jnp.einsum("tbgc,gpc->tbgp", u_blk.astype(b_bar.dtype), b_bar)
        a = jnp.broadcast_to(a_bar, bu.shape)
        a_cum, h_loc = lax.associative_scan(_ssm_combine, (a, bu), axis=0)
        h = h_loc + a_cum * h_prev[None]
        y = jnp.real(jnp.einsum("tbgp,gcp->tbgc", h, c))
        return h[-1], y

    h0 = jnp.zeros((bsz, SSM_GROUPS, SSM_STATE), dtype=b_bar.dtype)
    _, y = lax.scan(step_fn, h0, u_c)
    y = y.transpose(2, 0, 1, 3, 4).reshape(bsz, seq, BRANCH)
    return y + d.astype(f32) * u


def _ssm_layer(x, norm, w_in, a_re, a_im, log_step, b_re, b_im, c_re, c_im, d, w_glu, b_glu, w_out):
    f32 = jnp.float32
    h = _rmsnorm(x, norm)
    proj = h @ w_in
    u, gate = jnp.split(proj, [BRANCH], axis=-1)
    y = _s5_scan(u, a_re, a_im, log_step, b_re, b_im, c_re, c_im, d)
    z = jax.nn.gelu(y)
    z = z * jax.nn.sigmoid(z @ w_glu.astype(f32) + b_glu.astype(f32))
    out = (z * jax.nn.silu(gate.astype(f32))).astype(x.dtype) @ w_out
    return x + out


def _rope(t, cos, sin):
    t1, t2 = jnp.split(t, 2, axis=-1)
    return jnp.concatenate([t1 * cos - t2 * sin, t2 * cos + t1 * sin], axis=-1)


def _swa_sinks(q, k, v, sinks):
    f32 = jnp.float32
    bsz, seq = q.shape[:2]
    nb = seq // ATTN_BLOCK
    qb = q.reshape(bsz, nb, ATTN_BLOCK, N_KV_HEADS, GQA_GROUP, HEAD_DIM)
    kb = k.reshape(bsz, nb, ATTN_BLOCK, N_KV_HEADS, HEAD_DIM)
    vb = v.reshape(bsz, nb, ATTN_BLOCK, N_KV_HEADS, HEAD_DIM)

    def with_prev(t):
        prev = jnp.concatenate([jnp.zeros_like(t[:, :1]), t[:, :-1]], axis=1)
        return jnp.concatenate([prev, t], axis=2)

    kk = with_prev(kb)
    vv = with_prev(vb)
    s = jnp.einsum("bnqhgd,bnkhd->bnhgqk", qb, kk) * (HEAD_DIM ** -0.5)
    qi = jnp.arange(ATTN_BLOCK)[:, None]
    kj = jnp.arange(2 * ATTN_BLOCK)[None, :]
    dist = qi + ATTN_BLOCK - kj
    band = (dist >= 0) & (dist < WINDOW)
    blk = jnp.arange(nb)[:, None, None]
    valid = band[None] & ((blk > 0) | (kj[None] >= ATTN_BLOCK))
    s = jnp.where(valid[None, :, None, None], s, NEG_INF)
    sink = sinks.astype(f32).reshape(N_KV_HEADS, GQA_GROUP)[None, None, :, :, None, None]
    m = jnp.maximum(jnp.max(s, axis=-1, keepdims=True), sink)
    p = jnp.exp(s - m)
    denom = jnp.sum(p, axis=-1, keepdims=True) + jnp.exp(sink - m)
    o = jnp.einsum("bnhgqk,bnkhd->bnqhgd", p / denom, vv)
    return o.reshape(bsz, seq, Q_DIM)


def _attn_layer(x, norm, w_in, sinks, w_out):
    f32 = jnp.float32
    bsz, seq, _ = x.shape
    h = _rmsnorm(x, norm)
    proj = (h @ w_in).astype(f32)
    q, k, v, gate = jnp.split(proj, [Q_DIM, Q_DIM + KV_DIM, Q_DIM + 2 * KV_DIM], axis=-1)
    q = q.reshape(bsz, seq, N_Q_HEADS, HEAD_DIM)
    k = k.reshape(bsz, seq, N_KV_HEADS, HEAD_DIM)
    v = v.reshape(bsz, seq, N_KV_HEADS, HEAD_DIM)
    pos = jnp.arange(seq, dtype=f32)
    inv_freq = ROPE_THETA ** (-jnp.arange(0, HEAD_DIM, 2, dtype=f32) / HEAD_DIM)
    ang = pos[:, None] * inv_freq[None, :]
    cos = jnp.cos(ang)[None, :, None, :]
    sin = jnp.sin(ang)[None, :, None, :]
    o = _swa_sinks(_rope(q, cos, sin), _rope(k, cos, sin), v, sinks)
    out = (o * jax.nn.silu(gate)).astype(x.dtype) @ w_out
    return x + out


def setup_inputs(seed: int = 0) -> dict:
    key = jax.random.key(seed)
    keys = iter(jax.random.split(key, 64))
    f32 = jnp.float32

    def nrm(shape, scale):
        return jax.random.normal(next(keys), shape, f32) * scale

    inputs = {"x": nrm((BATCH, SEQ, D_MODEL), 1.0)}
    for i in range(DEPTH):
        p = "l%d_" % i
        inputs[p + "norm"] = 1.0 + nrm((D_MODEL,), 0.05)
        if i % N_MIXERS == 0:
            inputs[p + "w_in"] = nrm((D_MODEL, 2 * BRANCH), D_MODEL ** -0.5)
            inputs[p + "a_re"] = -0.5 + nrm((SSM_GROUPS, SSM_STATE), 0.01)
            inputs[p + "a_im"] = math.pi * jnp.arange(SSM_STATE, dtype=f32)[None, :] + nrm((SSM_GROUPS, SSM_STATE), 0.01)
            inputs[p + "log_step"] = jax.random.uniform(next(keys), (SSM_GROUPS,), f32, math.log(DT_MIN), math.log(DT_MAX))
            inputs[p + "b_re"] = nrm((SSM_GROUPS, SSM_STATE, SSM_GROUP), (2 * SSM_GROUP) ** -0.5)
            inputs[p + "b_im"] = nrm((SSM_GROUPS, SSM_STATE, SSM_GROUP), (2 * SSM_GROUP) ** -0.5)
            inputs[p + "c_re"] = nrm((SSM_GROUPS, SSM_GROUP, SSM_STATE), SSM_STATE ** -0.5)
            inputs[p + "c_im"] = nrm((SSM_GROUPS, SSM_GROUP, SSM_STATE), SSM_STATE ** -0.5)
            inputs[p + "d"] = nrm((BRANCH,), 1.0)
            inputs[p + "w_glu"] = nrm((BRANCH, BRANCH), BRANCH ** -0.5)
            inputs[p + "b_glu"] = nrm((BRANCH,), 0.01)
            inputs[p + "w_out"] = nrm((BRANCH, D_MODEL), BRANCH ** -0.5)
        else:
            inputs[p + "w_in"] = nrm((D_MODEL, Q_DIM + 2 * KV_DIM + BRANCH), D_MODEL ** -0.5)
            inputs[p + "sinks"] = nrm((N_Q_HEADS,), 1.0)
            inputs[p + "w_out"] = nrm((Q_DIM, D_MODEL), Q_DIM ** -0.5)
    inputs["final_norm"] = 1.0 + nrm((D_MODEL,), 0.05)
    return inputs


def reference(x,
              l0_norm, l0_w_in, l0_a_re, l0_a_im, l0_log_step, l0_b_re, l0_b_im, l0_c_re, l0_c_im, l0_d, l0_w_glu, l0_b_glu, l0_w_out,
              l1_norm, l1_w_in, l1_sinks, l1_w_out,
              l2_norm, l2_w_in, l2_a_re, l2_a_im, l2_log_step, l2_b_re, l2_b_im, l2_c_re, l2_c_im, l2_d, l2_w_glu, l2_b_glu, l2_w_out,
              l3_norm, l3_w_in, l3_sinks, l3_w_out,
              final_norm):
    ssm_params = [
        (l0_norm, l0_w_in, l0_a_re, l0_a_im, l0_log_step, l0_b_re, l0_b_im, l0_c_re, l0_c_im, l0_d, l0_w_glu, l0_b_glu, l0_w_out),
        (l2_norm, l2_w_in, l2_a_re, l2_a_im, l2_log_step, l2_b_re, l2_b_im, l2_c_re, l2_c_im, l2_d, l2_w_glu, l2_b_glu, l2_w_out),
    ]
    attn_params = [
        (l1_norm, l1_w_in, l1_sinks, l1_w_out),
        (l3_norm, l3_w_in, l3_sinks, l3_w_out),
    ]
    for i in range(DEPTH):
        if i % N_MIXERS == 0:
            x = _ssm_layer(x, *ssm_params[i // N_MIXERS])
        else:
            x = _attn_layer(x, *attn_params[i // N_MIXERS])
    return _rmsnorm(x, final_norm)
```

```python
import math
import os
from contextlib import ExitStack

import numpy as np
import concourse.bass as bass
import concourse.mybir as mybir
from concourse.bass_utils import run_bass_kernel_spmd

F32 = mybir.dt.float32
BF16 = mybir.dt.bfloat16
I32 = mybir.dt.int32
AF = mybir.ActivationFunctionType
ALU = mybir.AluOpType

T = 512
NCH = 64
NH = 32
SEQ = 8192
DM = 1024
EPS = 1e-5
TWO_PI = 2.0 * math.pi
SIN_SCALE = 6.28318


class Sched:
    CE = ("pe", "act", "dve", "pool")

    def __init__(self, nc, stack):
        self.nc = nc
        self.stack = stack
        self.q = {e: [] for e in ("pe", "act", "dve", "pool", "sp")}
        self.esem = {e: [stack.enter_context(nc.semaphore("c_%s_%d" % (e, i))) for i in range(4)] for e in self.CE}
        self.ecnt = {(e, i): 0 for e in self.CE for i in range(4)}
        self.epoch = 0
        self.last_w = {}
        self.readers = {}
        self.known = {e: {} for e in self.q}
        self.dsem = {}
        self.dcnt = {}
        self.semkey_of = {}
        self.all_tokens = {}

    def _deps(self, eng, R, W):
        deps = []
        for k in R:
            t = self.last_w.get(k)
            if t is not None:
                deps.append(t)
        for k in W:
            t = self.last_w.get(k)
            if t is not None:
                deps.append(t)
            deps.extend(self.readers.get(k, ()))
        waits = {}
        for (sem, val, e2) in deps:
            if e2 == eng and eng == "pe":
                continue
            if e2 == "dma":
                val = self.dcnt[self.semkey_of[sem]]
            if self.known[eng].get(sem, 0) >= val:
                continue
            if waits.get(sem, 0) < val:
                waits[sem] = val
        for sem, val in waits.items():
            self.q[eng].append(lambda e, s=sem, v=val: e.wait_ge(s, v))
            self.known[eng][sem] = val

    def _record(self, tok, R, W):
        for k in R:
            self.readers.setdefault(k, []).append(tok)
        for k in W:
            self.last_w[k] = tok
            self.readers[k] = []
        self.all_tokens[tok[0]] = max(self.all_tokens.get(tok[0], 0), tok[1])

    def op(self, eng, fn, R=(), W=()):
        self._deps(eng, R, W)
        i = self.epoch % 4
        sem = self.esem[eng][i]
        self.ecnt[(eng, i)] += 1
        val = self.ecnt[(eng, i)]
        self.q[eng].append(lambda e, f=fn, s=sem: f(e).then_inc(s, 1))
        self._record((sem, val, eng), R, W)

    def dma(self, queue, out, in_, R=(), W=(), semkey=None):
        self._deps(queue, R, W)
        if semkey not in self.dsem:
            self.dsem[semkey] = self.stack.enter_context(self.nc.semaphore("d_%s" % semkey))
            self.dcnt[semkey] = 0
            self.semkey_of[self.dsem[semkey]] = semkey
        sem = self.dsem[semkey]
        self.dcnt[semkey] += 16
        val = self.dcnt[semkey]
        self.q[queue].append(lambda e, o=out, i=in_, s=sem: e.dma_start(out=o, in_=i).then_inc(s, 16))
        self._record((sem, val, "dma"), R, W)

    def barrier(self, skip=()):
        for eng in self.q:
            for sem, val in self.all_tokens.items():
                if self.semkey_of.get(sem) in skip:
                    continue
                if self.known[eng].get(sem, 0) < val:
                    self.q[eng].append(lambda e, s=sem, v=val: e.wait_ge(s, v))
                    self.known[eng][sem] = val

    def emit(self, block):
        q = self.q

        @block.tensor
        def _(e):
            for f in q["pe"]:
                f(e)

        @block.scalar
        def _(e):
            for f in q["act"]:
                f(e)

        @block.vector
        def _(e):
            for f in q["dve"]:
                f(e)

        @block.gpsimd
        def _(e):
            for f in q["pool"]:
                f(e)

        @block.sync
        def _(e):
            for f in q["sp"]:
                f(e)


SSM_HOST = ("norm", "bglu", "aRe", "aIm", "ls", "cRe", "cIm", "bRe", "bIm", "dcol")


def build(n_tiles=16, layers=(0, 1, 2, 3)):
    nc = bass.Bass("TRN2", target_bir_lowering=False)
    stack = ExitStack()
    S = Sched(nc, stack)

    def din(name, shape, dt=F32):
        return nc.dram_tensor(name, list(shape), dt, kind="ExternalInput").ap()

    def dscr(name, shape, dt):
        return nc.dram_tensor(name, list(shape), dt, kind="Internal").ap()

    x = din("x", [SEQ, DM])
    out = nc.dram_tensor("out", [SEQ, DM], F32, kind="ExternalOutput").ap()
    c_identf = din("c_identf", [128, 128])
    c_mask2 = din("c_mask2", [128, 2, 128])
    c_ones2 = din("c_ones2", [128, 2, 128])
    c_tri = din("c_tri", [128, 128])
    c_fidx = din("c_fidx", [128, 1])
    c_sgn = din("c_sgn", [128, 1])
    c_pos = din("c_pos", [SEQ])
    fnorm = din("fnorm_col", [128, 8])
    L = {}
    for l in layers:
        p = "l%d_" % l
        d = {}
        d["norm"] = din(p + "norm_col", [128, 8])
        if l % 2 == 0:
            d["w_in"] = din(p + "w_in", [DM, 2048])
            d["w_glu"] = din(p + "w_glu", [DM, DM])
            d["w_out"] = din(p + "w_out", [DM, DM])
            d["bglu"] = din(p + "bglu_col", [128, 8])
            for nm in ("aRe", "aIm", "ls", "dcol"):
                d[nm] = din(p + nm, [128, 64])
            for nm in ("cRe", "cIm", "bRe", "bIm"):
                d[nm] = din(p + nm, [128, 64, 16])
            d["w_in_b"] = dscr(p + "w_in_b", [DM, 2048], BF16)
            d["w_glu_b"] = dscr(p + "w_glu_b", [DM, DM], BF16)
            d["w_out_b"] = dscr(p + "w_out_b", [DM, DM], BF16)
            d["WS3"] = dscr(p + "WS3", [128, 64, 192], BF16)
            d["WHK"] = dscr(p + "WHK", [128, 2, 64, 128], BF16)
        else:
            d["w_in"] = din(p + "w_in", [DM, 2304])
            d["w_qp"] = din(p + "w_qp", [DM, DM])
            d["w_kk"] = din(p + "w_kk", [DM, 512])
            d["w_out"] = din(p + "w_out", [DM, DM])
            d["sink"] = din(p + "sink_col", [128, 8])
            d["w_in_b"] = dscr(p + "w_in_b", [DM, 2304], BF16)
            d["w_qp_b"] = dscr(p + "w_qp_b", [DM, DM], BF16)
            d["w_kk_b"] = dscr(p + "w_kk_b", [DM, 512], BF16)
            d["w_out_b"] = dscr(p + "w_out_b", [DM, DM], BF16)
        L[l] = d
    has_att = any(l % 2 == 1 for l in layers)
    ropeC = dscr("ropeC", [128, SEQ], F32)
    ropeS = dscr("ropeS", [128, SEQ], F32)

    ARENA_W = 53000
    arena = stack.enter_context(nc.sbuf_tensor("arena", [128, ARENA_W], F32))
    cur = [0]

    def carve(shape, dt, at=None):
        n = int(np.prod(shape[1:]))
        words = n if dt in (F32, I32) else (n + 1) // 2
        if at is None:
            off = cur[0]
            cur[0] += words
            assert cur[0] <= ARENA_W, ("sbuf overflow", cur[0])
        else:
            off = at
            assert off + words <= ARENA_W
        v = arena[:, off:off + words]
        if dt != F32:
            v = v.bitcast(dt)
        if len(shape) > 2:
            names = " ".join("d%d" % i for i in range(1, len(shape)))
            kw = {"d%d" % i: shape[i] for i in range(1, len(shape))}
            v = v.rearrange("p (%s) -> p %s" % (names, names), **kw)
        return v

    identf = carve([128, 128], F32)
    identb = carve([128, 128], BF16)
    onesb = carve([128, 128], BF16)
    mask2 = carve([128, 2, 128], BF16)
    ones2 = carve([128, 2, 128], BF16)
    stage_c = carve([128, 2, 128], F32)
    normc = carve([128, 5, 8], F32)
    bgluc = carve([128, 2, 8], F32)
    sinke = carve([128, 2, 8], F32)
    CAB = carve([128, 2, 2, 2, 64], F32)
    HHc = carve([128, 2, 2, 64], F32)
    pro0 = cur[0]
    rstd = carve([128, T], F32)
    tmpf = carve([128, 2, T], F32)
    rtmp = tmpf[:, 1, :]
    sigb = carve([128, 2, T], BF16)
    cosT = carve([128, T], F32)
    sinT = carve([128, T], F32)
    KT = carve([128, 2, 640], BF16)
    Vpad = carve([128, 5, 2, 2, 128], BF16)
    KTc = carve([128, 2, 2, 128], BF16)
    Vc = carve([128, 2, 2, 2, 128], BF16)
    xB = carve([128, 8, T], F32)
    hB = carve([128, 8, T], BF16)
    gB = carve([128, 8, T], BF16)
    zB = carve([128, 8, T], BF16)
    main0 = cur[0]
    tokA_off = cur[0]
    tokA = carve([128, 8, DM], BF16)
    XC = carve([128, 64, 64], BF16)
    HP_off = cur[0]
    HP = carve([128, 64, 64], BF16)
    xtok = carve([128, 2, DM], F32, at=HP_off)
    SS = carve([128, NH, 2, 64], F32)
    s5w = carve([128, 64 * 256], BF16)
    wsl = [carve([128, 8, DM], BF16) for _ in range(3)]
    PT = tokA.rearrange("p a b -> p (a b)")[:, 0:8 * T].rearrange("p (s t) -> p s t", s=8)
    yB = SS.rearrange("p a b c -> p (a b c)").rearrange("p (k t) -> p k t", k=8)
    WS3 = s5w[:, 0:64 * 192].rearrange("p (g r) -> p g r", g=64)
    WHK = s5w.rearrange("p (s g r) -> p s g r", s=2, g=64)
    kvw = s5w[:, 0:8 * 640].rearrange("p (k n) -> p k n", k=8)

    psf = [stack.enter_context(nc.psum_tensor("psf%d" % i, [128, 512], F32)) for i in range(6)]
    psb = [stack.enter_context(nc.psum_tensor("psb%d" % i, [128, 1024], BF16)) for i in range(2)]
    rr = {"f": 0, "b": 0, "e": 0}

    def bankf():
        i = rr["f"] % 6
        rr["f"] += 1
        return psf[i], ("psf", i)

    def bankb():
        i = rr["b"] % 2
        rr["b"] += 1
        return psb[i], ("psb", i)

    def ev_eng():
        rr["e"] += 1
        return "act" if rr["e"] % 2 else "dve"

    def copy(eng, o, i, R, W):
        if eng != "act" and o.dtype == i.dtype:
            S.op(eng, lambda e: e.tensor_single_scalar(out=o, in_=i, scalar=1.0, op=ALU.mult), R, W)
            return
        if eng == "act":
            S.op("act", lambda e: e.activation(out=o, in_=i, func=AF.Copy), R, W)
        else:
            S.op(eng, lambda e: e.tensor_copy(out=o, in_=i), R, W)

    def tt(eng, o, a, b, op, R, W):
        S.op(eng, lambda e: e.tensor_tensor(out=o, in0=a, in1=b, op=op), R, W)

    def ts(eng, o, a, s1, op, R, W):
        S.op(eng, lambda e: e.tensor_single_scalar(out=o, in_=a, scalar=s1, op=op), R, W)

    def actf(o, i, func, R, W, scale=1.0, bias=0.0):
        S.op("act", lambda e: e.activation(out=o, in_=i, func=func, scale=scale, bias=bias), R, W)

    def mm(o, lhsT, rhs, start, stop, R, W):
        S.op("pe", lambda e: e.matmul(o, lhsT=lhsT, rhs=rhs, start=start, stop=stop), R, W)

    def tr(o, i, ident, R, W):
        S.op("pe", lambda e: e.transpose(o, i, ident), R, W)

    S.dma("sp", identf, c_identf, W=["identf"], semkey="c0")
    copy("dve", identb, identf, ["identf"], ["identb"])
    S.op("dve", lambda e: e.memset(onesb, 1.0), (), ["onesb"])
    S.dma("sp", stage_c, c_mask2, W=["stage_c"], semkey="c1")
    copy("dve", mask2, stage_c, ["stage_c"], ["mask2"])
    S.dma("sp", stage_c, c_ones2, R=(), W=["stage_c"], semkey="c1")
    copy("dve", ones2, stage_c, ["stage_c"], ["ones2"])
    S.op("dve", lambda e: e.memset(HHc, 0.0), (), ["HHc"])
    for li, l in enumerate(layers):
        S.dma("sp", normc[:, li, :], L[l]["norm"], W=[("normc", li)], semkey="c2")
    S.dma("sp", normc[:, 4, :], fnorm, W=[("normc", 4)], semkey="c2")

    for l in layers:
        d = L[l]
        names = ("w_in", "w_glu", "w_out") if l % 2 == 0 else ("w_in", "w_qp", "w_kk", "w_out")
        for nm in names:
            for rb in range(8):
                S.dma("pool", d[nm + "_b"][rb * 128:(rb + 1) * 128, :], d[nm][rb * 128:(rb + 1) * 128, :], W=[("wscr", l, nm, rb)], semkey="wc_%d_%s" % (l, nm))

    if has_att:
        pw = 2048
        base = pro0
        posb = carve([128, pw], F32, at=base)
        tq = carve([128, pw], F32, at=base + pw)
        tqi = carve([128, pw], I32, at=base + 2 * pw)
        tqf = carve([128, pw], F32, at=base + 3 * pw)
        res = carve([128, pw], F32, at=base + 4 * pw)
        fidx = carve([128, 1], F32, at=base + 5 * pw)
        invf = carve([128, 1], F32, at=base + 5 * pw + 1)
        sgn = carve([128, 1], F32, at=base + 5 * pw + 2)
        S.dma("sp", fidx, c_fidx, W=["pp"], semkey="c3")
        S.dma("sp", sgn, c_sgn, W=["pp"], semkey="c3")
        actf(invf, fidx, AF.Exp, ["pp"], ["pp"], scale=-math.log(10000.0) / 32.0)
        for ch in range(SEQ // pw):
            S.dma("sp", posb, c_pos[ch * pw:(ch + 1) * pw].partition_broadcast(128), W=["pp"], semkey="c4")
            S.op("dve", lambda e: e.tensor_scalar_mul(out=posb, in0=posb, scalar1=invf[:, 0:1]), ["pp", "pp"], ["pp"])
            ts("dve", posb, posb, 1.0 / TWO_PI, ALU.mult, ["pp"], ["pp"])
            for which, shift, dst in (("s", 0.0, ropeS), ("c", 0.25, ropeC)):
                ts("dve", tq, posb, shift, ALU.add, ["pp"], ["pp"])
                copy("dve", tqi, tq, ["pp"], ["pp"])
                copy("dve", tqf, tqi, ["pp"], ["pp"])
                tt("dve", tq, tq, tqf, ALU.subtract, ["pp", "pp"], ["pp"])
                actf(res, tq, AF.Sin, ["pp"], ["pp"], scale=SIN_SCALE)
                if which == "s":
                    S.op("dve", lambda e: e.tensor_scalar_mul(out=res, in0=res, scalar1=sgn[:, 0:1]), ["pp", "pp"], ["pp"])
                S.dma("sp", dst[:, ch * pw:(ch + 1) * pw], res, R=["pp"], W=[("rope", which)], semkey="c5")
        for li, l in enumerate(layers):
            if l % 2 == 1:
                si = (l // 2)
                S.dma("sp", sinke[:, si, :], L[l]["sink"], W=[("sinke", si)], semkey="c6")
                actf(sinke[:, si, :], sinke[:, si, :], AF.Exp, [("sinke", si)], [("sinke", si)])

    for l in layers:
        if l % 2:
            continue
        d = L[l]
        si = l // 2
        o = [pro0]

        def pc(shape, dt=F32):
            n = int(np.prod(shape[1:]))
            w = n if dt in (F32, I32) else (n + 1) // 2
            v = carve(shape, dt, at=o[0])
            o[0] += w
            return v
        NK = 17
        aRe = pc([128, 64]); aIm = pc([128, 64]); ls = pc([128, 64]); dcol = pc([128, 64])
        cRe = pc([128, 64, 16]); cIm = pc([128, 64, 16]); bRe = pc([128, 64, 16]); bIm = pc([128, 64, 16])
        lr = pc([128, 64]); lim = pc([128, 64])
        Er = pc([128, NK, 64]); Ei = pc([128, NK, 64]); Fr = pc([128, NK, 64]); Fi = pc([128, NK, 64])
        cr = pc([128, 64]); ci = pc([128, 64]); w1 = pc([128, 64]); w2 = pc([128, 64])
        WST = pc([128, 64, 8, 16]); WS2T = pc([128, 64, 8, 16], BF16); WHf = pc([128, 64, 8, 16])
        tri = pc([128, 128]); k0t = pc([128, 128])
        U1 = o[0]
        u1 = pc([128, 64, 8, 16])
        U2 = o[0]
        u2 = pc([128, 64, 8, 16])
        o2 = [U1]
        def pc2(shape, dt=F32):
            n = int(np.prod(shape[1:]))
            v = carve(shape, dt, at=o2[0])
            o2[0] += n
            return v
        ex = pc2([128, NK, 64]); tq_ = pc2([128, NK, 64]); t2_ = pc2([128, NK, 64]); ti_ = pc2([128, NK, 64], I32)
        tf_ = pc2([128, NK, 64]); sn = pc2([128, NK, 64]); cs = pc2([128, NK, 64])
        assert o2[0] <= U2
        WS3s = carve([128, 64, 192], BF16, at=U1)
        WHKs = carve([128, 2, 64, 128], BF16, at=U2)
        assert o[0] <= ARENA_W, o[0]
        k = "pp"
        for nm, tl in (("aRe", aRe), ("aIm", aIm), ("ls", ls), ("dcol", dcol), ("cRe", cRe), ("cIm", cIm), ("bRe", bRe), ("bIm", bIm)):
            S.dma("sp", tl, d[nm], W=[k + nm], semkey="pp")
        S.dma("sp", tri, c_tri, W=[k + "tri"], semkey="pp")
        S.dma("sp", bgluc[:, si, :], d["bglu"], W=[("bgluc", si)], semkey="pp")
        lk = [k] + [k + nm for nm in ("aRe", "aIm", "ls", "dcol", "cRe", "cIm", "bRe", "bIm", "tri")]
        actf(ls, ls, AF.Exp, lk, [k, "WS3s", "WHKs"])
        tt("dve", lr, aRe, ls, ALU.mult, [k], [k])
        tt("dve", lim, aIm, ls, ALU.mult, [k], [k])
        for kk in range(-8, 9):
            idx = kk + 8
            actf(ex[:, idx, :], lr, AF.Exp, [k], [k], scale=float(kk))
            ts("dve", tq_[:, idx, :], lim, kk / TWO_PI, ALU.mult, [k], [k])
        for dst, shift in ((sn, 0.0), (cs, 0.25)):
            ts("dve", t2_, tq_, shift, ALU.add, [k], [k])
            copy("dve", ti_, t2_, [k], [k])
            copy("dve", tf_, ti_, [k], [k])
            tt("dve", t2_, t2_, tf_, ALU.subtract, [k], [k])
            actf(dst, t2_, AF.Sin, [k], [k], scale=SIN_SCALE)
        tt("dve", Er, ex, cs, ALU.mult, [k], [k])
        tt("dve", Ei, ex, sn, ALU.mult, [k], [k])
        E1r = Er[:, 9, :]; E1i = Ei[:, 9, :]
        ts("dve", w1, E1r, -1.0, ALU.add, [k], [k])
        tt("dve", cr, w1, aRe, ALU.mult, [k], [k])
        tt("dve", w2, E1i, aIm, ALU.mult, [k], [k])
        tt("dve", cr, cr, w2, ALU.add, [k], [k])
        tt("dve", ci, E1i, aRe, ALU.mult, [k], [k])
        tt("dve", w2, w1, aIm, ALU.mult, [k], [k])
        tt("dve", ci, ci, w2, ALU.subtract, [k], [k])
        tt("dve", w1, aRe, aRe, ALU.mult, [k], [k])
        tt("dve", w2, aIm, aIm, ALU.mult, [k], [k])
        tt("dve", w1, w1, w2, ALU.add, [k], [k])
        S.op("dve", lambda e: e.reciprocal(out=w1, in_=w1), [k], [k])
        tt("dve", cr, cr, w1, ALU.mult, [k], [k])
        tt("dve", ci, ci, w1, ALU.mult, [k], [k])
        crb = cr.unsqueeze(1).broadcast_to([128, NK, 64])
        cib = ci.unsqueeze(1).broadcast_to([128, NK, 64])
        tt("dve", Fr, Er, crb, ALU.mult, [k], [k])
        tt("dve", t2_, Ei, cib, ALU.mult, [k], [k])
        tt("dve", Fr, Fr, t2_, ALU.subtract, [k], [k])
        tt("dve", Fi, Er, cib, ALU.mult, [k], [k])
        tt("dve", t2_, Ei, crb, ALU.mult, [k], [k])
        tt("dve", Fi, Fi, t2_, ALU.add, [k], [k])

        def kview(v, lo, hi, k0, rev):
            vv = v[lo:hi, :, :].rearrange("p k g -> p g k")
            vv = vv[:, :, k0 - 7:k0 + 1][:, :, ::-1] if rev else vv[:, :, k0:k0 + 8]
            return vv.unsqueeze(3).broadcast_to([hi - lo, 64, 8, 16])

        def bview(v, lo, hi):
            return v[lo:hi].unsqueeze(2).broadcast_to([hi - lo, 64, 8, 16])

        for dstT, k0 in ((WST, 15), (WS2T, 7)):
            tt("dve", u1[0:64], bview(bRe, 0, 64), kview(Fr, 0, 64, k0, True), ALU.mult, [k], [k])
            tt("dve", u2[0:64], bview(bIm, 0, 64), kview(Fi, 0, 64, k0, True), ALU.mult, [k], [k])
            tt("dve", dstT[0:64], u1[0:64], u2[0:64], ALU.subtract, [k], [k])
            tt("dve", u1[64:128], bview(bIm, 64, 128), kview(Fr, 64, 128, k0, True), ALU.mult, [k], [k])
            tt("dve", u2[64:128], bview(bRe, 64, 128), kview(Fi, 64, 128, k0, True), ALU.mult, [k], [k])
            tt("dve", dstT[64:128], u1[64:128], u2[64:128], ALU.add, [k], [k])
        tt("dve", u1[0:64], bview(cRe, 0, 64), kview(Er, 0, 64, 9, False), ALU.mult, [k], [k])
        tt("dve", u2[0:64], bview(cIm, 0, 64), kview(Ei, 0, 64, 9, False), ALU.mult, [k], [k])
        tt("dve", WHf[0:64], u1[0:64], u2[0:64], ALU.subtract, [k], [k])
        tt("dve", u1[64:128], bview(cRe, 64, 128), kview(Ei, 64, 128, 9, False), ALU.mult, [k], [k])
        tt("dve", u2[64:128], bview(cIm, 64, 128), kview(Er, 64, 128, 9, False), ALU.mult, [k], [k])
        tt("dve", u1[64:128], u1[64:128], u2[64:128], ALU.add, [k], [k])
        ts("dve", WHf[64:128], u1[64:128], -1.0, ALU.mult, [k], [k])
        for s2 in range(2):
            copy("dve", CAB[:, si, 0, s2, :], Er[:, 16, :], [k], [("CAB", si)])
        ts("dve", CAB[0:64, si, 1, 0, :], Ei[0:64, 16, :], -1.0, ALU.mult, [k], [("CAB", si)])
        copy("dve", CAB[64:128, si, 1, 0, :], Ei[64:128, 16, :], [k], [("CAB", si)])
        ts("dve", CAB[:, si, 1, 1, :], CAB[:, si, 1, 0, :], -1.0, ALU.mult, [("CAB", si)], [("CAB", si)])
        copy("act", WHKs[:, 0, :, :], WHf.rearrange("p g j c -> p g (j c)"), [k], ["WHKs"])
        for g in range(64):
            pb, pk = bankf()
            pb2, pk2 = bankf()
            tr(pb[:, 0:128], WST[:, g, :, :].rearrange("p j c -> p (j c)"), identf, [k, "identf"], [pk])
            mm(pb2[:, 0:128], WS2T[:, g, :, :].rearrange("p j c -> p (j c)"),
               WHKs[:, 0, g, :], True, True, [k, "WHKs"], [pk2])
            copy("act", WS3s[:, g, 0:128], pb[:, 0:128], [pk], ["WS3s"])
            copy("act", WS3s[:, g, 128:192], pb[:, 0:64], [pk], ["WS3s"])
            tt("dve", k0t, pb2[:, 0:128], tri, ALU.mult, [pk2, k], ["k0t"])
            S.op("dve", lambda e, g=g: e.scalar_tensor_tensor(out=WHKs[:, 1, g, :], in0=identf, scalar=dcol[:, g:g + 1],
                                                               in1=k0t, op0=ALU.mult, op1=ALU.add),
                 ["k0t", k, "identf"], ["WHKs"])
        S.dma("sp", d["WS3"], WS3s, R=["WS3s"], W=[("s5scr", l)], semkey="pst")
        S.dma("sp", d["WHK"], WHKs, R=["WHKs"], W=[("s5scr", l)], semkey="pst")
    S.barrier(skip=[kk_ for kk_ in S.dsem if kk_.startswith("wc_")])
    S.op("pool", lambda e: e.memset(Vpad, 0.0), (), ["Vpad"])
    S.op("pool", lambda e: e.memset(KT, 0.0), (), ["KT"])
    S.op("pool", lambda e: e.memset(KTc, 0.0), (), ["KTc"])
    S.op("pool", lambda e: e.memset(Vc, 0.0), (), ["Vc"])

    HT = T // 2
    NHC = NH

    def tk(h):
        return slice(h * HT, (h + 1) * HT)

    def load_w(slot, src, cols, l, nm):
        ncol = cols[1] - cols[0]
        S.dma("sp", wsl[slot][:, :, 0:ncol], src[:, cols[0]:cols[1]].rearrange("(k p) n -> p k n", p=128),
              R=[("wscr", l, nm, rb) for rb in range(8)], W=[("wsl", slot)], semkey="wsl%d" % slot)

    def load_s5w(l, which):
        d = L[l]
        if which == "WS3":
            S.dma("sp", WS3, d["WS3"], R=[("s5scr", l)], W=["s5w"], semkey="s5w")
        elif which == "WHK":
            S.dma("sp", WHK, d["WHK"], R=[("s5scr", l)], W=["s5w"], semkey="s5w")
        else:
            S.dma("sp", kvw[:, :, 0:512], d["w_kk_b"].rearrange("(k p) n -> p k n", p=128),
                  R=[("wscr", l, "w_kk", rb) for rb in range(8)], W=["s5w"], semkey="s5w")
            S.dma("sp", kvw[:, :, 512:640], d["w_in_b"][:, 1152:1280].rearrange("(k p) n -> p k n", p=128),
                  R=[("wscr", l, "w_in", rb) for rb in range(8)], W=["s5w"], semkey="s5w")

    def layer_loads(l, which):
        d = L[l]
        if l % 2 == 0:
            if which == "slot0":
                load_w(0, d["w_in_b"], (0, 1024), l, "w_in")
            elif which == "slot1":
                load_w(1, d["w_in_b"], (1024, 2048), l, "w_in")
            elif which == "slot2":
                load_w(2, d["w_glu_b"], (0, 1024), l, "w_glu")
            else:
                load_s5w(l, "WS3")
        else:
            if which == "slot0":
                load_w(0, d["w_in_b"], (0, 1024), l, "w_in")
            elif which == "slot1":
                load_w(1, d["w_qp_b"], (0, 1024), l, "w_qp")
            elif which == "slot2":
                load_w(2, d["w_in_b"], (1280, 2304), l, "w_in")
            else:
                load_s5w(l, "KV")

    def rmsnorm(h):
        t = tk(h)
        actf(zB[:, :, t], xB[:, :, t], AF.Square, [("xB", h)], [("zB", h)])
        pb, pk = bankf()
        for kc in range(8):
            mm(pb[:, 0:HT], onesb, zB[:, kc, t], kc == 0, kc == 7, ["onesb", ("zB", h)], [pk])
        actf(tmpf[:, 1, t], pb[:, 0:HT], AF.Ln, [pk], [("t2", h)], scale=1.0 / DM, bias=EPS)
        actf(rstd[:, t], tmpf[:, 1, t], AF.Exp, [("t2", h)], [("rstd", h)], scale=-0.5)

    def norm_apply(h, li, dst, dkey):
        t = tk(h)
        for kc in range(8):
            S.op("dve", lambda e, kc=kc: e.scalar_tensor_tensor(out=dst[:, kc, :], in0=xB[:, kc, t],
                                                                 scalar=normc[:, li, kc:kc + 1], in1=rstd[:, t],
                                                                 op0=ALU.mult, op1=ALU.mult),
                 [("xB", h), ("rstd", h), ("normc", li)], [dkey])

    def mm_fm(h, slot, act, akey, n_mt, evac):
        t = tk(h)
        for mt in range(n_mt):
            pb, pk = bankf()
            for kc in range(8):
                mm(pb[:, 0:HT], wsl[slot][:, kc, mt * 128:(mt + 1) * 128], act[:, kc, t], kc == 0, kc == 7,
                   [("wsl", slot), akey], [pk])
            evac(mt, pb, pk)

    def out_proj(h, slot):
        t = tk(h)

        def ev(mt, pb, pk):
            tt("dve", xB[:, mt, t], xB[:, mt, t], pb[:, 0:HT], ALU.add, [pk, ("xB", h)], [("xB", h)])
        mm_fm(h, slot, gB, ("gB", h), 8, ev)

    def nxt(ti, li):
        if li + 1 < len(layers):
            return layers[li + 1]
        if ti + 1 < n_tiles:
            return layers[0]
        return None

    def ssm_gen(ti, h, li, l):
        d = L[l]
        si = l // 2
        t = tk(h)
        n0 = h * NHC
        r0, r1 = h * 32, h * 32 + 32
        nl = nxt(ti, li)
        rmsnorm(h)
        norm_apply(h, li, hB[:, :, t], ("hB", h))
        yield
        mm_fm(h, 1, hB, ("hB", h), 8, lambda mt, pb, pk: actf(gB[:, mt, t], pb[:, 0:HT], AF.Silu, [pk], [("gB", h)]))
        if h == 1 and nl is not None:
            layer_loads(nl, "slot1")
        yield
        uA = tokA.rearrange("p a b -> p (a b)").rearrange("p (g j c) -> p g j c", g=64, j=8, c=16)
        for j in range(8):
            for nt in range(2):
                pb, pk = bankf()
                for kc in range(8):
                    mm(pb[0:32, :], hB[:, kc, h * HT + j:(h + 1) * HT:8], wsl[0][:, kc, nt * 512:(nt + 1) * 512], kc == 0, kc == 7,
                       [("hB", h), ("wsl", 0)], [pk])
                copy(ev_eng(), uA[r0:r1, nt * 32:(nt + 1) * 32, j, :],
                     pb[0:32, :].rearrange("p (g c) -> p g c", g=32), [pk], [("tokA", h)])
        if h == 1:
            load_w(0, d["w_out_b"], (0, 1024), l, "w_out")
        yield
        for gq in range(2):
            pb, pk = bankb()
            for g32 in range(32):
                g = gq * 32 + g32
                tr(pb[:, g32 * 32:(g32 + 1) * 32], uA[r0:r1, g, :, :].rearrange("p j c -> p (j c)"), identb[r0:r1, r0:r1],
                   [("tokA", h), "identb"], [pk])
            copy(ev_eng(), XC[:, gq * 32:(gq + 1) * 32, n0:n0 + NHC], pb[:, :].rearrange("p (g n) -> p g n", g=32), [pk], [("XC", h)])
        yield
        for gq in range(8):
            pb, pk = bankf()
            for g8 in range(8):
                g = gq * 8 + g8
                for s_ in range(2):
                    c0 = (g8 * 2 + s_) * NHC
                    mm(pb[:, c0:c0 + NHC], WS3[:, g, s_ * 64:s_ * 64 + 128], XC[:, g, n0:n0 + NHC], True, True,
                       ["s5w", ("XC", h)], [pk])
            copy(ev_eng(), SS[:, :, :, gq * 8:(gq + 1) * 8],
                 pb[:, :].rearrange("p (g s n) -> p n s g", g=8, s=2), [pk], ["SS"])
        if h == 1:
            load_s5w(l, "WHK")
        yield
        HPv = HP.rearrange("p g n -> p n g")
        t1 = tmpf[:, 0, h * HT:h * HT + 128].rearrange("p (s g) -> p s g", s=2)
        t2 = tmpf[:, 1, h * HT:h * HT + 128].rearrange("p (s g) -> p s g", s=2)
        CA = CAB[:, si, 0, :, :]
        CB = CAB[:, si, 1, :, :]
        kq = [("CAB", si)]
        copy("act", HPv[:, n0, :], HHc[:, si, 0, :], ["HHc"], ["HP"])
        for n in range(NHC):
            prev = HHc[:, si, :, :] if n == 0 else SS[:, n - 1, :, :]
            pk_ = "HHc" if n == 0 else "SS"
            tt("dve", t1, prev, CA, ALU.mult, [pk_] + kq, [("t1", h)])
            tt("dve", t2, prev[:, ::-1, :], CB, ALU.mult, [pk_] + kq, [("t2", h)])
            tt("dve", t1, t1, t2, ALU.add, [("t1", h), ("t2", h)], [("t1", h)])
            tt("dve", SS[:, n, :, :], SS[:, n, :, :], t1, ALU.add, [("t1", h), "SS"], ["SS"])
        copy("act", HPv[:, n0 + 1:n0 + NHC, :], SS[:, 0:NHC - 1, 0, :], ["SS"], ["HP"])
        copy("pool", HHc[:, si, :, :], SS[:, NHC - 1, :, :], ["SS"], ["HHc"])
        yield
        zA = tokA
        for gq in range(16):
            pb, pk = bankf()
            for g4 in range(4):
                g = gq * 4 + g4
                mm(pb[0:32, g4 * 128:(g4 + 1) * 128], HP[:, g, n0:n0 + NHC], WHK[:, 0, g, :], True, False, ["HP", "s5w"], [pk])
                mm(pb[0:32, g4 * 128:(g4 + 1) * 128], XC[:, g, n0:n0 + NHC], WHK[:, 1, g, :], False, True, [("XC", h), "s5w"], [pk])
            pv = pb[0:32, :].rearrange("p (g j c) -> p j g c", g=4, j=8)
            for j in range(8):
                actf(zA[r0:r1, j, gq * 64:(gq + 1) * 64].rearrange("p (g c) -> p g c", g=4), pv[:, j, :, :],
                     AF.Gelu_apprx_tanh, [pk], [("tokA", h)])
        if h == 1 and nl is not None:
            layer_loads(nl, "s5w")
        yield
        for jp in range(2):
            pb, pk = bankb()
            for jj in range(4):
                j = jp * 4 + jj
                for kc in range(8):
                    c0 = (jj * 8 + kc) * 32
                    tr(pb[:, c0:c0 + 32], zA[r0:r1, j, kc * 128:(kc + 1) * 128], identb[r0:r1, r0:r1],
                       [("tokA", h), "identb"], [pk])
            for jj in range(4):
                j = jp * 4 + jj
                copy(ev_eng(), zB[:, :, h * HT + j:(h + 1) * HT:8], pb[:, jj * 256:(jj + 1) * 256].rearrange("p (k n) -> p k n", k=8),
                     [pk], [("zB", h)])
        yield
        def ev_glu(mt, pb, pk):
            sb = sigb[:, mt % 2, t]
            S.op("act", lambda e: e.activation(out=sb, in_=pb[:, 0:HT], func=AF.Sigmoid, bias=bgluc[:, si, mt:mt + 1], scale=1.0),
                 [pk, ("bgluc", si)], [("sigb", h, mt % 2)])
            tt("dve", gB[:, mt, t], gB[:, mt, t], sb, ALU.mult, [("sigb", h, mt % 2), ("gB", h)], [("gB", h)])
            tt("dve", gB[:, mt, t], gB[:, mt, t], zB[:, mt, t], ALU.mult, [("zB", h), ("gB", h)], [("gB", h)])
        mm_fm(h, 2, zB, ("zB", h), 8, ev_glu)
        if h == 1 and nl is not None:
            layer_loads(nl, "slot2")
        yield
        out_proj(h, 0)
        if h == 1 and nl is not None:
            layer_loads(nl, "slot0")
        yield

    def att_gen(ti, h, li, l):
        d = L[l]
        si = l // 2
        t = tk(h)
        t0 = ti * T + h * HT
        nl = nxt(ti, li)
        S.dma("sp", cosT[:, t], ropeC[:, t0:t0 + HT], R=[("rope", "c")], W=[("cosT", h)], semkey="rope%d" % h)
        S.dma("sp", sinT[:, t], ropeS[:, t0:t0 + HT], R=[("rope", "s")], W=[("sinT", h)], semkey="rope%d" % h)
        rmsnorm(h)
        norm_apply(h, li, hB[:, :, t], ("hB", h))
        if h == 0:
            copy("pool", KT[:, :, 0:128], KTc[:, si, :, :], ["KTc"], [("KT", 0)])
            copy("pool", Vpad[:, 0, :, :, :], Vc[:, si, :, :, :], ["Vc"], [("Vpad", 0)])
        yield

        def rope_pair(dst, dkey, wA, wB, wkeys):
            pa, pka = bankf()
            pbb, pkb = bankf()
            for kc in range(8):
                mm(pa[:, 0:HT], wA(kc), hB[:, kc, t], kc == 0, kc == 7, [("hB", h)] + wkeys, [pka])
            for kc in range(8):
                mm(pbb[:, 0:HT], wB(kc), hB[:, kc, t], kc == 0, kc == 7, [("hB", h)] + wkeys, [pkb])
            tt("dve", tmpf[:, 0, t], pa[:, 0:HT], cosT[:, t], ALU.mult, [pka, ("cosT", h)], [("t1", h)])
            tt("dve", tmpf[:, 1, t], pbb[:, 0:HT], sinT[:, t], ALU.mult, [pkb, ("sinT", h)], [("t2", h)])
            tt("pool", dst, tmpf[:, 0, t], tmpf[:, 1, t], ALU.add, [("t1", h), ("t2", h)], dkey)

        for mt in range(8):
            rope_pair(zB[:, mt, t], [("zB", h)], lambda kc, mt=mt: wsl[0][:, kc, mt * 128:(mt + 1) * 128],
                      lambda kc, mt=mt: wsl[1][:, kc, mt * 128:(mt + 1) * 128], [("wsl", 0), ("wsl", 1)])
        if h == 1:
            load_w(0, d["w_out_b"], (0, 1024), l, "w_out")
            if nl is not None:
                layer_loads(nl, "slot1")
        yield
        kslots = [("KT", 1 + 2 * h), ("KT", 2 + 2 * h)]
        for g in range(2):
            rope_pair(KT[:, g, 128 + h * HT:128 + (h + 1) * HT], kslots, lambda kc, g=g: kvw[:, kc, g * 128:(g + 1) * 128],
                      lambda kc, g=g: kvw[:, kc, 256 + g * 128:256 + (g + 1) * 128], ["s5w"])
        for b2 in range(2):
            blk = 2 * h + b2
            pb, pk = bankf()
            for kc in range(8):
                mm(pb[:, 0:128], hB[:, kc, blk * 128:(blk + 1) * 128], kvw[:, kc, 512:640], kc == 0, kc == 7, [("hB", h), "s5w"], [pk])
            pv = pb[:, 0:128].rearrange("p (g d) -> p g d", g=2)
            copy("act", Vpad[:, blk + 1, :, 0, 0:64], pv, [pk], [("Vpad", blk + 1)])
            copy("act", Vpad[:, blk + 1, :, 1, 64:128], pv, [pk], [("Vpad", blk + 1)])
        if h == 1 and nl is not None:
            layer_loads(nl, "s5w")
        yield
        mm_fm(h, 2, hB, ("hB", h), 8, lambda mt, pb, pk: actf(gB[:, mt, t], pb[:, 0:HT], AF.Silu, [pk], [("gB", h)]))
        if h == 1 and nl is not None:
            layer_loads(nl, "slot2")
        yield
        for b2 in range(2):
            blk = 2 * h + b2
            first = (ti == 0 and blk == 0)
            for g in range(2):
                combos = [(par, kb) for par in range(2) for kb in ((1,) if first else (0, 1))]
                pts = []
                for ci_, (par, kb) in enumerate(combos):
                    s_ = blk + kb
                    pb, pk = bankf()
                    lo, hi = par * 64, (par + 1) * 64
                    mm(pb[:, :], KT[lo:hi, g, s_ * 128:(s_ + 1) * 128], zB[lo:hi, g * 4:(g + 1) * 4, blk * 128:(blk + 1) * 128],
                       True, True, [("KT", s_), ("zB", h)], [pk])
                    slot = h * 4 + ci_
                    actf(PT[:, slot, :], pb[:, :], AF.Exp, [pk], [("PT", slot)], scale=0.125)
                    pt3 = PT[:, slot, :].rearrange("p (k q) -> p k q", k=4)
                    tt("pool", pt3, pt3, mask2[:, kb, :].unsqueeze(1).broadcast_to([128, 4, 128]), ALU.mult,
                       [("PT", slot), "mask2"], [("PT", slot)])
                    pts.append((par, s_, slot))
                po, pko = bankf()
                pd, pkd = bankf()
                for i_, (par, s_, slot) in enumerate(pts):
                    mm(po[:, :], Vpad[:, s_, g, par, :], PT[:, slot, :], i_ == 0, i_ == len(pts) - 1, [("Vpad", s_), ("PT", slot)], [pko])
                for i_, (par, s_, slot) in enumerate(pts):
                    mm(pd[:, :], ones2[:, par, :], PT[:, slot, :], i_ == 0, i_ == len(pts) - 1, ["ones2", ("PT", slot)], [pkd])
                tq0 = tmpf[:, 0, t]
                tq1 = tmpf[:, 1, t]
                hq = slice(0, 128) if False else None
                for c2 in range(2):
                    ks = slice(c2 * 2, c2 * 2 + 2)
                    den = tq0.rearrange("p (k q) -> p k q", k=2)
                    tt("dve", den, pd[:, c2 * 256:(c2 + 1) * 256].rearrange("p (k q) -> p k q", k=2),
                       sinke[:, si, g * 4 + c2 * 2:g * 4 + c2 * 2 + 2].unsqueeze(2).broadcast_to([128, 2, 128]), ALU.add,
                       [pkd, ("sinke", si)], [("t1", h)])
                    actf(tq0, tq0, AF.Ln, [("t1", h)], [("t1", h)])
                    actf(tq0, tq0, AF.Exp, [("t1", h)], [("t1", h)], scale=-1.0)
                    tt("dve", tq1, po[:, c2 * 256:(c2 + 1) * 256], tq0, ALU.mult, [pko, ("t1", h)], [("t2", h)])
                    gv = gB[:, g * 4 + c2 * 2:g * 4 + c2 * 2 + 2, blk * 128:(blk + 1) * 128]
                    tt("dve", gv, gv, tq1.rearrange("p (k q) -> p k q", k=2), ALU.mult, [("t2", h), ("gB", h)], [("gB", h)])
            if b2 == 1 and h == 1:
                copy("pool", KTc[:, si, :, :], KT[:, :, 512:640], [("KT", 4)], ["KTc"])
                copy("pool", Vc[:, si, :, :, :], Vpad[:, 4, :, :, :], [("Vpad", 4)], ["Vc"])
            yield
        out_proj(h, 0)
        if h == 1 and nl is not None:
            layer_loads(nl, "slot0")
        yield

    def tile_gen(ti, h):
        S.epoch = ti
        t = tk(h)
        t0 = ti * T + h * HT
        for b2 in range(2):
            blk = 2 * h + b2
            xs = xtok[:, b2, :]
            S.dma("sp", xs, x[t0 + b2 * 128:t0 + (b2 + 1) * 128, :], W=["HP"], semkey="xin")
            for hh in range(2):
                pb, pk = bankf()
                for q in range(4):
                    kc = hh * 4 + q
                    tr(pb[:, q * 128:(q + 1) * 128], xs[:, kc * 128:(kc + 1) * 128], identf, ["HP", "identf"], [pk])
                copy(ev_eng(), xB[:, hh * 4:(hh + 1) * 4, blk * 128:(blk + 1) * 128],
                     pb[:, :].rearrange("p (k t) -> p k t", k=4), [pk], [("xB", h)])
        yield
        for li, l in enumerate(layers):
            g_ = ssm_gen(ti, h, li, l) if l % 2 == 0 else att_gen(ti, h, li, l)
            for _ in g_:
                yield
        rmsnorm(h)
        yh = yB[:, :, t]
        norm_apply(h, 4, yh, "SS")
        yield
        for b2 in range(2):
            blk = 2 * h + b2
            xs = xtok[:, b2, :]
            for hh in range(2):
                pb, pk = bankf()
                for q in range(4):
                    kc = hh * 4 + q
                    tr(pb[:, q * 128:(q + 1) * 128], yB[:, kc, blk * 128:(blk + 1) * 128], identf, ["SS", "identf"], [pk])
                copy(ev_eng(), xs[:, hh * 512:(hh + 1) * 512], pb[:, :], [pk], ["HP"])
            S.dma("sp", out[t0 + b2 * 128:t0 + (b2 + 1) * 128, :], xs, R=["HP"], W=["out"], semkey="xin")
        yield

    if n_tiles > 0 and len(layers) > 0:
        for w_ in ("slot0", "slot1", "slot2", "s5w"):
            layer_loads(layers[0], w_)
    LAG = 1

    def chain(h):
        for ti in range(n_tiles):
            for _ in tile_gen(ti, h):
                yield

    gens = [chain(0), chain(1)]
    alive = [True, True]
    step = 0
    while any(alive):
        for hh_ in range(2):
            if not alive[hh_]:
                continue
            if hh_ == 1 and step < LAG and alive[0]:
                continue
            try:
                next(gens[hh_])
            except StopIteration:
                alive[hh_] = False
        step += 1
    S.barrier()

    with nc.Block() as block:
        S.emit(block)
    stack.close()
    return nc


def host_inputs(inputs, layers=(0, 1, 2, 3)):
    f32 = np.float32
    m = {}
    m["c_identf"] = np.eye(128, dtype=f32)
    q = np.arange(128)
    kj = np.arange(128)
    mprev = (kj[:, None] > q[None, :]).astype(f32)
    mcur = (kj[:, None] <= q[None, :]).astype(f32)
    m["c_mask2"] = np.ascontiguousarray(np.stack([mprev, mcur], axis=1))
    o2 = np.zeros((128, 2, 128), f32)
    o2[:, 0, 0:64] = 1.0
    o2[:, 1, 64:128] = 1.0
    m["c_ones2"] = o2
    ii = np.arange(128) // 16
    m["c_tri"] = (ii[None, :] >= ii[:, None]).astype(f32)
    m["c_fidx"] = (np.arange(128) % 32).astype(f32).reshape(128, 1)
    m["c_sgn"] = np.where((np.arange(128) % 64) < 32, -1.0, 1.0).astype(f32).reshape(128, 1)
    m["c_pos"] = np.arange(SEQ, dtype=f32)

    def col(v):
        return np.ascontiguousarray(np.asarray(v, f32).reshape(8, 128).T)

    m["fnorm_col"] = col(inputs["final_norm"])
    for l in layers:
        p = "l%d_" % l
        m[p + "norm_col"] = col(inputs[p + "norm"])
        if l % 2 == 0:
            m[p + "w_in"] = np.ascontiguousarray(inputs[p + "w_in"], f32)
            m[p + "w_glu"] = np.ascontiguousarray(inputs[p + "w_glu"], f32)
            m[p + "w_out"] = np.ascontiguousarray(inputs[p + "w_out"], f32)
            m[p + "bglu_col"] = col(inputs[p + "b_glu"])
            m[p + "aRe"] = np.ascontiguousarray(np.tile(np.asarray(inputs[p + "a_re"], f32).T, (2, 1)))
            m[p + "aIm"] = np.ascontiguousarray(np.tile(np.asarray(inputs[p + "a_im"], f32).T, (2, 1)))
            m[p + "ls"] = np.ascontiguousarray(np.tile(np.asarray(inputs[p + "log_step"], f32)[None, :], (128, 1)))
            m[p + "dcol"] = np.ascontiguousarray(np.tile(np.asarray(inputs[p + "d"], f32).reshape(64, 16).T, (8, 1)))
            m[p + "cRe"] = np.ascontiguousarray(np.tile(np.asarray(inputs[p + "c_re"], f32).transpose(2, 0, 1), (2, 1, 1)))
            m[p + "cIm"] = np.ascontiguousarray(np.tile(np.asarray(inputs[p + "c_im"], f32).transpose(2, 0, 1), (2, 1, 1)))
            m[p + "bRe"] = np.ascontiguousarray(np.tile(np.asarray(inputs[p + "b_re"], f32).transpose(1, 0, 2), (2, 1, 1)))
            m[p + "bIm"] = np.ascontiguousarray(np.tile(np.asarray(inputs[p + "b_im"], f32).transpose(1, 0, 2), (2, 1, 1)))
        else:
            w = np.asarray(inputs[p + "w_in"], f32)
            m[p + "w_in"] = np.ascontiguousarray(w)
            dd = np.arange(64)
            partner = np.where(dd < 32, dd + 32, dd - 32)
            qperm = (np.arange(16)[:, None] * 64 + partner[None, :]).reshape(-1)
            m[p + "w_qp"] = np.ascontiguousarray(w[:, qperm])
            k0 = w[:, 1024:1088]
            k1 = w[:, 1088:1152]
            m[p + "w_kk"] = np.ascontiguousarray(np.concatenate(
                [k0, k0, k1, k1, k0[:, partner], k0[:, partner], k1[:, partner], k1[:, partner]], axis=1))
            m[p + "w_out"] = np.ascontiguousarray(inputs[p + "w_out"], f32)
            s = np.asarray(inputs[p + "sinks"], f32)
            sc = np.zeros((128, 8), f32)
            for kc in range(8):
                sc[0:64, kc] = s[2 * kc]
                sc[64:128, kc] = s[2 * kc + 1]
            m[p + "sink_col"] = sc
    return m


_NC_CACHE = {}


def kernel(**inputs):
    layers = (0, 1, 2, 3)
    if "full" not in _NC_CACHE:
        _NC_CACHE["full"] = build(16, layers)
    nc = _NC_CACHE["full"]
    shared = host_inputs(inputs, layers)
    xs = np.asarray(inputs["x"], np.float32)
    in_maps = []
    for c in range(8):
        mc = dict(shared)
        mc["x"] = np.ascontiguousarray(xs[c])
        in_maps.append(mc)
    res = run_bass_kernel_spmd(nc, in_maps, core_ids=list(range(8)))
    return np.stack([np.asarray(r["out"], np.float32) for r in res.results], axis=0)
```

```python
import math
import os
from contextlib import ExitStack

import numpy as np
import concourse.bass as bass
import concourse.mybir as mybir
from concourse.bass_utils import run_bass_kernel_spmd

F32 = mybir.dt.float32
BF16 = mybir.dt.bfloat16
I32 = mybir.dt.int32
AF = mybir.ActivationFunctionType
ALU = mybir.AluOpType

T = 512
NCH = 64
NH = 32
SEQ = 8192
DM = 1024
EPS = 1e-5
TWO_PI = 2.0 * math.pi
SIN_SCALE = 6.28318


class Sched:
    CE = ("pe", "act", "dve", "pool")

    def __init__(self, nc, stack):
        self.nc = nc
        self.stack = stack
        self.q = {e: [] for e in ("pe", "act", "dve", "pool", "sp")}
        self.esem = {e: [stack.enter_context(nc.semaphore("c_%s_%d" % (e, i))) for i in range(4)] for e in self.CE}
        self.ecnt = {(e, i): 0 for e in self.CE for i in range(4)}
        self.epoch = 0
        self.last_w = {}
        self.readers = {}
        self.known = {e: {} for e in self.q}
        self.dsem = {}
        self.dcnt = {}
        self.semkey_of = {}
        self.all_tokens = {}

    def _deps(self, eng, R, W):
        deps = []
        for k in R:
            t = self.last_w.get(k)
            if t is not None:
                deps.append(t)
        for k in W:
            t = self.last_w.get(k)
            if t is not None:
                deps.append(t)
            deps.extend(self.readers.get(k, ()))
        waits = {}
        for (sem, val, e2) in deps:
            if e2 == eng and eng == "pe":
                continue
            if e2 == "dma":
                val = self.dcnt[self.semkey_of[sem]]
            if self.known[eng].get(sem, 0) >= val:
                continue
            if waits.get(sem, 0) < val:
                waits[sem] = val
        for sem, val in waits.items():
            self.q[eng].append(lambda e, s=sem, v=val: e.wait_ge(s, v))
            self.known[eng][sem] = val

    def _record(self, tok, R, W):
        for k in R:
            self.readers.setdefault(k, []).append(tok)
        for k in W:
            self.last_w[k] = tok
            self.readers[k] = []
        self.all_tokens[tok[0]] = max(self.all_tokens.get(tok[0], 0), tok[1])

    def op(self, eng, fn, R=(), W=()):
        self._deps(eng, R, W)
        i = self.epoch % 4
        sem = self.esem[eng][i]
        self.ecnt[(eng, i)] += 1
        val = self.ecnt[(eng, i)]
        self.q[eng].append(lambda e, f=fn, s=sem: f(e).then_inc(s, 1))
        self._record((sem, val, eng), R, W)

    def dma(self, queue, out, in_, R=(), W=(), semkey=None):
        self._deps(queue, R, W)
        if semkey not in self.dsem:
            self.dsem[semkey] = self.stack.enter_context(self.nc.semaphore("d_%s" % semkey))
            self.dcnt[semkey] = 0
            self.semkey_of[self.dsem[semkey]] = semkey
        sem = self.dsem[semkey]
        self.dcnt[semkey] += 16
        val = self.dcnt[semkey]
        self.q[queue].append(lambda e, o=out, i=in_, s=sem: e.dma_start(out=o, in_=i).then_inc(s, 16))
        self._record((sem, val, "dma"), R, W)

    def barrier(self, skip=()):
        for eng in self.q:
            for sem, val in self.all_tokens.items():
                if self.semkey_of.get(sem) in skip:
                    continue
                if self.known[eng].get(sem, 0) < val:
                    self.q[eng].append(lambda e, s=sem, v=val: e.wait_ge(s, v))
                    self.known[eng][sem] = val

    def emit(self, block):
        q = self.q

        @block.tensor
        def _(e):
            for f in q["pe"]:
                f(e)

        @block.scalar
        def _(e):
            for f in q["act"]:
                f(e)

        @block.vector
        def _(e):
            for f in q["dve"]:
                f(e)

        @block.gpsimd
        def _(e):
            for f in q["pool"]:
                f(e)

        @block.sync
        def _(e):
            for f in q["sp"]:
                f(e)


SSM_HOST = ("norm", "bglu", "aRe", "aIm", "ls", "cRe", "cIm", "bRe", "bIm", "dcol")


def build(n_tiles=16, layers=(0, 1, 2, 3)):
    nc = bass.Bass("TRN2", target_bir_lowering=False)
    stack = ExitStack()
    S = Sched(nc, stack)

    def din(name, shape, dt=F32):
        return nc.dram_tensor(name, list(shape), dt, kind="ExternalInput").ap()

    def dscr(name, shape, dt):
        return nc.dram_tensor(name, list(shape), dt, kind="Internal").ap()

    x = din("x", [SEQ, DM])
    out = nc.dram_tensor("out", [SEQ, DM], F32, kind="ExternalOutput").ap()
    c_identf = din("c_identf", [128, 128])
    c_mask2 = din("c_mask2", [128, 2, 128])
    c_ones2 = din("c_ones2", [128, 2, 128])
    c_tri = din("c_tri", [128, 128])
    c_fidx = din("c_fidx", [128, 1])
    c_sgn = din("c_sgn", [128, 1])
    c_pos = din("c_pos", [SEQ])
    fnorm = din("fnorm_col", [128, 8])
    L = {}
    for l in layers:
        p = "l%d_" % l
        d = {}
        d["norm"] = din(p + "norm_col", [128, 8])
        if l % 2 == 0:
            d["w_in"] = din(p + "w_in", [DM, 2048])
            d["w_glu"] = din(p + "w_glu", [DM, DM])
            d["w_out"] = din(p + "w_out", [DM, DM])
            d["bglu"] = din(p + "bglu_col", [128, 8])
            for nm in ("aRe", "aIm", "ls", "dcol"):
                d[nm] = din(p + nm, [128, 64])
            for nm in ("cRe", "cIm", "bRe", "bIm"):
                d[nm] = din(p + nm, [128, 64, 16])
            d["w_in_b"] = dscr(p + "w_in_b", [DM, 2048], BF16)
            d["w_glu_b"] = dscr(p + "w_glu_b", [DM, DM], BF16)
            d["w_out_b"] = dscr(p + "w_out_b", [DM, DM], BF16)
            d["WS3"] = dscr(p + "WS3", [128, 64, 192], BF16)
            d["WHK"] = dscr(p + "WHK", [128, 2, 64, 128], BF16)
        else:
            d["w_in"] = din(p + "w_in", [DM, 2304])
            d["w_qp"] = din(p + "w_qp", [DM, DM])
            d["w_kk"] = din(p + "w_kk", [DM, 512])
            d["w_out"] = din(p + "w_out", [DM, DM])
            d["sink"] = din(p + "sink_col", [128, 8])
            d["w_in_b"] = dscr(p + "w_in_b", [DM, 2304], BF16)
            d["w_qp_b"] = dscr(p + "w_qp_b", [DM, DM], BF16)
            d["w_kk_b"] = dscr(p + "w_kk_b", [DM, 512], BF16)
            d["w_out_b"] = dscr(p + "w_out_b", [DM, DM], BF16)
        L[l] = d
    has_att = any(l % 2 == 1 for l in layers)
    ropeC = dscr("ropeC", [128, SEQ], F32)
    ropeS = dscr("ropeS", [128, SEQ], F32)

    ARENA_W = 53000
    arena = stack.enter_context(nc.sbuf_tensor("arena", [128, ARENA_W], F32))
    cur = [0]

    def carve(shape, dt, at=None):
        n = int(np.prod(shape[1:]))
        words = n if dt in (F32, I32) else (n + 1) // 2
        if at is None:
            off = cur[0]
            cur[0] += words
            assert cur[0] <= ARENA_W, ("sbuf overflow", cur[0])
        else:
            off = at
            assert off + words <= ARENA_W
        v = arena[:, off:off + words]
        if dt != F32:
            v = v.bitcast(dt)
        if len(shape) > 2:
            names = " ".join("d%d" % i for i in range(1, len(shape)))
            kw = {"d%d" % i: shape[i] for i in range(1, len(shape))}
            v = v.rearrange("p (%s) -> p %s" % (names, names), **kw)
        return v

    identf = carve([128, 128], F32)
    identb = carve([128, 128], BF16)
    onesb = carve([128, 128], BF16)
    mask2 = carve([128, 2, 128], BF16)
    ones2 = carve([128, 2, 128], BF16)
    stage_c = carve([128, 2, 128], F32)
    normc = carve([128, 5, 8], F32)
    bgluc = carve([128, 2, 8], F32)
    sinke = carve([128, 2, 8], F32)
    CAB = carve([128, 2, 2, 2, 64], F32)
    HHc = carve([128, 2, 2, 64], F32)
    pro0 = cur[0]
    rstd = carve([128, T], F32)
    tmpf = carve([128, 2, T], F32)
    rtmp = tmpf[:, 1, :]
    sigb = carve([128, 2, T], BF16)
    cosT = carve([128, T], F32)
    sinT = carve([128, T], F32)
    KT = carve([128, 2, 640], BF16)
    Vpad = carve([128, 5, 2, 2, 128], BF16)
    KTc = carve([128, 2, 2, 128], BF16)
    Vc = carve([128, 2, 2, 2, 128], BF16)
    xB = carve([128, 8, T], F32)
    hB = carve([128, 8, T], BF16)
    gB = carve([128, 8, T], BF16)
    zB = carve([128, 8, T], BF16)
    main0 = cur[0]
    tokA_off = cur[0]
    tokA = carve([128, 8, DM], BF16)
    XC = carve([128, 64, 64], BF16)
    HP_off = cur[0]
    HP = carve([128, 64, 64], BF16)
    xtok = carve([128, 2, DM], F32, at=HP_off)
    SS = carve([128, NH, 2, 64], F32)
    s5w = carve([128, 64 * 256], BF16)
    wsl = [carve([128, 8, DM], BF16) for _ in range(3)]
    PT = tokA.rearrange("p a b -> p (a b)")[:, 0:8 * T].rearrange("p (s t) -> p s t", s=8)
    yB = SS.rearrange("p a b c -> p (a b c)").rearrange("p (k t) -> p k t", k=8)
    WS3 = s5w[:, 0:64 * 192].rearrange("p (g r) -> p g r", g=64)
    WHK = s5w.rearrange("p (s g r) -> p s g r", s=2, g=64)
    kvw = s5w[:, 0:8 * 640].rearrange("p (k n) -> p k n", k=8)

    psf = [stack.enter_context(nc.psum_tensor("psf%d" % i, [128, 512], F32)) for i in range(6)]
    psb = [stack.enter_context(nc.psum_tensor("psb%d" % i, [128, 1024], BF16)) for i in range(2)]
    rr = {"f": 0, "b": 0, "e": 0}

    def bankf():
        i = rr["f"] % 6
        rr["f"] += 1
        return psf[i], ("psf", i)

    def bankb():
        i = rr["b"] % 2
        rr["b"] += 1
        return psb[i], ("psb", i)

    def ev_eng():
        rr["e"] += 1
        return "act" if rr["e"] % 2 else "dve"

    def copy(eng, o, i, R, W):
        if eng != "act" and o.dtype == i.dtype:
            S.op(eng, lambda e: e.tensor_single_scalar(out=o, in_=i, scalar=1.0, op=ALU.mult), R, W)
            return
        if eng == "act":
            S.op("act", lambda e: e.activation(out=o, in_=i, func=AF.Copy), R, W)
        else:
            S.op(eng, lambda e: e.tensor_copy(out=o, in_=i), R, W)

    def tt(eng, o, a, b, op, R, W):
        S.op(eng, lambda e: e.tensor_tensor(out=o, in0=a, in1=b, op=op), R, W)

    def ts(eng, o, a, s1, op, R, W):
        S.op(eng, lambda e: e.tensor_single_scalar(out=o, in_=a, scalar=s1, op=op), R, W)

    def actf(o, i, func, R, W, scale=1.0, bias=0.0):
        S.op("act", lambda e: e.activation(out=o, in_=i, func=func, scale=scale, bias=bias), R, W)

    def mm(o, lhsT, rhs, start, stop, R, W):
        S.op("pe", lambda e: e.matmul(o, lhsT=lhsT, rhs=rhs, start=start, stop=stop), R, W)

    def tr(o, i, ident, R, W):
        S.op("pe", lambda e: e.transpose(o, i, ident), R, W)

    S.dma("sp", identf, c_identf, W=["identf"], semkey="c0")
    copy("dve", identb, identf, ["identf"], ["identb"])
    S.op("dve", lambda e: e.memset(onesb, 1.0), (), ["onesb"])
    S.dma("sp", stage_c, c_mask2, W=["stage_c"], semkey="c1")
    S.op("dve", lambda e: e.tensor_scalar(out=mask2, in0=stage_c, scalar1=-1.0, scalar2=30000.0, op0=ALU.add, op1=ALU.mult),
         ["stage_c"], ["mask2"])
    S.dma("sp", stage_c, c_ones2, R=(), W=["stage_c"], semkey="c1")
    copy("dve", ones2, stage_c, ["stage_c"], ["ones2"])
    S.op("dve", lambda e: e.memset(HHc, 0.0), (), ["HHc"])
    for li, l in enumerate(layers):
        S.dma("sp", normc[:, li, :], L[l]["norm"], W=[("normc", li)], semkey="c2")
    S.dma("sp", normc[:, 4, :], fnorm, W=[("normc", 4)], semkey="c2")

    for l in layers:
        d = L[l]
        names = ("w_in", "w_glu", "w_out") if l % 2 == 0 else ("w_in", "w_qp", "w_kk", "w_out")
        for nm in names:
            for rb in range(8):
                S.dma("pool", d[nm + "_b"][rb * 128:(rb + 1) * 128, :], d[nm][rb * 128:(rb + 1) * 128, :], W=[("wscr", l, nm, rb)], semkey="wc_%d_%s" % (l, nm))

    if has_att:
        pw = 2048
        base = pro0
        posb = carve([128, pw], F32, at=base)
        tq = carve([128, pw], F32, at=base + pw)
        tqi = carve([128, pw], I32, at=base + 2 * pw)
        tqf = carve([128, pw], F32, at=base + 3 * pw)
        res = carve([128, pw], F32, at=base + 4 * pw)
        fidx = carve([128, 1], F32, at=base + 5 * pw)
        invf = carve([128, 1], F32, at=base + 5 * pw + 1)
        sgn = carve([128, 1], F32, at=base + 5 * pw + 2)
        S.dma("sp", fidx, c_fidx, W=["pp"], semkey="c3")
        S.dma("sp", sgn, c_sgn, W=["pp"], semkey="c3")
        actf(invf, fidx, AF.Exp, ["pp"], ["pp"], scale=-math.log(10000.0) / 32.0)
        for ch in range(SEQ // pw):
            S.dma("sp", posb, c_pos[ch * pw:(ch + 1) * pw].partition_broadcast(128), W=["pp"], semkey="c4")
            S.op("dve", lambda e: e.tensor_scalar_mul(out=posb, in0=posb, scalar1=invf[:, 0:1]), ["pp", "pp"], ["pp"])
            ts("dve", posb, posb, 1.0 / TWO_PI, ALU.mult, ["pp"], ["pp"])
            for which, shift, dst in (("s", 0.0, ropeS), ("c", 0.25, ropeC)):
                ts("dve", tq, posb, shift, ALU.add, ["pp"], ["pp"])
                copy("dve", tqi, tq, ["pp"], ["pp"])
                copy("dve", tqf, tqi, ["pp"], ["pp"])
                tt("dve", tq, tq, tqf, ALU.subtract, ["pp", "pp"], ["pp"])
                actf(res, tq, AF.Sin, ["pp"], ["pp"], scale=SIN_SCALE)
                if which == "s":
                    S.op("dve", lambda e: e.tensor_scalar_mul(out=res, in0=res, scalar1=sgn[:, 0:1]), ["pp", "pp"], ["pp"])
                S.dma("sp", dst[:, ch * pw:(ch + 1) * pw], res, R=["pp"], W=[("rope", which)], semkey="c5")
        for li, l in enumerate(layers):
            if l % 2 == 1:
                si = (l // 2)
                S.dma("sp", sinke[:, si, :], L[l]["sink"], W=[("sinke", si)], semkey="c6")
                actf(sinke[:, si, :], sinke[:, si, :], AF.Exp, [("sinke", si)], [("sinke", si)])

    for l in layers:
        if l % 2:
            continue
        d = L[l]
        si = l // 2
        o = [pro0]

        def pc(shape, dt=F32):
            n = int(np.prod(shape[1:]))
            w = n if dt in (F32, I32) else (n + 1) // 2
            v = carve(shape, dt, at=o[0])
            o[0] += w
            return v
        NK = 17
        aRe = pc([128, 64]); aIm = pc([128, 64]); ls = pc([128, 64]); dcol = pc([128, 64])
        cRe = pc([128, 64, 16]); cIm = pc([128, 64, 16]); bRe = pc([128, 64, 16]); bIm = pc([128, 64, 16])
        lr = pc([128, 64]); lim = pc([128, 64])
        Er = pc([128, NK, 64]); Ei = pc([128, NK, 64]); Fr = pc([128, NK, 64]); Fi = pc([128, NK, 64])
        cr = pc([128, 64]); ci = pc([128, 64]); w1 = pc([128, 64]); w2 = pc([128, 64])
        WST = pc([128, 64, 8, 16]); WS2T = pc([128, 64, 8, 16], BF16); WHf = pc([128, 64, 8, 16])
        tri = pc([128, 128]); k0t = pc([128, 128])
        U1 = o[0]
        u1 = pc([128, 64, 8, 16])
        U2 = o[0]
        u2 = pc([128, 64, 8, 16])
        o2 = [U1]
        def pc2(shape, dt=F32):
            n = int(np.prod(shape[1:]))
            v = carve(shape, dt, at=o2[0])
            o2[0] += n
            return v
        ex = pc2([128, NK, 64]); tq_ = pc2([128, NK, 64]); t2_ = pc2([128, NK, 64]); ti_ = pc2([128, NK, 64], I32)
        tf_ = pc2([128, NK, 64]); sn = pc2([128, NK, 64]); cs = pc2([128, NK, 64])
        assert o2[0] <= U2
        WS3s = carve([128, 64, 192], BF16, at=U1)
        WHKs = carve([128, 2, 64, 128], BF16, at=U2)
        assert o[0] <= ARENA_W, o[0]
        k = "pp"
        for nm, tl in (("aRe", aRe), ("aIm", aIm), ("ls", ls), ("dcol", dcol), ("cRe", cRe), ("cIm", cIm), ("bRe", bRe), ("bIm", bIm)):
            S.dma("sp", tl, d[nm], W=[k + nm], semkey="pp")
        S.dma("sp", tri, c_tri, W=[k + "tri"], semkey="pp")
        S.dma("sp", bgluc[:, si, :], d["bglu"], W=[("bgluc", si)], semkey="pp")
        lk = [k] + [k + nm for nm in ("aRe", "aIm", "ls", "dcol", "cRe", "cIm", "bRe", "bIm", "tri")]
        actf(ls, ls, AF.Exp, lk, [k, "WS3s", "WHKs"])
        tt("dve", lr, aRe, ls, ALU.mult, [k], [k])
        tt("dve", lim, aIm, ls, ALU.mult, [k], [k])
        for kk in range(-8, 9):
            idx = kk + 8
            actf(ex[:, idx, :], lr, AF.Exp, [k], [k], scale=float(kk))
            ts("dve", tq_[:, idx, :], lim, kk / TWO_PI, ALU.mult, [k], [k])
        for dst, shift in ((sn, 0.0), (cs, 0.25)):
            ts("dve", t2_, tq_, shift, ALU.add, [k], [k])
            copy("dve", ti_, t2_, [k], [k])
            copy("dve", tf_, ti_, [k], [k])
            tt("dve", t2_, t2_, tf_, ALU.subtract, [k], [k])
            actf(dst, t2_, AF.Sin, [k], [k], scale=SIN_SCALE)
        tt("dve", Er, ex, cs, ALU.mult, [k], [k])
        tt("dve", Ei, ex, sn, ALU.mult, [k], [k])
        E1r = Er[:, 9, :]; E1i = Ei[:, 9, :]
        ts("dve", w1, E1r, -1.0, ALU.add, [k], [k])
        tt("dve", cr, w1, aRe, ALU.mult, [k], [k])
        tt("dve", w2, E1i, aIm, ALU.mult, [k], [k])
        tt("dve", cr, cr, w2, ALU.add, [k], [k])
        tt("dve", ci, E1i, aRe, ALU.mult, [k], [k])
        tt("dve", w2, w1, aIm, ALU.mult, [k], [k])
        tt("dve", ci, ci, w2, ALU.subtract, [k], [k])
        tt("dve", w1, aRe, aRe, ALU.mult, [k], [k])
        tt("dve", w2, aIm, aIm, ALU.mult, [k], [k])
        tt("dve", w1, w1, w2, ALU.add, [k], [k])
        S.op("dve", lambda e: e.reciprocal(out=w1, in_=w1), [k], [k])
        tt("dve", cr, cr, w1, ALU.mult, [k], [k])
        tt("dve", ci, ci, w1, ALU.mult, [k], [k])
        crb = cr.unsqueeze(1).broadcast_to([128, NK, 64])
        cib = ci.unsqueeze(1).broadcast_to([128, NK, 64])
        tt("dve", Fr, Er, crb, ALU.mult, [k], [k])
        tt("dve", t2_, Ei, cib, ALU.mult, [k], [k])
        tt("dve", Fr, Fr, t2_, ALU.subtract, [k], [k])
        tt("dve", Fi, Er, cib, ALU.mult, [k], [k])
        tt("dve", t2_, Ei, crb, ALU.mult, [k], [k])
        tt("dve", Fi, Fi, t2_, ALU.add, [k], [k])

        def kview(v, lo, hi, k0, rev):
            vv = v[lo:hi, :, :].rearrange("p k g -> p g k")
            vv = vv[:, :, k0 - 7:k0 + 1][:, :, ::-1] if rev else vv[:, :, k0:k0 + 8]
            return vv.unsqueeze(3).broadcast_to([hi - lo, 64, 8, 16])

        def bview(v, lo, hi):
            return v[lo:hi].unsqueeze(2).broadcast_to([hi - lo, 64, 8, 16])

        for dstT, k0 in ((WST, 15), (WS2T, 7)):
            tt("dve", u1[0:64], bview(bRe, 0, 64), kview(Fr, 0, 64, k0, True), ALU.mult, [k], [k])
            tt("dve", u2[0:64], bview(bIm, 0, 64), kview(Fi, 0, 64, k0, True), ALU.mult, [k], [k])
            tt("dve", dstT[0:64], u1[0:64], u2[0:64], ALU.subtract, [k], [k])
            tt("dve", u1[64:128], bview(bIm, 64, 128), kview(Fr, 64, 128, k0, True), ALU.mult, [k], [k])
            tt("dve", u2[64:128], bview(bRe, 64, 128), kview(Fi, 64, 128, k0, True), ALU.mult, [k], [k])
            tt("dve", dstT[64:128], u1[64:128], u2[64:128], ALU.add, [k], [k])
        tt("dve", u1[0:64], bview(cRe, 0, 64), kview(Er, 0, 64, 9, False), ALU.mult, [k], [k])
        tt("dve", u2[0:64], bview(cIm, 0, 64), kview(Ei, 0, 64, 9, False), ALU.mult, [k], [k])
        tt("dve", WHf[0:64], u1[0:64], u2[0:64], ALU.subtract, [k], [k])
        tt("dve", u1[64:128], bview(cRe, 64, 128), kview(Ei, 64, 128, 9, False), ALU.mult, [k], [k])
        tt("dve", u2[64:128], bview(cIm, 64, 128), kview(Er, 64, 128, 9, False), ALU.mult, [k], [k])
        tt("dve", u1[64:128], u1[64:128], u2[64:128], ALU.add, [k], [k])
        ts("dve", WHf[64:128], u1[64:128], -1.0, ALU.mult, [k], [k])
        for s2 in range(2):
            copy("dve", CAB[:, si, 0, s2, :], Er[:, 16, :], [k], [("CAB", si)])
        ts("dve", CAB[0:64, si, 1, 0, :], Ei[0:64, 16, :], -1.0, ALU.mult, [k], [("CAB", si)])
        copy("dve", CAB[64:128, si, 1, 0, :], Ei[64:128, 16, :], [k], [("CAB", si)])
        ts("dve", CAB[:, si, 1, 1, :], CAB[:, si, 1, 0, :], -1.0, ALU.mult, [("CAB", si)], [("CAB", si)])
        copy("act", WHKs[:, 0, :, :], WHf.rearrange("p g j c -> p g (j c)"), [k], ["WHKs"])
        for g in range(64):
            pb, pk = bankf()
            pb2, pk2 = bankf()
            tr(pb[:, 0:128], WST[:, g, :, :].rearrange("p j c -> p (j c)"), identf, [k, "identf"], [pk])
            mm(pb2[:, 0:128], WS2T[:, g, :, :].rearrange("p j c -> p (j c)"),
               WHKs[:, 0, g, :], True, True, [k, "WHKs"], [pk2])
            copy("act", WS3s[:, g, 0:128], pb[:, 0:128], [pk], ["WS3s"])
            copy("act", WS3s[:, g, 128:192], pb[:, 0:64], [pk], ["WS3s"])
            tt("dve", k0t, pb2[:, 0:128], tri, ALU.mult, [pk2, k], ["k0t"])
            S.op("dve", lambda e, g=g: e.scalar_tensor_tensor(out=WHKs[:, 1, g, :], in0=identf, scalar=dcol[:, g:g + 1],
                                                               in1=k0t, op0=ALU.mult, op1=ALU.add),
                 ["k0t", k, "identf"], ["WHKs"])
        S.dma("sp", d["WS3"], WS3s, R=["WS3s"], W=[("s5scr", l)], semkey="pst")
        S.dma("sp", d["WHK"], WHKs, R=["WHKs"], W=[("s5scr", l)], semkey="pst")
    S.barrier(skip=[kk_ for kk_ in S.dsem if kk_.startswith("wc_")])
    S.op("pool", lambda e: e.memset(Vpad, 0.0), (), ["Vpad"])
    S.op("pool", lambda e: e.memset(KT, 0.0), (), ["KT"])
    S.op("pool", lambda e: e.memset(KTc, 0.0), (), ["KTc"])
    S.op("pool", lambda e: e.memset(Vc, 0.0), (), ["Vc"])

    HT = T // 2
    NHC = NH

    def tk(h):
        return slice(h * HT, (h + 1) * HT)

    def load_w(slot, src, cols, l, nm):
        ncol = cols[1] - cols[0]
        S.dma("sp", wsl[slot][:, :, 0:ncol], src[:, cols[0]:cols[1]].rearrange("(k p) n -> p k n", p=128),
              R=[("wscr", l, nm, rb) for rb in range(8)], W=[("wsl", slot)], semkey="wsl%d" % slot)

    def load_s5w(l, which):
        d = L[l]
        if which == "WS3":
            S.dma("sp", WS3, d["WS3"], R=[("s5scr", l)], W=["s5w"], semkey="s5w")
        elif which == "WHK":
            S.dma("sp", WHK, d["WHK"], R=[("s5scr", l)], W=["s5w"], semkey="s5w")
        else:
            S.dma("sp", kvw[:, :, 0:512], d["w_kk_b"].rearrange("(k p) n -> p k n", p=128),
                  R=[("wscr", l, "w_kk", rb) for rb in range(8)], W=["s5w"], semkey="s5w")
            S.dma("sp", kvw[:, :, 512:640], d["w_in_b"][:, 1152:1280].rearrange("(k p) n -> p k n", p=128),
                  R=[("wscr", l, "w_in", rb) for rb in range(8)], W=["s5w"], semkey="s5w")

    def layer_loads(l, which):
        d = L[l]
        if l % 2 == 0:
            if which == "slot0":
                load_w(0, d["w_in_b"], (0, 1024), l, "w_in")
            elif which == "slot1":
                load_w(1, d["w_in_b"], (1024, 2048), l, "w_in")
            elif which == "slot2":
                load_w(2, d["w_glu_b"], (0, 1024), l, "w_glu")
            else:
                load_s5w(l, "WS3")
        else:
            if which == "slot0":
                load_w(0, d["w_in_b"], (0, 1024), l, "w_in")
            elif which == "slot1":
                load_w(1, d["w_qp_b"], (0, 1024), l, "w_qp")
            elif which == "slot2":
                load_w(2, d["w_in_b"], (1280, 2304), l, "w_in")
            else:
                load_s5w(l, "KV")

    def rmsnorm(h):
        t = tk(h)
        actf(zB[:, :, t], xB[:, :, t], AF.Square, [("xB", h)], [("zB", h)])
        pb, pk = bankf()
        for kc in range(8):
            mm(pb[:, 0:HT], onesb, zB[:, kc, t], kc == 0, kc == 7, ["onesb", ("zB", h)], [pk])
        actf(tmpf[:, 1, t], pb[:, 0:HT], AF.Ln, [pk], [("t2", h)], scale=1.0 / DM, bias=EPS)
        actf(rstd[:, t], tmpf[:, 1, t], AF.Exp, [("t2", h)], [("rstd", h)], scale=-0.5)

    def norm_apply(h, li, dst, dkey):
        t = tk(h)
        for kc in range(8):
            S.op("dve", lambda e, kc=kc: e.scalar_tensor_tensor(out=dst[:, kc, :], in0=xB[:, kc, t],
                                                                 scalar=normc[:, li, kc:kc + 1], in1=rstd[:, t],
                                                                 op0=ALU.mult, op1=ALU.mult),
                 [("xB", h), ("rstd", h), ("normc", li)], [dkey])

    def mm_fm(h, slot, act, akey, mts, evac):
        t = tk(h)
        for mt in (range(mts) if isinstance(mts, int) else mts):
            pb, pk = bankf()
            for kc in range(8):
                mm(pb[:, 0:HT], wsl[slot][:, kc, mt * 128:(mt + 1) * 128], act[:, kc, t], kc == 0, kc == 7,
                   [("wsl", slot), akey], [pk])
            evac(mt, pb, pk)

    def out_proj(h, slot, mts=8):
        t = tk(h)

        def ev(mt, pb, pk):
            tt("dve", xB[:, mt, t], xB[:, mt, t], pb[:, 0:HT], ALU.add, [pk, ("xB", h)], [("xB", h)])
        mm_fm(h, slot, gB, ("gB", h), mts, ev)

    NL = len(layers)
    wtag = {}
    locks = {}

    def roles(ti, li):
        gl = ti * NL + li
        X, Z = (0, 2) if gl % 2 == 0 else (2, 0)
        return X, 1, Z

    def load_role(ti, li, role):
        l = layers[li]
        d = L[l]
        X, Y, Z = roles(ti, li)
        if l % 2 == 0:
            spec = {"E1": (Y, d["w_in_b"], (1024, 2048), "w_in"), "E2": (X, d["w_in_b"], (0, 1024), "w_in"),
                    "M": (Z, d["w_glu_b"], (0, 1024), "w_glu"), "OUT": (X, d["w_out_b"], (0, 1024), "w_out")}[role]
        else:
            spec = {"E1": (Y, d["w_qp_b"], (0, 1024), "w_qp"), "E2": (X, d["w_in_b"], (0, 1024), "w_in"),
                    "M": (Z, d["w_in_b"], (1280, 2304), "w_in"), "OUT": (X, d["w_out_b"], (0, 1024), "w_out")}[role]
        slot, src, cols, nm = spec
        load_w(slot, src, cols, l, nm)
        wtag[slot] = (ti, li, role)

    def use(ti, li, role):
        X, Y, Z = roles(ti, li)
        slot = {"E1": Y, "E2": X, "M": Z, "OUT": X}[role]
        assert wtag.get(slot) == (ti, li, role), ("weight slot content mismatch", slot, wtag.get(slot), (ti, li, role))
        return slot

    def nxt_pos(ti, li):
        if li + 1 < NL:
            return (ti, li + 1)
        if ti + 1 < n_tiles:
            return (ti + 1, 0)
        return None

    def s5w_load(ti, li, which):
        load_s5w(layers[li], which)
        wtag["s5w"] = (ti, li, which)

    def s5w_use(ti, li, which):
        assert wtag.get("s5w") == (ti, li, which), ("s5w content mismatch", wtag.get("s5w"), (ti, li, which))

    def acquire(name, me):
        while locks.get(name) not in (None, me):
            yield "BLOCK"
        locks[name] = me

    def release(name, me):
        assert locks.get(name) == me
        locks[name] = None

    def nxt(ti, li):
        if li + 1 < len(layers):
            return layers[li + 1]
        if ti + 1 < n_tiles:
            return layers[0]
        return None

    def ssm_gen(ti, h, li, l):
        d = L[l]
        si = l // 2
        t = tk(h)
        n0 = h * NHC
        r0, r1 = h * 32, h * 32 + 32
        me = (ti, li, h)
        np_ = nxt_pos(ti, li)
        rmsnorm(h)
        norm_apply(h, li, hB[:, :, t], ("hB", h))
        yield
        sl = use(ti, li, "E1")
        ev_g = lambda mt, pb, pk: actf(gB[:, mt, t], pb[:, 0:HT], AF.Silu, [pk], [("gB", h)])
        mm_fm(h, sl, hB, ("hB", h), range(0, 4), ev_g)
        yield
        mm_fm(h, sl, hB, ("hB", h), range(4, 8), ev_g)
        if h == 1 and np_ is not None:
            load_role(np_[0], np_[1], "E1")
        yield
        sl = use(ti, li, "E2")
        uA = tokA.rearrange("p a b -> p (a b)").rearrange("p (g j c) -> p g j c", g=64, j=8, c=16)
        for j in range(8):
            for nt in range(2):
                pb, pk = bankf()
                for kc in range(8):
                    mm(pb[0:32, :], hB[:, kc, h * HT + j:(h + 1) * HT:8], wsl[sl][:, kc, nt * 512:(nt + 1) * 512], kc == 0, kc == 7,
                       [("hB", h), ("wsl", sl)], [pk])
                copy(ev_eng(), uA[r0:r1, nt * 32:(nt + 1) * 32, j, :],
                     pb[0:32, :].rearrange("p (g c) -> p g c", g=32), [pk], [("tokA", h)])
            if j % 2 == 1 and j < 7:
                yield
        if h == 1:
            load_role(ti, li, "OUT")
        yield
        for gq in range(2):
            pb, pk = bankb()
            for g32 in range(32):
                g = gq * 32 + g32
                tr(pb[:, g32 * 32:(g32 + 1) * 32], uA[r0:r1, g, :, :].rearrange("p j c -> p (j c)"), identb[r0:r1, r0:r1],
                   [("tokA", h), "identb"], [pk])
            copy(ev_eng(), XC[:, gq * 32:(gq + 1) * 32, n0:n0 + NHC], pb[:, :].rearrange("p (g n) -> p g n", g=32), [pk], [("XC", h)])
        yield
        for _ in acquire("SS", me):
            yield _
        s5w_use(ti, li, "WS3")
        for gq in range(8):
            pb, pk = bankf()
            for g8 in range(8):
                g = gq * 8 + g8
                for s_ in range(2):
                    c0 = (g8 * 2 + s_) * NHC
                    mm(pb[:, c0:c0 + NHC], WS3[:, g, s_ * 64:s_ * 64 + 128], XC[:, g, n0:n0 + NHC], True, True,
                       ["s5w", ("XC", h)], [pk])
            copy(ev_eng(), SS[:, :, :, gq * 8:(gq + 1) * 8],
                 pb[:, :].rearrange("p (g s n) -> p n s g", g=8, s=2), [pk], ["SS"])
        if h == 1:
            s5w_load(ti, li, "WHK")
        yield
        for _ in acquire("HP", me):
            yield _
        HPv = HP.rearrange("p g n -> p n g")
        t1 = tmpf[:, 0, h * HT:h * HT + 128].rearrange("p (s g) -> p s g", s=2)
        t2 = tmpf[:, 1, h * HT:h * HT + 128].rearrange("p (s g) -> p s g", s=2)
        CA = CAB[:, si, 0, :, :]
        CB = CAB[:, si, 1, :, :]
        kq = [("CAB", si)]
        copy("act", HPv[:, n0, :], HHc[:, si, 0, :], ["HHc"], ["HP"])
        for n in range(NHC):
            prev = HHc[:, si, :, :] if n == 0 else SS[:, n - 1, :, :]
            pk_ = "HHc" if n == 0 else "SS"
            tt("dve", t1, prev, CA, ALU.mult, [pk_] + kq, [("t1", h)])
            tt("dve", t2, prev[:, ::-1, :], CB, ALU.mult, [pk_] + kq, [("t2", h)])
            tt("dve", t1, t1, t2, ALU.add, [("t1", h), ("t2", h)], [("t1", h)])
            tt("dve", SS[:, n, :, :], SS[:, n, :, :], t1, ALU.add, [("t1", h), "SS"], ["SS"])
            if n % 4 == 3 and n < NHC - 1:
                yield
        copy("act", HPv[:, n0 + 1:n0 + NHC, :], SS[:, 0:NHC - 1, 0, :], ["SS"], ["HP"])
        copy("pool", HHc[:, si, :, :], SS[:, NHC - 1, :, :], ["SS"], ["HHc"])
        release("SS", me)
        yield
        while wtag.get("s5w") != (ti, li, "WHK"):
            yield "BLOCK"
        zA = tokA
        for gq in range(16):
            pb, pk = bankf()
            for g4 in range(4):
                g = gq * 4 + g4
                mm(pb[0:32, g4 * 128:(g4 + 1) * 128], HP[:, g, n0:n0 + NHC], WHK[:, 0, g, :], True, False, ["HP", "s5w"], [pk])
                mm(pb[0:32, g4 * 128:(g4 + 1) * 128], XC[:, g, n0:n0 + NHC], WHK[:, 1, g, :], False, True, [("XC", h), "s5w"], [pk])
            pv = pb[0:32, :].rearrange("p (g j c) -> p j g c", g=4, j=8)
            actf(zA[r0:r1, :, gq * 64:(gq + 1) * 64].rearrange("p j (g c) -> p j g c", g=4), pv,
                 AF.Gelu_apprx_tanh, [pk], [("tokA", h)])
            if gq % 4 == 3 and gq < 15:
                yield
        release("HP", me)
        if h == 1 and np_ is not None:
            s5w_load(np_[0], np_[1], "WS3" if layers[np_[1]] % 2 == 0 else "KV")
        yield
        for jp in range(2):
            pb, pk = bankb()
            for jj in range(4):
                j = jp * 4 + jj
                for kc in range(8):
                    c0 = (jj * 8 + kc) * 32
                    tr(pb[:, c0:c0 + 32], zA[r0:r1, j, kc * 128:(kc + 1) * 128], identb[r0:r1, r0:r1],
                       [("tokA", h), "identb"], [pk])
            for jj in range(4):
                j = jp * 4 + jj
                copy(ev_eng(), zB[:, :, h * HT + j:(h + 1) * HT:8], pb[:, jj * 256:(jj + 1) * 256].rearrange("p (k n) -> p k n", k=8),
                     [pk], [("zB", h)])
        yield
        sl = use(ti, li, "M")

        def ev_glu(mt, pb, pk):
            sb = sigb[:, mt % 2, t]
            S.op("act", lambda e: e.activation(out=sb, in_=pb[:, 0:HT], func=AF.Sigmoid, bias=bgluc[:, si, mt:mt + 1], scale=1.0),
                 [pk, ("bgluc", si)], [("sigb", h, mt % 2)])
            tt("dve", gB[:, mt, t], gB[:, mt, t], sb, ALU.mult, [("sigb", h, mt % 2), ("gB", h)], [("gB", h)])
            tt("dve", gB[:, mt, t], gB[:, mt, t], zB[:, mt, t], ALU.mult, [("zB", h), ("gB", h)], [("gB", h)])
        mm_fm(h, sl, zB, ("zB", h), range(0, 4), ev_glu)
        yield
        mm_fm(h, sl, zB, ("zB", h), range(4, 8), ev_glu)
        if h == 1 and np_ is not None:
            load_role(np_[0], np_[1], "E2")
        yield
        sl = use(ti, li, "OUT")
        out_proj(h, sl, range(0, 4))
        yield
        out_proj(h, sl, range(4, 8))
        if h == 1 and np_ is not None:
            load_role(np_[0], np_[1], "M")
        yield

    def wait_slot(ti, li, role):
        X, Y, Z = roles(ti, li)
        slot = {"E1": Y, "E2": X, "M": Z, "OUT": X}[role]
        while wtag.get(slot) != (ti, li, role):
            yield "BLOCK"

    def att_gen(ti, h, li, l):
        d = L[l]
        si = l // 2
        t = tk(h)
        t0 = ti * T + h * HT
        me = (ti, li, h)
        np_ = nxt_pos(ti, li)
        X, Y, Z = roles(ti, li)
        S.dma("sp", cosT[:, t], ropeC[:, t0:t0 + HT], R=[("rope", "c")], W=[("cosT", h)], semkey="rope%d" % h)
        S.dma("sp", sinT[:, t], ropeS[:, t0:t0 + HT], R=[("rope", "s")], W=[("sinT", h)], semkey="rope%d" % h)
        rmsnorm(h)
        norm_apply(h, li, hB[:, :, t], ("hB", h))
        if h == 0:
            copy("pool", KT[:, :, 0:128], KTc[:, si, :, :], ["KTc"], [("KT", 0)])
            copy("pool", Vpad[:, 0, :, :, :], Vc[:, si, :, :, :], ["Vc"], [("Vpad", 0)])
        yield

        def rope_pair(dst, dkey, wA, wB, wkeys):
            pa, pka = bankf()
            pbb, pkb = bankf()
            for kc in range(8):
                mm(pa[:, 0:HT], wA(kc), hB[:, kc, t], kc == 0, kc == 7, [("hB", h)] + wkeys, [pka])
            for kc in range(8):
                mm(pbb[:, 0:HT], wB(kc), hB[:, kc, t], kc == 0, kc == 7, [("hB", h)] + wkeys, [pkb])
            tt("dve", tmpf[:, 0, t], pa[:, 0:HT], cosT[:, t], ALU.mult, [pka, ("cosT", h)], [("t1", h)])
            tt("dve", tmpf[:, 1, t], pbb[:, 0:HT], sinT[:, t], ALU.mult, [pkb, ("sinT", h)], [("t2", h)])
            tt("pool", dst, tmpf[:, 0, t], tmpf[:, 1, t], ALU.add, [("t1", h), ("t2", h)], dkey)

        for _ in wait_slot(ti, li, "E2"):
            yield _
        for _ in wait_slot(ti, li, "E1"):
            yield _
        for mt in range(8):
            rope_pair(zB[:, mt, t], [("zB", h)], lambda kc, mt=mt: wsl[X][:, kc, mt * 128:(mt + 1) * 128],
                      lambda kc, mt=mt: wsl[Y][:, kc, mt * 128:(mt + 1) * 128], [("wsl", X), ("wsl", Y)])
            if mt % 2 == 1 and mt < 7:
                yield
        if h == 1:
            load_role(ti, li, "OUT")
            if np_ is not None:
                load_role(np_[0], np_[1], "E1")
        yield
        while wtag.get("s5w") != (ti, li, "KV"):
            yield "BLOCK"
        kslots = [("KT", 1 + 2 * h), ("KT", 2 + 2 * h)]
        for g in range(2):
            rope_pair(KT[:, g, 128 + h * HT:128 + (h + 1) * HT], kslots, lambda kc, g=g: kvw[:, kc, g * 128:(g + 1) * 128],
                      lambda kc, g=g: kvw[:, kc, 256 + g * 128:256 + (g + 1) * 128], ["s5w"])
        for b2 in range(2):
            blk = 2 * h + b2
            pb, pk = bankf()
            for kc in range(8):
                mm(pb[:, 0:128], hB[:, kc, blk * 128:(blk + 1) * 128], kvw[:, kc, 512:640], kc == 0, kc == 7, [("hB", h), "s5w"], [pk])
            pv = pb[:, 0:128].rearrange("p (g d) -> p g d", g=2)
            copy("act", Vpad[:, blk + 1, :, 0, 0:64], pv, [pk], [("Vpad", blk + 1)])
            copy("act", Vpad[:, blk + 1, :, 1, 64:128], pv, [pk], [("Vpad", blk + 1)])
        if h == 1 and np_ is not None:
            s5w_load(np_[0], np_[1], "WS3" if layers[np_[1]] % 2 == 0 else "KV")
        yield
        for _ in wait_slot(ti, li, "M"):
            yield _
        ev_g = lambda mt, pb, pk: actf(gB[:, mt, t], pb[:, 0:HT], AF.Silu, [pk], [("gB", h)])
        mm_fm(h, Z, hB, ("hB", h), range(0, 4), ev_g)
        yield
        mm_fm(h, Z, hB, ("hB", h), range(4, 8), ev_g)
        if h == 1 and np_ is not None:
            load_role(np_[0], np_[1], "E2")
        yield
        for b2 in range(2):
            blk = 2 * h + b2
            first = (ti == 0 and blk == 0)
            for g in range(2):
                combos = [(par, kb) for par in range(2) for kb in ((1,) if first else (0, 1))]
                pts = []
                for ci_, (par, kb) in enumerate(combos):
                    s_ = blk + kb
                    pb, pk = bankf()
                    lo, hi = par * 64, (par + 1) * 64
                    mm(pb[:, :], KT[lo:hi, g, s_ * 128:(s_ + 1) * 128], zB[lo:hi, g * 4:(g + 1) * 4, blk * 128:(blk + 1) * 128],
                       True, False, [("KT", s_), ("zB", h)], [pk])
                    mm(pb[:, :], identb, mask2[:, kb, :].unsqueeze(1).broadcast_to([128, 4, 128]),
                       False, True, ["identb", "mask2"], [pk])
                    slot = h * 4 + ci_
                    actf(PT[:, slot, :], pb[:, :], AF.Exp, [pk], [("PT", slot)], scale=0.125)
                    pts.append((par, s_, slot))
                po, pko = bankf()
                pd, pkd = bankf()
                for i_, (par, s_, slot) in enumerate(pts):
                    mm(po[:, :], Vpad[:, s_, g, par, :], PT[:, slot, :], i_ == 0, i_ == len(pts) - 1, [("Vpad", s_), ("PT", slot)], [pko])
                for i_, (par, s_, slot) in enumerate(pts):
                    mm(pd[:, :], ones2[:, par, :], PT[:, slot, :], i_ == 0, i_ == len(pts) - 1, ["ones2", ("PT", slot)], [pkd])
                tq0 = tmpf[:, 0, t]
                tq1 = tmpf[:, 1, t]
                for c2 in range(2):
                    den = tq0.rearrange("p (k q) -> p k q", k=2)
                    tt("dve", den, pd[:, c2 * 256:(c2 + 1) * 256].rearrange("p (k q) -> p k q", k=2),
                       sinke[:, si, g * 4 + c2 * 2:g * 4 + c2 * 2 + 2].unsqueeze(2).broadcast_to([128, 2, 128]), ALU.add,
                       [pkd, ("sinke", si)], [("t1", h)])
                    actf(tq0, tq0, AF.Ln, [("t1", h)], [("t1", h)])
                    actf(tq0, tq0, AF.Exp, [("t1", h)], [("t1", h)], scale=-1.0)
                    tt("dve", tq1, po[:, c2 * 256:(c2 + 1) * 256], tq0, ALU.mult, [pko, ("t1", h)], [("t2", h)])
                    gv = gB[:, g * 4 + c2 * 2:g * 4 + c2 * 2 + 2, blk * 128:(blk + 1) * 128]
                    tt("dve", gv, gv, tq1.rearrange("p (k q) -> p k q", k=2), ALU.mult, [("t2", h), ("gB", h)], [("gB", h)])
                if not (g == 1 and b2 == 1):
                    yield
            if b2 == 1 and h == 1:
                copy("pool", KTc[:, si, :, :], KT[:, :, 512:640], [("KT", 4)], ["KTc"])
                copy("pool", Vc[:, si, :, :, :], Vpad[:, 4, :, :, :], [("Vpad", 4)], ["Vc"])
        yield
        for _ in wait_slot(ti, li, "OUT"):
            yield _
        out_proj(h, X, range(0, 4))
        yield
        out_proj(h, X, range(4, 8))
        if h == 1 and np_ is not None:
            load_role(np_[0], np_[1], "M")
        yield

    def tile_gen(ti, h):
        S.epoch = ti
        t = tk(h)
        t0 = ti * T + h * HT
        me = (ti, "io", h)
        for _ in acquire("HP", me):
            yield _
        for b2 in range(2):
            blk = 2 * h + b2
            xs = xtok[:, b2, :]
            S.dma("sp", xs, x[t0 + b2 * 128:t0 + (b2 + 1) * 128, :], W=["HP"], semkey="xin")
            for hh in range(2):
                pb, pk = bankf()
                for q in range(4):
                    kc = hh * 4 + q
                    tr(pb[:, q * 128:(q + 1) * 128], xs[:, kc * 128:(kc + 1) * 128], identf, ["HP", "identf"], [pk])
                copy(ev_eng(), xB[:, hh * 4:(hh + 1) * 4, blk * 128:(blk + 1) * 128],
                     pb[:, :].rearrange("p (k t) -> p k t", k=4), [pk], [("xB", h)])
        release("HP", me)
        yield
        for li, l in enumerate(layers):
            g_ = ssm_gen(ti, h, li, l) if l % 2 == 0 else att_gen(ti, h, li, l)
            for v_ in g_:
                yield v_
        for _ in acquire("SS", me):
            yield _
        rmsnorm(h)
        yh = yB[:, :, t]
        norm_apply(h, 4, yh, "SS")
        yield
        for _ in acquire("HP", me):
            yield _
        for b2 in range(2):
            blk = 2 * h + b2
            xs = xtok[:, b2, :]
            for hh in range(2):
                pb, pk = bankf()
                for q in range(4):
                    kc = hh * 4 + q
                    tr(pb[:, q * 128:(q + 1) * 128], yB[:, kc, blk * 128:(blk + 1) * 128], identf, ["SS", "identf"], [pk])
                copy(ev_eng(), xs[:, hh * 512:(hh + 1) * 512], pb[:, :], [pk], ["HP"])
            S.dma("sp", out[t0 + b2 * 128:t0 + (b2 + 1) * 128, :], xs, R=["HP"], W=["out"], semkey="xin")
        release("HP", me)
        release("SS", me)
        yield

    if n_tiles > 0 and NL > 0:
        for r_ in ("E1", "E2", "M"):
            load_role(0, 0, r_)
        s5w_load(0, 0, "WS3" if layers[0] % 2 == 0 else "KV")
    LAG = int(os.environ.get("KLAG", "4"))

    def chain(h):
        for ti in range(n_tiles):
            for v_ in tile_gen(ti, h):
                yield v_

    gens = [chain(0), chain(1)]
    alive = [True, True]
    step = 0
    blocked = 0
    while any(alive):
        progressed = False
        for hh_ in range(2):
            if not alive[hh_]:
                continue
            if hh_ == 1 and step < LAG and alive[0]:
                continue
            try:
                v_ = next(gens[hh_])
                if v_ != "BLOCK":
                    progressed = True
            except StopIteration:
                alive[hh_] = False
                progressed = True
        step += 1
        blocked = 0 if progressed else blocked + 1
        assert blocked < 1000, ("interleave deadlock", wtag, locks)
    S.barrier()

    with nc.Block() as block:
        S.emit(block)
    stack.close()
    return nc


def host_inputs(inputs, layers=(0, 1, 2, 3)):
    f32 = np.float32
    m = {}
    m["c_identf"] = np.eye(128, dtype=f32)
    q = np.arange(128)
    kj = np.arange(128)
    mprev = (kj[:, None] > q[None, :]).astype(f32)
    mcur = (kj[:, None] <= q[None, :]).astype(f32)
    m["c_mask2"] = np.ascontiguousarray(np.stack([mprev, mcur], axis=1))
    o2 = np.zeros((128, 2, 128), f32)
    o2[:, 0, 0:64] = 1.0
    o2[:, 1, 64:128] = 1.0
    m["c_ones2"] = o2
    ii = np.arange(128) // 16
    m["c_tri"] = (ii[None, :] >= ii[:, None]).astype(f32)
    m["c_fidx"] = (np.arange(128) % 32).astype(f32).reshape(128, 1)
    m["c_sgn"] = np.where((np.arange(128) % 64) < 32, -1.0, 1.0).astype(f32).reshape(128, 1)
    m["c_pos"] = np.arange(SEQ, dtype=f32)

    def col(v):
        return np.ascontiguousarray(np.asarray(v, f32).reshape(8, 128).T)

    m["fnorm_col"] = col(inputs["final_norm"])
    for l in layers:
        p = "l%d_" % l
        m[p + "norm_col"] = col(inputs[p + "norm"])
        if l % 2 == 0:
            m[p + "w_in"] = np.ascontiguousarray(inputs[p + "w_in"], f32)
            m[p + "w_glu"] = np.ascontiguousarray(inputs[p + "w_glu"], f32)
            m[p + "w_out"] = np.ascontiguousarray(inputs[p + "w_out"], f32)
            m[p + "bglu_col"] = col(inputs[p + "b_glu"])
            m[p + "aRe"] = np.ascontiguousarray(np.tile(np.asarray(inputs[p + "a_re"], f32).T, (2, 1)))
            m[p + "aIm"] = np.ascontiguousarray(np.tile(np.asarray(inputs[p + "a_im"], f32).T, (2, 1)))
            m[p + "ls"] = np.ascontiguousarray(np.tile(np.asarray(inputs[p + "log_step"], f32)[None, :], (128, 1)))
            m[p + "dcol"] = np.ascontiguousarray(np.tile(np.asarray(inputs[p + "d"], f32).reshape(64, 16).T, (8, 1)))
            m[p + "cRe"] = np.ascontiguousarray(np.tile(np.asarray(inputs[p + "c_re"], f32).transpose(2, 0, 1), (2, 1, 1)))
            m[p + "cIm"] = np.ascontiguousarray(np.tile(np.asarray(inputs[p + "c_im"], f32).transpose(2, 0, 1), (2, 1, 1)))
            m[p + "bRe"] = np.ascontiguousarray(np.tile(np.asarray(inputs[p + "b_re"], f32).transpose(1, 0, 2), (2, 1, 1)))
            m[p + "bIm"] = np.ascontiguousarray(np.tile(np.asarray(inputs[p + "b_im"], f32).transpose(1, 0, 2), (2, 1, 1)))
        else:
            w = np.asarray(inputs[p + "w_in"], f32)
            m[p + "w_in"] = np.ascontiguousarray(w)
            dd = np.arange(64)
            partner = np.where(dd < 32, dd + 32, dd - 32)
            qperm = (np.arange(16)[:, None] * 64 + partner[None, :]).reshape(-1)
            m[p + "w_qp"] = np.ascontiguousarray(w[:, qperm])
            k0 = w[:, 1024:1088]
            k1 = w[:, 1088:1152]
            m[p + "w_kk"] = np.ascontiguousarray(np.concatenate(
                [k0, k0, k1, k1, k0[:, partner], k0[:, partner], k1[:, partner], k1[:, partner]], axis=1))
            m[p + "w_out"] = np.ascontiguousarray(inputs[p + "w_out"], f32)
            s = np.asarray(inputs[p + "sinks"], f32)
            sc = np.zeros((128, 8), f32)
            for kc in range(8):
                sc[0:64, kc] = s[2 * kc]
                sc[64:128, kc] = s[2 * kc + 1]
            m[p + "sink_col"] = sc
    return m


_NC_CACHE = {}


def kernel(**inputs):
    layers = (0, 1, 2, 3)
    if "full" not in _NC_CACHE:
        _NC_CACHE["full"] = build(16, layers)
    nc = _NC_CACHE["full"]
    shared = host_inputs(inputs, layers)
    xs = np.asarray(inputs["x"], np.float32)
    in_maps = []
    for c in range(8):
        mc = dict(shared)
        mc["x"] = np.ascontiguousarray(xs[c])
        in_maps.append(mc)
    res = run_bass_kernel_spmd(nc, in_maps, core_ids=list(range(8)))
    return np.stack([np.asarray(r["out"], np.float32) for r in res.results], axis=0)
```

```python
import math
import os
from contextlib import ExitStack

import numpy as np
import concourse.bass as bass
import concourse.mybir as mybir
from concourse.bass_utils import run_bass_kernel_spmd

F32 = mybir.dt.float32
BF16 = mybir.dt.bfloat16
I32 = mybir.dt.int32
AF = mybir.ActivationFunctionType
ALU = mybir.AluOpType

T = 512
NCH = 64
NH = 32
SEQ = 8192
DM = 1024
EPS = 1e-5
TWO_PI = 2.0 * math.pi
SIN_SCALE = 6.28318


class Sched:
    CE = ("pe", "act", "dve", "pool")

    def __init__(self, nc, stack):
        self.nc = nc
        self.stack = stack
        self.q = {e: [] for e in ("pe", "act", "dve", "pool", "sp")}
        self.esem = {e: [stack.enter_context(nc.semaphore("c_%s_%d" % (e, i))) for i in range(4)] for e in self.CE}
        self.ecnt = {(e, i): 0 for e in self.CE for i in range(4)}
        self.epoch = 0
        self.last_w = {}
        self.readers = {}
        self.known = {e: {} for e in self.q}
        self.dsem = {}
        self.dcnt = {}
        self.semkey_of = {}
        self.all_tokens = {}

    def _deps(self, eng, R, W):
        deps = []
        for k in R:
            t = self.last_w.get(k)
            if t is not None:
                deps.append(t)
        for k in W:
            t = self.last_w.get(k)
            if t is not None:
                deps.append(t)
            deps.extend(self.readers.get(k, ()))
        waits = {}
        for (sem, val, e2) in deps:
            if e2 == eng and eng == "pe":
                continue
            if e2 == "dma":
                val = self.dcnt[self.semkey_of[sem]]
            if self.known[eng].get(sem, 0) >= val:
                continue
            if waits.get(sem, 0) < val:
                waits[sem] = val
        for sem, val in waits.items():
            self.q[eng].append(lambda e, s=sem, v=val: e.wait_ge(s, v))
            self.known[eng][sem] = val

    def _record(self, tok, R, W):
        for k in R:
            self.readers.setdefault(k, []).append(tok)
        for k in W:
            self.last_w[k] = tok
            self.readers[k] = []
        self.all_tokens[tok[0]] = max(self.all_tokens.get(tok[0], 0), tok[1])

    def op(self, eng, fn, R=(), W=()):
        self._deps(eng, R, W)
        i = self.epoch % 4
        sem = self.esem[eng][i]
        self.ecnt[(eng, i)] += 1
        val = self.ecnt[(eng, i)]
        self.q[eng].append(lambda e, f=fn, s=sem: f(e).then_inc(s, 1))
        self._record((sem, val, eng), R, W)

    def dma(self, queue, out, in_, R=(), W=(), semkey=None):
        self._deps(queue, R, W)
        if semkey not in self.dsem:
            self.dsem[semkey] = self.stack.enter_context(self.nc.semaphore("d_%s" % semkey))
            self.dcnt[semkey] = 0
            self.semkey_of[self.dsem[semkey]] = semkey
        sem = self.dsem[semkey]
        self.dcnt[semkey] += 16
        val = self.dcnt[semkey]
        self.q[queue].append(lambda e, o=out, i=in_, s=sem: e.dma_start(out=o, in_=i).then_inc(s, 16))
        self._record((sem, val, "dma"), R, W)

    def barrier(self, skip=()):
        for eng in self.q:
            for sem, val in self.all_tokens.items():
                if self.semkey_of.get(sem) in skip:
                    continue
                if self.known[eng].get(sem, 0) < val:
                    self.q[eng].append(lambda e, s=sem, v=val: e.wait_ge(s, v))
                    self.known[eng][sem] = val

    def emit(self, block):
        q = self.q

        @block.tensor
        def _(e):
            for f in q["pe"]:
                f(e)

        @block.scalar
        def _(e):
            for f in q["act"]:
                f(e)

        @block.vector
        def _(e):
            for f in q["dve"]:
                f(e)

        @block.gpsimd
        def _(e):
            for f in q["pool"]:
                f(e)

        @block.sync
        def _(e):
            for f in q["sp"]:
                f(e)


SSM_HOST = ("norm", "bglu", "aRe", "aIm", "ls", "cRe", "cIm", "bRe", "bIm", "dcol")


def build(n_tiles=16, layers=(0, 1, 2, 3)):
    nc = bass.Bass("TRN2", target_bir_lowering=False)
    stack = ExitStack()
    S = Sched(nc, stack)

    def din(name, shape, dt=F32):
        return nc.dram_tensor(name, list(shape), dt, kind="ExternalInput").ap()

    def dscr(name, shape, dt):
        return nc.dram_tensor(name, list(shape), dt, kind="Internal").ap()

    x = din("x", [SEQ, DM])
    out = nc.dram_tensor("out", [SEQ, DM], F32, kind="ExternalOutput").ap()
    c_identf = din("c_identf", [128, 128])
    c_mask2 = din("c_mask2", [128, 2, 128])
    c_ones2 = din("c_ones2", [128, 2, 128])
    c_tri = din("c_tri", [128, 128])
    c_fidx = din("c_fidx", [128, 1])
    c_sgn = din("c_sgn", [128, 1])
    c_pos = din("c_pos", [SEQ])
    fnorm = din("fnorm_col", [128, 8])
    L = {}
    for l in layers:
        p = "l%d_" % l
        d = {}
        d["norm"] = din(p + "norm_col", [128, 8])
        if l % 2 == 0:
            d["w_in"] = din(p + "w_in", [DM, 2048])
            d["w_glu"] = din(p + "w_glu", [DM, DM])
            d["w_out"] = din(p + "w_out", [DM, DM])
            d["bglu"] = din(p + "bglu_col", [128, 8])
            for nm in ("aRe", "aIm", "ls", "dcol"):
                d[nm] = din(p + nm, [128, 64])
            for nm in ("cRe", "cIm", "bRe", "bIm"):
                d[nm] = din(p + nm, [128, 64, 16])
            d["w_in_b"] = dscr(p + "w_in_b", [DM, 2048], BF16)
            d["w_glu_b"] = dscr(p + "w_glu_b", [DM, DM], BF16)
            d["w_out_b"] = dscr(p + "w_out_b", [DM, DM], BF16)
            d["WS3"] = dscr(p + "WS3", [128, 64, 192], BF16)
            d["WHK"] = dscr(p + "WHK", [128, 2, 64, 128], BF16)
        else:
            d["w_in"] = din(p + "w_in", [DM, 2304])
            d["w_qp"] = din(p + "w_qp", [DM, DM])
            d["w_kk"] = din(p + "w_kk", [DM, 512])
            d["w_out"] = din(p + "w_out", [DM, DM])
            d["sink"] = din(p + "sink_col", [128, 8])
            d["w_in_b"] = dscr(p + "w_in_b", [DM, 2304], BF16)
            d["w_qp_b"] = dscr(p + "w_qp_b", [DM, DM], BF16)
            d["w_kk_b"] = dscr(p + "w_kk_b", [DM, 512], BF16)
            d["w_out_b"] = dscr(p + "w_out_b", [DM, DM], BF16)
        L[l] = d
    has_att = any(l % 2 == 1 for l in layers)
    ropeC = dscr("ropeC", [128, SEQ], F32)
    ropeS = dscr("ropeS", [128, SEQ], F32)

    ARENA_W = 53000
    arena = stack.enter_context(nc.sbuf_tensor("arena", [128, ARENA_W], F32))
    cur = [0]

    def carve(shape, dt, at=None):
        n = int(np.prod(shape[1:]))
        words = n if dt in (F32, I32) else (n + 1) // 2
        if at is None:
            off = cur[0]
            cur[0] += words
            assert cur[0] <= ARENA_W, ("sbuf overflow", cur[0])
        else:
            off = at
            assert off + words <= ARENA_W
        v = arena[:, off:off + words]
        if dt != F32:
            v = v.bitcast(dt)
        if len(shape) > 2:
            names = " ".join("d%d" % i for i in range(1, len(shape)))
            kw = {"d%d" % i: shape[i] for i in range(1, len(shape))}
            v = v.rearrange("p (%s) -> p %s" % (names, names), **kw)
        return v

    identf = carve([128, 128], F32)
    identb = carve([128, 128], BF16)
    onesb = carve([128, 128], BF16)
    mask2 = carve([128, 2, 128], BF16)
    ones2 = carve([128, 2, 128], BF16)
    stage_c = carve([128, 2, 128], F32)
    normc = carve([128, 5, 8], F32)
    bgluc = carve([128, 2, 8], F32)
    sinke = carve([128, 2, 8], F32)
    CAB = carve([128, 2, 2, 2, 64], F32)
    HHc = carve([128, 2, 2, 64], F32)
    pro0 = cur[0]
    scr = carve([128, 2, 3, T // 2], F32)

    def RSTD(h):
        return scr[:, h, 0, :]

    def T1(h):
        return scr[:, h, 1, :]

    def T2(h):
        return scr[:, h, 2, :]

    def T12(h):
        return scr[:, h, 1:3, :].rearrange("p a b -> p (a b)")
    sigb = carve([128, 2, T], BF16)
    cosT = carve([128, T], F32)
    sinT = carve([128, T], F32)
    KT = carve([128, 2, 640], BF16)
    Vpad = carve([128, 5, 2, 2, 128], BF16)
    KTc = carve([128, 2, 2, 128], BF16)
    Vc = carve([128, 2, 2, 2, 128], BF16)
    xB = carve([128, 8, T], F32)
    hB = carve([128, 8, T], BF16)
    gB = carve([128, 8, T], BF16)
    zB = carve([128, 8, T], BF16)
    main0 = cur[0]
    tokA_off = cur[0]
    tokA = carve([128, 8, DM], BF16)
    XC = carve([128, 64, 64], BF16)
    HP_off = cur[0]
    HP = carve([128, 64, 64], BF16)
    xtok = carve([128, 2, DM], F32, at=HP_off)
    SS = carve([128, NH, 2, 64], F32)
    s5w = carve([128, 64 * 256], BF16)
    wsl = [carve([128, 8, DM], BF16) for _ in range(3)]
    PT = tokA.rearrange("p a b -> p (a b)")[:, 0:8 * T].rearrange("p (s t) -> p s t", s=8)
    yB = SS.rearrange("p a b c -> p (a b c)").rearrange("p (k t) -> p k t", k=8)
    WS3 = s5w[:, 0:64 * 192].rearrange("p (g r) -> p g r", g=64)
    WHK = s5w.rearrange("p (s g r) -> p s g r", s=2, g=64)
    kvw = s5w[:, 0:8 * 640].rearrange("p (k n) -> p k n", k=8)

    psf = [stack.enter_context(nc.psum_tensor("psf%d" % i, [128, 512], F32)) for i in range(6)]
    psb = [stack.enter_context(nc.psum_tensor("psb%d" % i, [128, 1024], BF16)) for i in range(2)]
    rr = {"f": 0, "b": 0, "e": 0}

    def bankf():
        i = rr["f"] % 6
        rr["f"] += 1
        return psf[i], ("psf", i)

    def bankb():
        i = rr["b"] % 2
        rr["b"] += 1
        return psb[i], ("psb", i)

    def ev_eng():
        rr["e"] += 1
        return "act" if rr["e"] % 2 else "dve"

    def copy(eng, o, i, R, W):
        if eng != "act" and o.dtype == i.dtype:
            S.op(eng, lambda e: e.tensor_single_scalar(out=o, in_=i, scalar=1.0, op=ALU.mult), R, W)
            return
        if eng == "act":
            S.op("act", lambda e: e.activation(out=o, in_=i, func=AF.Copy), R, W)
        else:
            S.op(eng, lambda e: e.tensor_copy(out=o, in_=i), R, W)

    def tt(eng, o, a, b, op, R, W):
        S.op(eng, lambda e: e.tensor_tensor(out=o, in0=a, in1=b, op=op), R, W)

    def ts(eng, o, a, s1, op, R, W):
        S.op(eng, lambda e: e.tensor_single_scalar(out=o, in_=a, scalar=s1, op=op), R, W)

    def actf(o, i, func, R, W, scale=1.0, bias=0.0):
        S.op("act", lambda e: e.activation(out=o, in_=i, func=func, scale=scale, bias=bias), R, W)

    def mm(o, lhsT, rhs, start, stop, R, W):
        S.op("pe", lambda e: e.matmul(o, lhsT=lhsT, rhs=rhs, start=start, stop=stop), R, W)

    def tr(o, i, ident, R, W):
        S.op("pe", lambda e: e.transpose(o, i, ident), R, W)

    S.dma("sp", identf, c_identf, W=["identf"], semkey="c0")
    copy("dve", identb, identf, ["identf"], ["identb"])
    S.op("dve", lambda e: e.memset(onesb, 1.0), (), ["onesb"])
    S.dma("sp", stage_c, c_mask2, W=["stage_c"], semkey="c1")
    S.op("dve", lambda e: e.tensor_scalar(out=mask2, in0=stage_c, scalar1=-1.0, scalar2=30000.0, op0=ALU.add, op1=ALU.mult),
         ["stage_c"], ["mask2"])
    S.dma("sp", stage_c, c_ones2, R=(), W=["stage_c"], semkey="c1")
    copy("dve", ones2, stage_c, ["stage_c"], ["ones2"])
    S.op("dve", lambda e: e.memset(HHc, 0.0), (), ["HHc"])
    for li, l in enumerate(layers):
        S.dma("sp", normc[:, li, :], L[l]["norm"], W=[("normc", li)], semkey="c2")
    S.dma("sp", normc[:, 4, :], fnorm, W=[("normc", 4)], semkey="c2")

    for l in layers:
        d = L[l]
        names = ("w_in", "w_glu", "w_out") if l % 2 == 0 else ("w_in", "w_qp", "w_kk", "w_out")
        for nm in names:
            for rb in range(8):
                S.dma("pool", d[nm + "_b"][rb * 128:(rb + 1) * 128, :], d[nm][rb * 128:(rb + 1) * 128, :], W=[("wscr", l, nm, rb)], semkey="wc_%d_%s" % (l, nm))

    if has_att:
        pw = 2048
        base = pro0
        posb = carve([128, pw], F32, at=base)
        tq = carve([128, pw], F32, at=base + pw)
        tqi = carve([128, pw], I32, at=base + 2 * pw)
        tqf = carve([128, pw], F32, at=base + 3 * pw)
        res = carve([128, pw], F32, at=base + 4 * pw)
        fidx = carve([128, 1], F32, at=base + 5 * pw)
        invf = carve([128, 1], F32, at=base + 5 * pw + 1)
        sgn = carve([128, 1], F32, at=base + 5 * pw + 2)
        S.dma("sp", fidx, c_fidx, W=["pp"], semkey="c3")
        S.dma("sp", sgn, c_sgn, W=["pp"], semkey="c3")
        actf(invf, fidx, AF.Exp, ["pp"], ["pp"], scale=-math.log(10000.0) / 32.0)
        for ch in range(SEQ // pw):
            S.dma("sp", posb, c_pos[ch * pw:(ch + 1) * pw].partition_broadcast(128), W=["pp"], semkey="c4")
            S.op("dve", lambda e: e.tensor_scalar_mul(out=posb, in0=posb, scalar1=invf[:, 0:1]), ["pp", "pp"], ["pp"])
            ts("dve", posb, posb, 1.0 / TWO_PI, ALU.mult, ["pp"], ["pp"])
            for which, shift, dst in (("s", 0.0, ropeS), ("c", 0.25, ropeC)):
                ts("dve", tq, posb, shift, ALU.add, ["pp"], ["pp"])
                copy("dve", tqi, tq, ["pp"], ["pp"])
                copy("dve", tqf, tqi, ["pp"], ["pp"])
                tt("dve", tq, tq, tqf, ALU.subtract, ["pp", "pp"], ["pp"])
                actf(res, tq, AF.Sin, ["pp"], ["pp"], scale=SIN_SCALE)
                if which == "s":
                    S.op("dve", lambda e: e.tensor_scalar_mul(out=res, in0=res, scalar1=sgn[:, 0:1]), ["pp", "pp"], ["pp"])
                S.dma("sp", dst[:, ch * pw:(ch + 1) * pw], res, R=["pp"], W=[("rope", which)], semkey="c5")
        for li, l in enumerate(layers):
            if l % 2 == 1:
                si = (l // 2)
                S.dma("sp", sinke[:, si, :], L[l]["sink"], W=[("sinke", si)], semkey="c6")
                actf(sinke[:, si, :], sinke[:, si, :], AF.Exp, [("sinke", si)], [("sinke", si)])

    for l in layers:
        if l % 2:
            continue
        d = L[l]
        si = l // 2
        o = [pro0]

        def pc(shape, dt=F32):
            n = int(np.prod(shape[1:]))
            w = n if dt in (F32, I32) else (n + 1) // 2
            v = carve(shape, dt, at=o[0])
            o[0] += w
            return v
        NK = 17
        aRe = pc([128, 64]); aIm = pc([128, 64]); ls = pc([128, 64]); dcol = pc([128, 64])
        cRe = pc([128, 64, 16]); cIm = pc([128, 64, 16]); bRe = pc([128, 64, 16]); bIm = pc([128, 64, 16])
        lr = pc([128, 64]); lim = pc([128, 64])
        Er = pc([128, NK, 64]); Ei = pc([128, NK, 64]); Fr = pc([128, NK, 64]); Fi = pc([128, NK, 64])
        cr = pc([128, 64]); ci = pc([128, 64]); w1 = pc([128, 64]); w2 = pc([128, 64])
        WST = pc([128, 64, 8, 16]); WS2T = pc([128, 64, 8, 16], BF16); WHf = pc([128, 64, 8, 16])
        tri = pc([128, 128]); k0t = pc([128, 128])
        U1 = o[0]
        u1 = pc([128, 64, 8, 16])
        U2 = o[0]
        u2 = pc([128, 64, 8, 16])
        o2 = [U1]
        def pc2(shape, dt=F32):
            n = int(np.prod(shape[1:]))
            v = carve(shape, dt, at=o2[0])
            o2[0] += n
            return v
        ex = pc2([128, NK, 64]); tq_ = pc2([128, NK, 64]); t2_ = pc2([128, NK, 64]); ti_ = pc2([128, NK, 64], I32)
        tf_ = pc2([128, NK, 64]); sn = pc2([128, NK, 64]); cs = pc2([128, NK, 64])
        assert o2[0] <= U2
        WS3s = carve([128, 64, 192], BF16, at=U1)
        WHKs = carve([128, 2, 64, 128], BF16, at=U2)
        assert o[0] <= ARENA_W, o[0]
        k = "pp"
        for nm, tl in (("aRe", aRe), ("aIm", aIm), ("ls", ls), ("dcol", dcol), ("cRe", cRe), ("cIm", cIm), ("bRe", bRe), ("bIm", bIm)):
            S.dma("sp", tl, d[nm], W=[k + nm], semkey="pp")
        S.dma("sp", tri, c_tri, W=[k + "tri"], semkey="pp")
        S.dma("sp", bgluc[:, si, :], d["bglu"], W=[("bgluc", si)], semkey="pp")
        lk = [k] + [k + nm for nm in ("aRe", "aIm", "ls", "dcol", "cRe", "cIm", "bRe", "bIm", "tri")]
        actf(ls, ls, AF.Exp, lk, [k, "WS3s", "WHKs"])
        tt("dve", lr, aRe, ls, ALU.mult, [k], [k])
        tt("dve", lim, aIm, ls, ALU.mult, [k], [k])
        for kk in range(-8, 9):
            idx = kk + 8
            actf(ex[:, idx, :], lr, AF.Exp, [k], [k], scale=float(kk))
            ts("dve", tq_[:, idx, :], lim, kk / TWO_PI, ALU.mult, [k], [k])
        for dst, shift in ((sn, 0.0), (cs, 0.25)):
            ts("dve", t2_, tq_, shift, ALU.add, [k], [k])
            copy("dve", ti_, t2_, [k], [k])
            copy("dve", tf_, ti_, [k], [k])
            tt("dve", t2_, t2_, tf_, ALU.subtract, [k], [k])
            actf(dst, t2_, AF.Sin, [k], [k], scale=SIN_SCALE)
        tt("dve", Er, ex, cs, ALU.mult, [k], [k])
        tt("dve", Ei, ex, sn, ALU.mult, [k], [k])
        E1r = Er[:, 9, :]; E1i = Ei[:, 9, :]
        ts("dve", w1, E1r, -1.0, ALU.add, [k], [k])
        tt("dve", cr, w1, aRe, ALU.mult, [k], [k])
        tt("dve", w2, E1i, aIm, ALU.mult, [k], [k])
        tt("dve", cr, cr, w2, ALU.add, [k], [k])
        tt("dve", ci, E1i, aRe, ALU.mult, [k], [k])
        tt("dve", w2, w1, aIm, ALU.mult, [k], [k])
        tt("dve", ci, ci, w2, ALU.subtract, [k], [k])
        tt("dve", w1, aRe, aRe, ALU.mult, [k], [k])
        tt("dve", w2, aIm, aIm, ALU.mult, [k], [k])
        tt("dve", w1, w1, w2, ALU.add, [k], [k])
        S.op("dve", lambda e: e.reciprocal(out=w1, in_=w1), [k], [k])
        tt("dve", cr, cr, w1, ALU.mult, [k], [k])
        tt("dve", ci, ci, w1, ALU.mult, [k], [k])
        crb = cr.unsqueeze(1).broadcast_to([128, NK, 64])
        cib = ci.unsqueeze(1).broadcast_to([128, NK, 64])
        tt("dve", Fr, Er, crb, ALU.mult, [k], [k])
        tt("dve", t2_, Ei, cib, ALU.mult, [k], [k])
        tt("dve", Fr, Fr, t2_, ALU.subtract, [k], [k])
        tt("dve", Fi, Er, cib, ALU.mult, [k], [k])
        tt("dve", t2_, Ei, crb, ALU.mult, [k], [k])
        tt("dve", Fi, Fi, t2_, ALU.add, [k], [k])

        def kview(v, lo, hi, k0, rev):
            vv = v[lo:hi, :, :].rearrange("p k g -> p g k")
            vv = vv[:, :, k0 - 7:k0 + 1][:, :, ::-1] if rev else vv[:, :, k0:k0 + 8]
            return vv.unsqueeze(3).broadcast_to([hi - lo, 64, 8, 16])

        def bview(v, lo, hi):
            return v[lo:hi].unsqueeze(2).broadcast_to([hi - lo, 64, 8, 16])

        for dstT, k0 in ((WST, 15), (WS2T, 7)):
            tt("dve", u1[0:64], bview(bRe, 0, 64), kview(Fr, 0, 64, k0, True), ALU.mult, [k], [k])
            tt("dve", u2[0:64], bview(bIm, 0, 64), kview(Fi, 0, 64, k0, True), ALU.mult, [k], [k])
            tt("dve", dstT[0:64], u1[0:64], u2[0:64], ALU.subtract, [k], [k])
            tt("dve", u1[64:128], bview(bIm, 64, 128), kview(Fr, 64, 128, k0, True), ALU.mult, [k], [k])
            tt("dve", u2[64:128], bview(bRe, 64, 128), kview(Fi, 64, 128, k0, True), ALU.mult, [k], [k])
            tt("dve", dstT[64:128], u1[64:128], u2[64:128], ALU.add, [k], [k])
        tt("dve", u1[0:64], bview(cRe, 0, 64), kview(Er, 0, 64, 9, False), ALU.mult, [k], [k])
        tt("dve", u2[0:64], bview(cIm, 0, 64), kview(Ei, 0, 64, 9, False), ALU.mult, [k], [k])
        tt("dve", WHf[0:64], u1[0:64], u2[0:64], ALU.subtract, [k], [k])
        tt("dve", u1[64:128], bview(cRe, 64, 128), kview(Ei, 64, 128, 9, False), ALU.mult, [k], [k])
        tt("dve", u2[64:128], bview(cIm, 64, 128), kview(Er, 64, 128, 9, False), ALU.mult, [k], [k])
        tt("dve", u1[64:128], u1[64:128], u2[64:128], ALU.add, [k], [k])
        ts("dve", WHf[64:128], u1[64:128], -1.0, ALU.mult, [k], [k])
        for s2 in range(2):
            copy("dve", CAB[:, si, 0, s2, :], Er[:, 16, :], [k], [("CAB", si)])
        ts("dve", CAB[0:64, si, 1, 0, :], Ei[0:64, 16, :], -1.0, ALU.mult, [k], [("CAB", si)])
        copy("dve", CAB[64:128, si, 1, 0, :], Ei[64:128, 16, :], [k], [("CAB", si)])
        ts("dve", CAB[:, si, 1, 1, :], CAB[:, si, 1, 0, :], -1.0, ALU.mult, [("CAB", si)], [("CAB", si)])
        copy("act", WHKs[:, 0, :, :], WHf.rearrange("p g j c -> p g (j c)"), [k], ["WHKs"])
        for g in range(64):
            pb, pk = bankf()
            pb2, pk2 = bankf()
            tr(pb[:, 0:128], WST[:, g, :, :].rearrange("p j c -> p (j c)"), identf, [k, "identf"], [pk])
            mm(pb2[:, 0:128], WS2T[:, g, :, :].rearrange("p j c -> p (j c)"),
               WHKs[:, 0, g, :], True, True, [k, "WHKs"], [pk2])
            copy("act", WS3s[:, g, 0:128], pb[:, 0:128], [pk], ["WS3s"])
            copy("act", WS3s[:, g, 128:192], pb[:, 0:64], [pk], ["WS3s"])
            tt("dve", k0t, pb2[:, 0:128], tri, ALU.mult, [pk2, k], ["k0t"])
            S.op("dve", lambda e, g=g: e.scalar_tensor_tensor(out=WHKs[:, 1, g, :], in0=identf, scalar=dcol[:, g:g + 1],
                                                               in1=k0t, op0=ALU.mult, op1=ALU.add),
                 ["k0t", k, "identf"], ["WHKs"])
        S.dma("sp", d["WS3"], WS3s, R=["WS3s"], W=[("s5scr", l)], semkey="pst")
        S.dma("sp", d["WHK"], WHKs, R=["WHKs"], W=[("s5scr", l)], semkey="pst")
    S.barrier(skip=[kk_ for kk_ in S.dsem if kk_.startswith("wc_")])
    S.op("pool", lambda e: e.memset(Vpad, 0.0), (), ["Vpad"])
    S.op("pool", lambda e: e.memset(KT, 0.0), (), ["KT"])
    S.op("pool", lambda e: e.memset(KTc, 0.0), (), ["KTc"])
    S.op("pool", lambda e: e.memset(Vc, 0.0), (), ["Vc"])

    HT = T // 2
    NHC = NH

    def tk(h):
        return slice(h * HT, (h + 1) * HT)

    def load_w(slot, src, cols, l, nm):
        ncol = cols[1] - cols[0]
        S.dma("sp", wsl[slot][:, :, 0:ncol], src[:, cols[0]:cols[1]].rearrange("(k p) n -> p k n", p=128),
              R=[("wscr", l, nm, rb) for rb in range(8)], W=[("wsl", slot)], semkey="wsl%d" % slot)

    def load_s5w(l, which):
        d = L[l]
        if which == "WS3":
            S.dma("sp", WS3, d["WS3"], R=[("s5scr", l)], W=["s5w"], semkey="s5w")
        elif which == "WHK":
            S.dma("sp", WHK, d["WHK"], R=[("s5scr", l)], W=["s5w"], semkey="s5w")
        else:
            S.dma("sp", kvw[:, :, 0:512], d["w_kk_b"].rearrange("(k p) n -> p k n", p=128),
                  R=[("wscr", l, "w_kk", rb) for rb in range(8)], W=["s5w"], semkey="s5w")
            S.dma("sp", kvw[:, :, 512:640], d["w_in_b"][:, 1152:1280].rearrange("(k p) n -> p k n", p=128),
                  R=[("wscr", l, "w_in", rb) for rb in range(8)], W=["s5w"], semkey="s5w")

    def layer_loads(l, which):
        d = L[l]
        if l % 2 == 0:
            if which == "slot0":
                load_w(0, d["w_in_b"], (0, 1024), l, "w_in")
            elif which == "slot1":
                load_w(1, d["w_in_b"], (1024, 2048), l, "w_in")
            elif which == "slot2":
                load_w(2, d["w_glu_b"], (0, 1024), l, "w_glu")
            else:
                load_s5w(l, "WS3")
        else:
            if which == "slot0":
                load_w(0, d["w_in_b"], (0, 1024), l, "w_in")
            elif which == "slot1":
                load_w(1, d["w_qp_b"], (0, 1024), l, "w_qp")
            elif which == "slot2":
                load_w(2, d["w_in_b"], (1280, 2304), l, "w_in")
            else:
                load_s5w(l, "KV")

    def rmsnorm(h):
        t = tk(h)
        actf(zB[:, :, t], xB[:, :, t], AF.Square, [("xB", h)], [("zB", h)])
        pb, pk = bankf()
        for kc in range(8):
            mm(pb[:, 0:HT], onesb, zB[:, kc, t], kc == 0, kc == 7, ["onesb", ("zB", h)], [pk])
        actf(T2(h), pb[:, 0:HT], AF.Ln, [pk], [("t2", h)], scale=1.0 / DM, bias=EPS)
        actf(RSTD(h), T2(h), AF.Exp, [("t2", h)], [("rstd", h)], scale=-0.5)

    def norm_apply(h, li, dst, dkey):
        t = tk(h)
        for kc in range(8):
            S.op("dve", lambda e, kc=kc: e.scalar_tensor_tensor(out=dst[:, kc, :], in0=xB[:, kc, t],
                                                                 scalar=normc[:, li, kc:kc + 1], in1=RSTD(h),
                                                                 op0=ALU.mult, op1=ALU.mult),
                 [("xB", h), ("rstd", h), ("normc", li)], [dkey])

    def mm_fm(h, slot, act, akey, mts, evac):
        t = tk(h)
        for mt in mts:
            if mt % 2:
                continue
            pb, pk = bankf()
            for i in range(2):
                for kc in range(8):
                    mm(pb[:, i * HT:(i + 1) * HT], wsl[slot][:, kc, (mt + i) * 128:(mt + i + 1) * 128], act[:, kc, t], kc == 0, kc == 7,
                       [("wsl", slot), akey], [pk])
            evac(mt, pb, pk)

    def out_proj(h, slot, mts=8):
        t = tk(h)

        def ev(mt, pb, pk):
            tt("dve", xB[:, mt:mt + 2, t], xB[:, mt:mt + 2, t], pb[:, :].rearrange("p (a b) -> p a b", a=2), ALU.add,
               [pk, ("xB", h)], [("xB", h)])
        mm_fm(h, slot, gB, ("gB", h), mts, ev)

    NL = len(layers)
    wtag = {}
    locks = {}

    def roles(ti, li):
        gl = ti * NL + li
        X, Z = (0, 2) if gl % 2 == 0 else (2, 0)
        return X, 1, Z

    def load_role(ti, li, role):
        l = layers[li]
        d = L[l]
        X, Y, Z = roles(ti, li)
        if l % 2 == 0:
            spec = {"E1": (Y, d["w_in_b"], (1024, 2048), "w_in"), "E2": (X, d["w_in_b"], (0, 1024), "w_in"),
                    "M": (Z, d["w_glu_b"], (0, 1024), "w_glu"), "OUT": (X, d["w_out_b"], (0, 1024), "w_out")}[role]
        else:
            spec = {"E1": (Y, d["w_qp_b"], (0, 1024), "w_qp"), "E2": (X, d["w_in_b"], (0, 1024), "w_in"),
                    "M": (Z, d["w_in_b"], (1280, 2304), "w_in"), "OUT": (X, d["w_out_b"], (0, 1024), "w_out")}[role]
        slot, src, cols, nm = spec
        load_w(slot, src, cols, l, nm)
        wtag[slot] = (ti, li, role)

    def use(ti, li, role):
        X, Y, Z = roles(ti, li)
        slot = {"E1": Y, "E2": X, "M": Z, "OUT": X}[role]
        assert wtag.get(slot) == (ti, li, role), ("weight slot content mismatch", slot, wtag.get(slot), (ti, li, role))
        return slot

    def nxt_pos(ti, li):
        if li + 1 < NL:
            return (ti, li + 1)
        if ti + 1 < n_tiles:
            return (ti + 1, 0)
        return None

    def s5w_load(ti, li, which):
        load_s5w(layers[li], which)
        wtag["s5w"] = (ti, li, which)

    def s5w_use(ti, li, which):
        assert wtag.get("s5w") == (ti, li, which), ("s5w content mismatch", wtag.get("s5w"), (ti, li, which))

    def acquire(name, me):
        while locks.get(name) not in (None, me):
            yield "BLOCK"
        locks[name] = me

    def release(name, me):
        assert locks.get(name) == me
        locks[name] = None

    def nxt(ti, li):
        if li + 1 < len(layers):
            return layers[li + 1]
        if ti + 1 < n_tiles:
            return layers[0]
        return None

    def ssm_gen(ti, h, li, l):
        d = L[l]
        si = l // 2
        t = tk(h)
        n0 = h * NHC
        r0, r1 = h * 32, h * 32 + 32
        me = (ti, li, h)
        np_ = nxt_pos(ti, li)
        rmsnorm(h)
        norm_apply(h, li, hB[:, :, t], ("hB", h))
        yield
        sl = use(ti, li, "E1")
        ev_g = lambda mt, pb, pk: actf(gB[:, mt:mt + 2, t], pb[:, :].rearrange("p (a b) -> p a b", a=2), AF.Silu, [pk], [("gB", h)])
        mm_fm(h, sl, hB, ("hB", h), range(0, 4), ev_g)
        yield
        mm_fm(h, sl, hB, ("hB", h), range(4, 8), ev_g)
        if h == 1 and np_ is not None:
            load_role(np_[0], np_[1], "E1")
        yield
        sl = use(ti, li, "E2")
        uA = tokA.rearrange("p a b -> p (a b)").rearrange("p (g j c) -> p g j c", g=64, j=8, c=16)
        for j in range(8):
            for nt in range(2):
                pb, pk = bankf()
                for kc in range(8):
                    mm(pb[0:32, :], hB[:, kc, h * HT + j:(h + 1) * HT:8], wsl[sl][:, kc, nt * 512:(nt + 1) * 512], kc == 0, kc == 7,
                       [("hB", h), ("wsl", sl)], [pk])
                copy(ev_eng(), uA[r0:r1, nt * 32:(nt + 1) * 32, j, :],
                     pb[0:32, :].rearrange("p (g c) -> p g c", g=32), [pk], [("tokA", h)])
            if j % 2 == 1 and j < 7:
                yield
        if h == 1:
            load_role(ti, li, "OUT")
        yield
        for gq in range(2):
            pb, pk = bankb()
            for g32 in range(32):
                g = gq * 32 + g32
                tr(pb[:, g32 * 32:(g32 + 1) * 32], uA[r0:r1, g, :, :].rearrange("p j c -> p (j c)"), identb[r0:r1, r0:r1],
                   [("tokA", h), "identb"], [pk])
            copy(ev_eng(), XC[:, gq * 32:(gq + 1) * 32, n0:n0 + NHC], pb[:, :].rearrange("p (g n) -> p g n", g=32), [pk], [("XC", h)])
        yield
        for _ in acquire("SS", me):
            yield _
        s5w_use(ti, li, "WS3")
        for gq in range(8):
            pb, pk = bankf()
            for g8 in range(8):
                g = gq * 8 + g8
                for s_ in range(2):
                    c0 = (g8 * 2 + s_) * NHC
                    mm(pb[:, c0:c0 + NHC], WS3[:, g, s_ * 64:s_ * 64 + 128], XC[:, g, n0:n0 + NHC], True, True,
                       ["s5w", ("XC", h)], [pk])
            copy(ev_eng(), SS[:, :, :, gq * 8:(gq + 1) * 8],
                 pb[:, :].rearrange("p (g s n) -> p n s g", g=8, s=2), [pk], ["SS"])
        if h == 1:
            s5w_load(ti, li, "WHK")
        yield
        for _ in acquire("HP", me):
            yield _
        HPv = HP.rearrange("p g n -> p n g")
        t1 = T1(h)[:, 0:128].rearrange("p (s g) -> p s g", s=2)
        t2 = T2(h)[:, 0:128].rearrange("p (s g) -> p s g", s=2)
        CA = CAB[:, si, 0, :, :]
        CB = CAB[:, si, 1, :, :]
        kq = [("CAB", si)]
        copy("act", HPv[:, n0, :], HHc[:, si, 0, :], ["HHc"], ["HP"])
        for n in range(NHC):
            prev = HHc[:, si, :, :] if n == 0 else SS[:, n - 1, :, :]
            pk_ = "HHc" if n == 0 else "SS"
            tt("dve", t1, prev, CA, ALU.mult, [pk_] + kq, [("t1", h)])
            tt("dve", t2, prev[:, ::-1, :], CB, ALU.mult, [pk_] + kq, [("t2", h)])
            tt("dve", t1, t1, t2, ALU.add, [("t1", h), ("t2", h)], [("t1", h)])
            tt("dve", SS[:, n, :, :], SS[:, n, :, :], t1, ALU.add, [("t1", h), "SS"], ["SS"])
            if n % 4 == 3 and n < NHC - 1:
                yield
        copy("act", HPv[:, n0 + 1:n0 + NHC, :], SS[:, 0:NHC - 1, 0, :], ["SS"], ["HP"])
        copy("pool", HHc[:, si, :, :], SS[:, NHC - 1, :, :], ["SS"], ["HHc"])
        release("SS", me)
        yield
        while wtag.get("s5w") != (ti, li, "WHK"):
            yield "BLOCK"
        zA = tokA
        for gq in range(16):
            pb, pk = bankf()
            for g4 in range(4):
                g = gq * 4 + g4
                mm(pb[0:32, g4 * 128:(g4 + 1) * 128], HP[:, g, n0:n0 + NHC], WHK[:, 0, g, :], True, False, ["HP", "s5w"], [pk])
                mm(pb[0:32, g4 * 128:(g4 + 1) * 128], XC[:, g, n0:n0 + NHC], WHK[:, 1, g, :], False, True, [("XC", h), "s5w"], [pk])
            pv = pb[0:32, :].rearrange("p (g j c) -> p j g c", g=4, j=8)
            actf(zA[r0:r1, :, gq * 64:(gq + 1) * 64].rearrange("p j (g c) -> p j g c", g=4), pv,
                 AF.Gelu_apprx_tanh, [pk], [("tokA", h)])
            if gq % 4 == 3 and gq < 15:
                yield
        release("HP", me)
        if h == 1 and np_ is not None:
            s5w_load(np_[0], np_[1], "WS3" if layers[np_[1]] % 2 == 0 else "KV")
        yield
        for jp in range(2):
            pb, pk = bankb()
            for jj in range(4):
                j = jp * 4 + jj
                for kc in range(8):
                    c0 = (jj * 8 + kc) * 32
                    tr(pb[:, c0:c0 + 32], zA[r0:r1, j, kc * 128:(kc + 1) * 128], identb[r0:r1, r0:r1],
                       [("tokA", h), "identb"], [pk])
            for jj in range(4):
                j = jp * 4 + jj
                copy(ev_eng(), zB[:, :, h * HT + j:(h + 1) * HT:8], pb[:, jj * 256:(jj + 1) * 256].rearrange("p (k n) -> p k n", k=8),
                     [pk], [("zB", h)])
        yield
        sl = use(ti, li, "M")

        def ev_glu(mt, pb, pk):
            sb = sigb[:, :, t]
            for i in range(2):
                S.op("act", lambda e, i=i: e.activation(out=sb[:, i, :], in_=pb[:, i * HT:(i + 1) * HT], func=AF.Sigmoid,
                                                        bias=bgluc[:, si, mt + i:mt + i + 1], scale=1.0),
                     [pk, ("bgluc", si)], [("sigb", h)])
            gv = gB[:, mt:mt + 2, t]
            tt("dve", gv, gv, sb, ALU.mult, [("sigb", h), ("gB", h)], [("gB", h)])
            tt("dve", gv, gv, zB[:, mt:mt + 2, t], ALU.mult, [("zB", h), ("gB", h)], [("gB", h)])
        mm_fm(h, sl, zB, ("zB", h), range(0, 4), ev_glu)
        yield
        mm_fm(h, sl, zB, ("zB", h), range(4, 8), ev_glu)
        if h == 1 and np_ is not None:
            load_role(np_[0], np_[1], "E2")
        yield
        sl = use(ti, li, "OUT")
        out_proj(h, sl, range(0, 4))
        yield
        out_proj(h, sl, range(4, 8))
        if h == 1 and np_ is not None:
            load_role(np_[0], np_[1], "M")
        yield

    def wait_slot(ti, li, role):
        X, Y, Z = roles(ti, li)
        slot = {"E1": Y, "E2": X, "M": Z, "OUT": X}[role]
        while wtag.get(slot) != (ti, li, role):
            yield "BLOCK"

    def att_gen(ti, h, li, l):
        d = L[l]
        si = l // 2
        t = tk(h)
        t0 = ti * T + h * HT
        me = (ti, li, h)
        np_ = nxt_pos(ti, li)
        X, Y, Z = roles(ti, li)
        S.dma("sp", cosT[:, t], ropeC[:, t0:t0 + HT], R=[("rope", "c")], W=[("cosT", h)], semkey="rope%d" % h)
        S.dma("sp", sinT[:, t], ropeS[:, t0:t0 + HT], R=[("rope", "s")], W=[("sinT", h)], semkey="rope%d" % h)
        rmsnorm(h)
        norm_apply(h, li, hB[:, :, t], ("hB", h))
        if h == 0:
            copy("pool", KT[:, :, 0:128], KTc[:, si, :, :], ["KTc"], [("KT", 0)])
            copy("pool", Vpad[:, 0, :, :, :], Vc[:, si, :, :, :], ["Vc"], [("Vpad", 0)])
        yield

        def rope_pair2(dst, dkey, wA, wB, wkeys):
            pa, pka = bankf()
            pbb, pkb = bankf()
            for i in range(2):
                for kc in range(8):
                    mm(pa[:, i * HT:(i + 1) * HT], wA(i, kc), hB[:, kc, t], kc == 0, kc == 7, [("hB", h)] + wkeys, [pka])
            for i in range(2):
                for kc in range(8):
                    mm(pbb[:, i * HT:(i + 1) * HT], wB(i, kc), hB[:, kc, t], kc == 0, kc == 7, [("hB", h)] + wkeys, [pkb])
            cb = cosT[:, t].unsqueeze(1).broadcast_to([128, 2, HT])
            sb_ = sinT[:, t].unsqueeze(1).broadcast_to([128, 2, HT])
            sc3 = T12(h).rearrange("p (a b) -> p a b", a=2)
            tkeys = [("t1", h), ("t2", h)]
            tt("dve", sc3, pa[:, :].rearrange("p (a b) -> p a b", a=2), cb, ALU.mult, [pka, ("cosT", h)], tkeys)
            tt("dve", dst, pbb[:, :].rearrange("p (a b) -> p a b", a=2), sb_, ALU.mult, [pkb, ("sinT", h)], dkey)
            tt("pool", dst, dst, sc3, ALU.add, tkeys + dkey, dkey)

        for _ in wait_slot(ti, li, "E2"):
            yield _
        for _ in wait_slot(ti, li, "E1"):
            yield _
        for m2 in range(4):
            rope_pair2(zB[:, 2 * m2:2 * m2 + 2, t], [("zB", h)],
                       lambda i, kc, m2=m2: wsl[X][:, kc, (2 * m2 + i) * 128:(2 * m2 + i + 1) * 128],
                       lambda i, kc, m2=m2: wsl[Y][:, kc, (2 * m2 + i) * 128:(2 * m2 + i + 1) * 128], [("wsl", X), ("wsl", Y)])
            if m2 < 3:
                yield
        if h == 1:
            load_role(ti, li, "OUT")
            if np_ is not None:
                load_role(np_[0], np_[1], "E1")
        yield
        while wtag.get("s5w") != (ti, li, "KV"):
            yield "BLOCK"
        kslots = [("KT", 1 + 2 * h), ("KT", 2 + 2 * h)]
        rope_pair2(KT[:, :, 128 + h * HT:128 + (h + 1) * HT], kslots,
                   lambda i, kc: kvw[:, kc, i * 128:(i + 1) * 128],
                   lambda i, kc: kvw[:, kc, 256 + i * 128:256 + (i + 1) * 128], ["s5w"])
        for b2 in range(2):
            blk = 2 * h + b2
            pb, pk = bankf()
            for kc in range(8):
                mm(pb[:, 0:128], hB[:, kc, blk * 128:(blk + 1) * 128], kvw[:, kc, 512:640], kc == 0, kc == 7, [("hB", h), "s5w"], [pk])
            pv = pb[:, 0:128].rearrange("p (g d) -> p g d", g=2)
            copy("act", Vpad[:, blk + 1, :, 0, 0:64], pv, [pk], [("Vpad", blk + 1)])
            copy("act", Vpad[:, blk + 1, :, 1, 64:128], pv, [pk], [("Vpad", blk + 1)])
        if h == 1 and np_ is not None:
            s5w_load(np_[0], np_[1], "WS3" if layers[np_[1]] % 2 == 0 else "KV")
        yield
        for _ in wait_slot(ti, li, "M"):
            yield _
        ev_g = lambda mt, pb, pk: actf(gB[:, mt:mt + 2, t], pb[:, :].rearrange("p (a b) -> p a b", a=2), AF.Silu, [pk], [("gB", h)])
        mm_fm(h, Z, hB, ("hB", h), range(0, 4), ev_g)
        yield
        mm_fm(h, Z, hB, ("hB", h), range(4, 8), ev_g)
        if h == 1 and np_ is not None:
            load_role(np_[0], np_[1], "E2")
        yield
        for b2 in range(2):
            blk = 2 * h + b2
            first = (ti == 0 and blk == 0)
            for g in range(2):
                combos = [(par, kb) for par in range(2) for kb in ((1,) if first else (0, 1))]
                pts = []
                for ci_, (par, kb) in enumerate(combos):
                    s_ = blk + kb
                    pb, pk = bankf()
                    lo, hi = par * 64, (par + 1) * 64
                    mm(pb[:, :], KT[lo:hi, g, s_ * 128:(s_ + 1) * 128], zB[lo:hi, g * 4:(g + 1) * 4, blk * 128:(blk + 1) * 128],
                       True, False, [("KT", s_), ("zB", h)], [pk])
                    mm(pb[:, :], identb, mask2[:, kb, :].unsqueeze(1).broadcast_to([128, 4, 128]),
                       False, True, ["identb", "mask2"], [pk])
                    slot = h * 4 + ci_
                    actf(PT[:, slot, :], pb[:, :], AF.Exp, [pk], [("PT", slot)], scale=0.125)
                    pts.append((par, s_, slot))
                po, pko = bankf()
                pd, pkd = bankf()
                for i_, (par, s_, slot) in enumerate(pts):
                    mm(po[:, :], Vpad[:, s_, g, par, :], PT[:, slot, :], i_ == 0, i_ == len(pts) - 1, [("Vpad", s_), ("PT", slot)], [pko])
                for i_, (par, s_, slot) in enumerate(pts):
                    mm(pd[:, :], ones2[:, par, :], PT[:, slot, :], i_ == 0, i_ == len(pts) - 1, ["ones2", ("PT", slot)], [pkd])
                sc = T12(h)
                sc4 = sc.rearrange("p (k q) -> p k q", k=4)
                tkeys = [("t1", h), ("t2", h)]
                tt("dve", sc4, pd[:, :].rearrange("p (k q) -> p k q", k=4),
                   sinke[:, si, g * 4:(g + 1) * 4].unsqueeze(2).broadcast_to([128, 4, 128]), ALU.add,
                   [pkd, ("sinke", si)], tkeys)
                actf(sc, sc, AF.Ln, tkeys, tkeys)
                actf(sc, sc, AF.Exp, tkeys, tkeys, scale=-1.0)
                tt("dve", sc, po[:, :], sc, ALU.mult, [pko] + tkeys, tkeys)
                gv = gB[:, g * 4:(g + 1) * 4, blk * 128:(blk + 1) * 128]
                tt("dve", gv, gv, sc4, ALU.mult, tkeys + [("gB", h)], [("gB", h)])
                if not (g == 1 and b2 == 1):
                    yield
            if b2 == 1 and h == 1:
                copy("pool", KTc[:, si, :, :], KT[:, :, 512:640], [("KT", 4)], ["KTc"])
                copy("pool", Vc[:, si, :, :, :], Vpad[:, 4, :, :, :], [("Vpad", 4)], ["Vc"])
        yield
        for _ in wait_slot(ti, li, "OUT"):
            yield _
        out_proj(h, X, range(0, 4))
        yield
        out_proj(h, X, range(4, 8))
        if h == 1 and np_ is not None:
            load_role(np_[0], np_[1], "M")
        yield

    def tile_gen(ti, h):
        S.epoch = ti
        t = tk(h)
        t0 = ti * T + h * HT
        me = (ti, "io", h)
        for _ in acquire("HP", me):
            yield _
        for b2 in range(2):
            blk = 2 * h + b2
            xs = xtok[:, b2, :]
            S.dma("sp", xs, x[t0 + b2 * 128:t0 + (b2 + 1) * 128, :], W=["HP"], semkey="xin")
            for hh in range(2):
                pb, pk = bankf()
                for q in range(4):
                    kc = hh * 4 + q
                    tr(pb[:, q * 128:(q + 1) * 128], xs[:, kc * 128:(kc + 1) * 128], identf, ["HP", "identf"], [pk])
                copy(ev_eng(), xB[:, hh * 4:(hh + 1) * 4, blk * 128:(blk + 1) * 128],
                     pb[:, :].rearrange("p (k t) -> p k t", k=4), [pk], [("xB", h)])
        release("HP", me)
        yield
        for li, l in enumerate(layers):
            g_ = ssm_gen(ti, h, li, l) if l % 2 == 0 else att_gen(ti, h, li, l)
            for v_ in g_:
                yield v_
        for _ in acquire("SS", me):
            yield _
        rmsnorm(h)
        yh = yB[:, :, t]
        norm_apply(h, 4, yh, "SS")
        yield
        for _ in acquire("HP", me):
            yield _
        for b2 in range(2):
            blk = 2 * h + b2
            xs = xtok[:, b2, :]
            for hh in range(2):
                pb, pk = bankf()
                for q in range(4):
                    kc = hh * 4 + q
                    tr(pb[:, q * 128:(q + 1) * 128], yB[:, kc, blk * 128:(blk + 1) * 128], identf, ["SS", "identf"], [pk])
                copy(ev_eng(), xs[:, hh * 512:(hh + 1) * 512], pb[:, :], [pk], ["HP"])
            S.dma("sp", out[t0 + b2 * 128:t0 + (b2 + 1) * 128, :], xs, R=["HP"], W=["out"], semkey="xin")
        release("HP", me)
        release("SS", me)
        yield

    if n_tiles > 0 and NL > 0:
        for r_ in ("E1", "E2", "M"):
            load_role(0, 0, r_)
        s5w_load(0, 0, "WS3" if layers[0] % 2 == 0 else "KV")
    LAG = 4

    def chain(h):
        for ti in range(n_tiles):
            for v_ in tile_gen(ti, h):
                yield v_

    gens = [chain(0), chain(1)]
    alive = [True, True]
    step = 0
    blocked = 0
    while any(alive):
        progressed = False
        for hh_ in range(2):
            if not alive[hh_]:
                continue
            if hh_ == 1 and step < LAG and alive[0]:
                continue
            try:
                v_ = next(gens[hh_])
                if v_ != "BLOCK":
                    progressed = True
            except StopIteration:
                alive[hh_] = False
                progressed = True
        step += 1
        blocked = 0 if progressed else blocked + 1
        assert blocked < 1000, ("interleave deadlock", wtag, locks)
    S.barrier()

    with nc.Block() as block:
        S.emit(block)
    stack.close()
    return nc


def host_inputs(inputs, layers=(0, 1, 2, 3)):
    f32 = np.float32
    m = {}
    m["c_identf"] = np.eye(128, dtype=f32)
    q = np.arange(128)
    kj = np.arange(128)
    mprev = (kj[:, None] > q[None, :]).astype(f32)
    mcur = (kj[:, None] <= q[None, :]).astype(f32)
    m["c_mask2"] = np.ascontiguousarray(np.stack([mprev, mcur], axis=1))
    o2 = np.zeros((128, 2, 128), f32)
    o2[:, 0, 0:64] = 1.0
    o2[:, 1, 64:128] = 1.0
    m["c_ones2"] = o2
    ii = np.arange(128) // 16
    m["c_tri"] = (ii[None, :] >= ii[:, None]).astype(f32)
    m["c_fidx"] = (np.arange(128) % 32).astype(f32).reshape(128, 1)
    m["c_sgn"] = np.where((np.arange(128) % 64) < 32, -1.0, 1.0).astype(f32).reshape(128, 1)
    m["c_pos"] = np.arange(SEQ, dtype=f32)

    def col(v):
        return np.ascontiguousarray(np.asarray(v, f32).reshape(8, 128).T)

    m["fnorm_col"] = col(inputs["final_norm"])
    for l in layers:
        p = "l%d_" % l
        m[p + "norm_col"] = col(inputs[p + "norm"])
        if l % 2 == 0:
            m[p + "w_in"] = np.ascontiguousarray(inputs[p + "w_in"], f32)
            m[p + "w_glu"] = np.ascontiguousarray(inputs[p + "w_glu"], f32)
            m[p + "w_out"] = np.ascontiguousarray(inputs[p + "w_out"], f32)
            m[p + "bglu_col"] = col(inputs[p + "b_glu"])
            m[p + "aRe"] = np.ascontiguousarray(np.tile(np.asarray(inputs[p + "a_re"], f32).T, (2, 1)))
            m[p + "aIm"] = np.ascontiguousarray(np.tile(np.asarray(inputs[p + "a_im"], f32).T, (2, 1)))
            m[p + "ls"] = np.ascontiguousarray(np.tile(np.asarray(inputs[p + "log_step"], f32)[None, :], (128, 1)))
            m[p + "dcol"] = np.ascontiguousarray(np.tile(np.asarray(inputs[p + "d"], f32).reshape(64, 16).T, (8, 1)))
            m[p + "cRe"] = np.ascontiguousarray(np.tile(np.asarray(inputs[p + "c_re"], f32).transpose(2, 0, 1), (2, 1, 1)))
            m[p + "cIm"] = np.ascontiguousarray(np.tile(np.asarray(inputs[p + "c_im"], f32).transpose(2, 0, 1), (2, 1, 1)))
            m[p + "bRe"] = np.ascontiguousarray(np.tile(np.asarray(inputs[p + "b_re"], f32).transpose(1, 0, 2), (2, 1, 1)))
            m[p + "bIm"] = np.ascontiguousarray(np.tile(np.asarray(inputs[p + "b_im"], f32).transpose(1, 0, 2), (2, 1, 1)))
        else:
            w = np.asarray(inputs[p + "w_in"], f32)
            m[p + "w_in"] = np.ascontiguousarray(w)
            dd = np.arange(64)
            partner = np.where(dd < 32, dd + 32, dd - 32)
            qperm = (np.arange(16)[:, None] * 64 + partner[None, :]).reshape(-1)
            m[p + "w_qp"] = np.ascontiguousarray(w[:, qperm])
            k0 = w[:, 1024:1088]
            k1 = w[:, 1088:1152]
            m[p + "w_kk"] = np.ascontiguousarray(np.concatenate(
                [k0, k0, k1, k1, k0[:, partner], k0[:, partner], k1[:, partner], k1[:, partner]], axis=1))
            m[p + "w_out"] = np.ascontiguousarray(inputs[p + "w_out"], f32)
            s = np.asarray(inputs[p + "sinks"], f32)
            sc = np.zeros((128, 8), f32)
            for kc in range(8):
                sc[0:64, kc] = s[2 * kc]
                sc[64:128, kc] = s[2 * kc + 1]
            m[p + "sink_col"] = sc
    return m


INPUT_NAMES = (
    "x",
    "l0_norm", "l0_w_in", "l0_a_re", "l0_a_im", "l0_log_step", "l0_b_re", "l0_b_im", "l0_c_re", "l0_c_im", "l0_d",
    "l0_w_glu", "l0_b_glu", "l0_w_out",
    "l1_norm", "l1_w_in", "l1_sinks", "l1_w_out",
    "l2_norm", "l2_w_in", "l2_a_re", "l2_a_im", "l2_log_step", "l2_b_re", "l2_b_im", "l2_c_re", "l2_c_im", "l2_d",
    "l2_w_glu", "l2_b_glu", "l2_w_out",
    "l3_norm", "l3_w_in", "l3_sinks", "l3_w_out",
    "final_norm",
)

_NC_CACHE = {}


def kernel(**inputs):
    layers = (0, 1, 2, 3)
    missing = [n for n in INPUT_NAMES if n not in inputs]
    assert not missing, missing
    if "full" not in _NC_CACHE:
        _NC_CACHE["full"] = build(16, layers)
    nc = _NC_CACHE["full"]
    shared = host_inputs(inputs, layers)
    xs = np.asarray(inputs["x"], np.float32)
    in_maps = []
    for c in range(8):
        mc = dict(shared)
        mc["x"] = np.ascontiguousarray(xs[c])
        in_maps.append(mc)
    res = run_bass_kernel_spmd(nc, in_maps, core_ids=list(range(8)))
    return np.stack([np.asarray(r["out"], np.float32) for r in res.results], axis=0)
```

```python
import math
import os
from contextlib import ExitStack

import numpy as np
import concourse.bass as bass
import concourse.mybir as mybir
from concourse.bass_utils import run_bass_kernel_spmd

F32 = mybir.dt.float32
BF16 = mybir.dt.bfloat16
I32 = mybir.dt.int32
AF = mybir.ActivationFunctionType
ALU = mybir.AluOpType

T = 512
NCH = 64
NH = 32
SEQ = 8192
DM = 1024
EPS = 1e-5
TWO_PI = 2.0 * math.pi
SIN_SCALE = 6.28318


class Sched:
    CE = ("pe", "act", "dve", "pool")

    def __init__(self, nc, stack):
        self.nc = nc
        self.stack = stack
        self.q = {e: [] for e in ("pe", "act", "dve", "pool", "sp")}
        self.esem = {e: [stack.enter_context(nc.semaphore("c_%s_%d" % (e, i))) for i in range(4)] for e in self.CE}
        self.ecnt = {(e, i): 0 for e in self.CE for i in range(4)}
        self.epoch = 0
        self.last_w = {}
        self.readers = {}
        self.known = {e: {} for e in self.q}
        self.dsem = {}
        self.dcnt = {}
        self.semkey_of = {}
        self.all_tokens = {}

    def _deps(self, eng, R, W):
        deps = []
        for k in R:
            t = self.last_w.get(k)
            if t is not None:
                deps.append(t)
        for k in W:
            t = self.last_w.get(k)
            if t is not None:
                deps.append(t)
            deps.extend(self.readers.get(k, ()))
        waits = {}
        for (sem, val, e2) in deps:
            if e2 == eng and eng == "pe":
                continue
            if e2 == "dma":
                val = self.dcnt[self.semkey_of[sem]]
            if self.known[eng].get(sem, 0) >= val:
                continue
            if waits.get(sem, 0) < val:
                waits[sem] = val
        for sem, val in waits.items():
            self.q[eng].append(lambda e, s=sem, v=val: e.wait_ge(s, v))
            self.known[eng][sem] = val

    def _record(self, tok, R, W):
        for k in R:
            self.readers.setdefault(k, []).append(tok)
        for k in W:
            self.last_w[k] = tok
            self.readers[k] = []
        self.all_tokens[tok[0]] = max(self.all_tokens.get(tok[0], 0), tok[1])

    def op(self, eng, fn, R=(), W=()):
        self._deps(eng, R, W)
        i = self.epoch % 4
        sem = self.esem[eng][i]
        self.ecnt[(eng, i)] += 1
        val = self.ecnt[(eng, i)]
        self.q[eng].append(lambda e, f=fn, s=sem: f(e).then_inc(s, 1))
        self._record((sem, val, eng), R, W)

    def dma(self, queue, out, in_, R=(), W=(), semkey=None):
        self._deps(queue, R, W)
        if semkey not in self.dsem:
            self.dsem[semkey] = self.stack.enter_context(self.nc.semaphore("d_%s" % semkey))
            self.dcnt[semkey] = 0
            self.semkey_of[self.dsem[semkey]] = semkey
        sem = self.dsem[semkey]
        self.dcnt[semkey] += 16
        val = self.dcnt[semkey]
        self.q[queue].append(lambda e, o=out, i=in_, s=sem: e.dma_start(out=o, in_=i).then_inc(s, 16))
        self._record((sem, val, "dma"), R, W)

    def barrier(self, skip=()):
        for eng in self.q:
            for sem, val in self.all_tokens.items():
                if self.semkey_of.get(sem) in skip:
                    continue
                if self.known[eng].get(sem, 0) < val:
                    self.q[eng].append(lambda e, s=sem, v=val: e.wait_ge(s, v))
                    self.known[eng][sem] = val

    def emit(self, block):
        q = self.q

        @block.tensor
        def _(e):
            for f in q["pe"]:
                f(e)

        @block.scalar
        def _(e):
            for f in q["act"]:
                f(e)

        @block.vector
        def _(e):
            for f in q["dve"]:
                f(e)

        @block.gpsimd
        def _(e):
            for f in q["pool"]:
                f(e)

        @block.sync
        def _(e):
            for f in q["sp"]:
                f(e)


SSM_HOST = ("norm", "bglu", "aRe", "aIm", "ls", "cRe", "cIm", "bRe", "bIm", "dcol")


def build(n_tiles=16, layers=(0, 1, 2, 3)):
    nc = bass.Bass("TRN2", target_bir_lowering=False)
    stack = ExitStack()
    S = Sched(nc, stack)

    def din(name, shape, dt=F32):
        return nc.dram_tensor(name, list(shape), dt, kind="ExternalInput").ap()

    def dscr(name, shape, dt):
        return nc.dram_tensor(name, list(shape), dt, kind="Internal").ap()

    x = din("x", [SEQ, DM])
    out = nc.dram_tensor("out", [SEQ, DM], F32, kind="ExternalOutput").ap()
    c_identf = din("c_identf", [128, 128])
    c_mask2 = din("c_mask2", [128, 2, 128])
    c_ones2 = din("c_ones2", [128, 2, 128])
    c_tri = din("c_tri", [128, 128])
    c_fidx = din("c_fidx", [128, 1])
    c_sgn = din("c_sgn", [128, 1])
    c_pos = din("c_pos", [SEQ])
    fnorm = din("fnorm_col", [128, 8])
    L = {}
    for l in layers:
        p = "l%d_" % l
        d = {}
        d["norm"] = din(p + "norm_col", [128, 8])
        if l % 2 == 0:
            d["w_in"] = din(p + "w_in", [DM, 2048])
            d["w_glu"] = din(p + "w_glu", [DM, DM])
            d["w_out"] = din(p + "w_out", [DM, DM])
            d["bglu"] = din(p + "bglu_col", [128, 8])
            for nm in ("aRe", "aIm", "ls", "dcol"):
                d[nm] = din(p + nm, [128, 64])
            for nm in ("cRe", "cIm", "bRe", "bIm"):
                d[nm] = din(p + nm, [128, 64, 16])
            d["w_in_b"] = dscr(p + "w_in_b", [DM, 2048], BF16)
            d["w_glu_b"] = dscr(p + "w_glu_b", [DM, DM], BF16)
            d["w_out_b"] = dscr(p + "w_out_b", [DM, DM], BF16)
            d["WS3"] = dscr(p + "WS3", [128, 64, 192], BF16)
            d["WHK"] = dscr(p + "WHK", [128, 2, 64, 128], BF16)
        else:
            d["w_in"] = din(p + "w_in", [DM, 2304])
            d["w_qp"] = din(p + "w_qp", [DM, DM])
            d["w_kk"] = din(p + "w_kk", [DM, 512])
            d["w_out"] = din(p + "w_out", [DM, DM])
            d["sink"] = din(p + "sink_col", [128, 8])
            d["w_in_b"] = dscr(p + "w_in_b", [DM, 2304], BF16)
            d["w_qp_b"] = dscr(p + "w_qp_b", [DM, DM], BF16)
            d["w_kk_b"] = dscr(p + "w_kk_b", [DM, 512], BF16)
            d["w_out_b"] = dscr(p + "w_out_b", [DM, DM], BF16)
        L[l] = d
    has_att = any(l % 2 == 1 for l in layers)
    ropeC = dscr("ropeC", [128, SEQ], F32)
    ropeS = dscr("ropeS", [128, SEQ], F32)

    ARENA_W = 53000
    arena = stack.enter_context(nc.sbuf_tensor("arena", [128, ARENA_W], F32))
    cur = [0]

    def carve(shape, dt, at=None):
        n = int(np.prod(shape[1:]))
        words = n if dt in (F32, I32) else (n + 1) // 2
        if at is None:
            off = cur[0]
            cur[0] += words
            assert cur[0] <= ARENA_W, ("sbuf overflow", cur[0])
        else:
            off = at
            assert off + words <= ARENA_W
        v = arena[:, off:off + words]
        if dt != F32:
            v = v.bitcast(dt)
        if len(shape) > 2:
            names = " ".join("d%d" % i for i in range(1, len(shape)))
            kw = {"d%d" % i: shape[i] for i in range(1, len(shape))}
            v = v.rearrange("p (%s) -> p %s" % (names, names), **kw)
        return v

    identf = carve([128, 128], F32)
    identb = carve([128, 128], BF16)
    onesb = carve([128, 128], BF16)
    mask2 = carve([128, 2, 128], BF16)
    ones2 = carve([128, 2, 128], BF16)
    stage_c = carve([128, 2, 128], F32)
    normc = carve([128, 5, 8], F32)
    bgluc = carve([128, 2, 8], F32)
    sinke = carve([128, 2, 8], F32)
    CAB = carve([128, 2, 2, 2, 64], F32)
    HHc = carve([128, 2, 2, 64], F32)
    pro0 = cur[0]
    scr = carve([128, 2, 3, T // 2], F32)

    def RSTD(h):
        return scr[:, h, 0, :]

    def T1(h):
        return scr[:, h, 1, :]

    def T2(h):
        return scr[:, h, 2, :]

    def T12(h):
        return scr[:, h, 1:3, :].rearrange("p a b -> p (a b)")
    sigb = carve([128, 2, T], BF16)
    cosT = carve([128, T], F32)
    sinT = carve([128, T], F32)
    KT = carve([128, 2, 640], BF16)
    Vpad = carve([128, 5, 2, 2, 128], BF16)
    KTc = carve([128, 2, 2, 128], BF16)
    Vc = carve([128, 2, 2, 2, 128], BF16)
    xB = carve([128, 8, T], F32)
    hB = carve([128, 8, T], BF16)
    gB = carve([128, 8, T], BF16)
    zB = carve([128, 8, T], BF16)
    main0 = cur[0]
    tokA_off = cur[0]
    tokA = carve([128, 8, DM], BF16)
    XC = carve([128, 64, 64], BF16)
    HP_off = cur[0]
    HP = carve([128, 64, 64], BF16)
    xtok = carve([128, 2, DM], F32, at=HP_off)
    SS = carve([128, NH, 2, 64], F32)
    s5w = carve([128, 64 * 256], BF16)
    wsl = [carve([128, 8, DM], BF16) for _ in range(3)]
    PT = tokA.rearrange("p a b -> p (a b)")[:, 0:8 * T].rearrange("p (s t) -> p s t", s=8)
    yB = SS.rearrange("p a b c -> p (a b c)").rearrange("p (k t) -> p k t", k=8)
    WS3 = s5w[:, 0:64 * 192].rearrange("p (g r) -> p g r", g=64)
    WHK = s5w.rearrange("p (s g r) -> p s g r", s=2, g=64)
    kvw = s5w[:, 0:8 * 640].rearrange("p (k n) -> p k n", k=8)

    psf = [stack.enter_context(nc.psum_tensor("psf%d" % i, [128, 512], F32)) for i in range(6)]
    psb = [stack.enter_context(nc.psum_tensor("psb%d" % i, [128, 1024], BF16)) for i in range(2)]
    rr = {"f": 0, "b": 0, "e": 0}

    def bankf():
        i = rr["f"] % 6
        rr["f"] += 1
        return psf[i], ("psf", i)

    def bankb():
        i = rr["b"] % 2
        rr["b"] += 1
        return psb[i], ("psb", i)

    def ev_eng():
        rr["e"] += 1
        return "act" if rr["e"] % 2 else "dve"

    def copy(eng, o, i, R, W):
        if eng != "act" and o.dtype == i.dtype:
            S.op(eng, lambda e: e.tensor_single_scalar(out=o, in_=i, scalar=1.0, op=ALU.mult), R, W)
            return
        if eng == "act":
            S.op("act", lambda e: e.activation(out=o, in_=i, func=AF.Copy), R, W)
        else:
            S.op(eng, lambda e: e.tensor_copy(out=o, in_=i), R, W)

    def tt(eng, o, a, b, op, R, W):
        S.op(eng, lambda e: e.tensor_tensor(out=o, in0=a, in1=b, op=op), R, W)

    def ts(eng, o, a, s1, op, R, W):
        S.op(eng, lambda e: e.tensor_single_scalar(out=o, in_=a, scalar=s1, op=op), R, W)

    def actf(o, i, func, R, W, scale=1.0, bias=0.0):
        S.op("act", lambda e: e.activation(out=o, in_=i, func=func, scale=scale, bias=bias), R, W)

    def mm(o, lhsT, rhs, start, stop, R, W):
        S.op("pe", lambda e: e.matmul(o, lhsT=lhsT, rhs=rhs, start=start, stop=stop), R, W)

    def tr(o, i, ident, R, W):
        S.op("pe", lambda e: e.transpose(o, i, ident), R, W)

    S.dma("sp", identf, c_identf, W=["identf"], semkey="c0")
    copy("dve", identb, identf, ["identf"], ["identb"])
    S.op("dve", lambda e: e.memset(onesb, 1.0), (), ["onesb"])
    S.dma("sp", stage_c, c_mask2, W=["stage_c"], semkey="c1")
    S.op("dve", lambda e: e.tensor_scalar(out=mask2, in0=stage_c, scalar1=-1.0, scalar2=30000.0, op0=ALU.add, op1=ALU.mult),
         ["stage_c"], ["mask2"])
    S.dma("sp", stage_c, c_ones2, R=(), W=["stage_c"], semkey="c1")
    copy("dve", ones2, stage_c, ["stage_c"], ["ones2"])
    S.op("dve", lambda e: e.memset(HHc, 0.0), (), ["HHc"])
    for li, l in enumerate(layers):
        S.dma("sp", normc[:, li, :], L[l]["norm"], W=[("normc", li)], semkey="c2")
    S.dma("sp", normc[:, 4, :], fnorm, W=[("normc", 4)], semkey="c2")

    for l in layers:
        d = L[l]
        names = ("w_in", "w_glu", "w_out") if l % 2 == 0 else ("w_in", "w_qp", "w_kk", "w_out")
        for nm in names:
            for rb in range(8):
                S.dma("pool", d[nm + "_b"][rb * 128:(rb + 1) * 128, :], d[nm][rb * 128:(rb + 1) * 128, :], W=[("wscr", l, nm, rb)], semkey="wc_%d_%s" % (l, nm))

    if has_att:
        pw = 2048
        base = pro0
        posb = carve([128, pw], F32, at=base)
        tq = carve([128, pw], F32, at=base + pw)
        tqi = carve([128, pw], I32, at=base + 2 * pw)
        tqf = carve([128, pw], F32, at=base + 3 * pw)
        res = carve([128, pw], F32, at=base + 4 * pw)
        fidx = carve([128, 1], F32, at=base + 5 * pw)
        invf = carve([128, 1], F32, at=base + 5 * pw + 1)
        sgn = carve([128, 1], F32, at=base + 5 * pw + 2)
        S.dma("sp", fidx, c_fidx, W=["pp"], semkey="c3")
        S.dma("sp", sgn, c_sgn, W=["pp"], semkey="c3")
        actf(invf, fidx, AF.Exp, ["pp"], ["pp"], scale=-math.log(10000.0) / 32.0)
        for ch in range(SEQ // pw):
            S.dma("sp", posb, c_pos[ch * pw:(ch + 1) * pw].partition_broadcast(128), W=["pp"], semkey="c4")
            S.op("dve", lambda e: e.tensor_scalar_mul(out=posb, in0=posb, scalar1=invf[:, 0:1]), ["pp", "pp"], ["pp"])
            ts("dve", posb, posb, 1.0 / TWO_PI, ALU.mult, ["pp"], ["pp"])
            for which, shift, dst in (("s", 0.0, ropeS), ("c", 0.25, ropeC)):
                ts("dve", tq, posb, shift, ALU.add, ["pp"], ["pp"])
                copy("dve", tqi, tq, ["pp"], ["pp"])
                copy("dve", tqf, tqi, ["pp"], ["pp"])
                tt("dve", tq, tq, tqf, ALU.subtract, ["pp", "pp"], ["pp"])
                actf(res, tq, AF.Sin, ["pp"], ["pp"], scale=SIN_SCALE)
                if which == "s":
                    S.op("dve", lambda e: e.tensor_scalar_mul(out=res, in0=res, scalar1=sgn[:, 0:1]), ["pp", "pp"], ["pp"])
                S.dma("sp", dst[:, ch * pw:(ch + 1) * pw], res, R=["pp"], W=[("rope", which)], semkey="c5")
        for li, l in enumerate(layers):
            if l % 2 == 1:
                si = (l // 2)
                S.dma("sp", sinke[:, si, :], L[l]["sink"], W=[("sinke", si)], semkey="c6_%d" % si)
                actf(sinke[:, si, :], sinke[:, si, :], AF.Exp, [("sinke", si)], [("sinke", si)])

    for l in layers:
        if l % 2:
            continue
        d = L[l]
        si = l // 2
        o = [pro0]

        def pc(shape, dt=F32):
            n = int(np.prod(shape[1:]))
            w = n if dt in (F32, I32) else (n + 1) // 2
            v = carve(shape, dt, at=o[0])
            o[0] += w
            return v
        NK = 17
        aRe = pc([128, 64]); aIm = pc([128, 64]); ls = pc([128, 64]); dcol = pc([128, 64])
        cRe = pc([128, 64, 16]); cIm = pc([128, 64, 16]); bRe = pc([128, 64, 16]); bIm = pc([128, 64, 16])
        lr = pc([128, 64]); lim = pc([128, 64])
        Er = pc([128, NK, 64]); Ei = pc([128, NK, 64]); Fr = pc([128, NK, 64]); Fi = pc([128, NK, 64])
        cr = pc([128, 64]); ci = pc([128, 64]); w1 = pc([128, 64]); w2 = pc([128, 64])
        WST = pc([128, 64, 8, 16]); WS2T = pc([128, 64, 8, 16], BF16); WHf = pc([128, 64, 8, 16])
        tri = pc([128, 128]); k0t = pc([128, 128])
        U1 = o[0]
        u1 = pc([128, 64, 8, 16])
        U2 = o[0]
        u2 = pc([128, 64, 8, 16])
        o2 = [U1]
        def pc2(shape, dt=F32):
            n = int(np.prod(shape[1:]))
            v = carve(shape, dt, at=o2[0])
            o2[0] += n
            return v
        ex = pc2([128, NK, 64]); tq_ = pc2([128, NK, 64]); t2_ = pc2([128, NK, 64]); ti_ = pc2([128, NK, 64], I32)
        tf_ = pc2([128, NK, 64]); sn = pc2([128, NK, 64]); cs = pc2([128, NK, 64])
        assert o2[0] <= U2
        WS3s = carve([128, 64, 192], BF16, at=U1)
        WHKs = carve([128, 2, 64, 128], BF16, at=U2)
        assert o[0] <= ARENA_W, o[0]
        k = "pp"
        for nm, tl in (("aRe", aRe), ("aIm", aIm), ("ls", ls), ("dcol", dcol), ("cRe", cRe), ("cIm", cIm), ("bRe", bRe), ("bIm", bIm)):
            S.dma("sp", tl, d[nm], W=[k + nm], semkey="pp")
        S.dma("sp", tri, c_tri, W=[k + "tri"], semkey="pp")
        S.dma("sp", bgluc[:, si, :], d["bglu"], W=[("bgluc", si)], semkey="pp")
        lk = [k] + [k + nm for nm in ("aRe", "aIm", "ls", "dcol", "cRe", "cIm", "bRe", "bIm", "tri")]
        actf(ls, ls, AF.Exp, lk, [k, "WS3s", "WHKs"])
        tt("dve", lr, aRe, ls, ALU.mult, [k], [k])
        tt("dve", lim, aIm, ls, ALU.mult, [k], [k])
        for kk in range(-8, 9):
            idx = kk + 8
            actf(ex[:, idx, :], lr, AF.Exp, [k], [k], scale=float(kk))
            ts("dve", tq_[:, idx, :], lim, kk / TWO_PI, ALU.mult, [k], [k])
        for dst, shift in ((sn, 0.0), (cs, 0.25)):
            ts("dve", t2_, tq_, shift, ALU.add, [k], [k])
            copy("dve", ti_, t2_, [k], [k])
            copy("dve", tf_, ti_, [k], [k])
            tt("dve", t2_, t2_, tf_, ALU.subtract, [k], [k])
            actf(dst, t2_, AF.Sin, [k], [k], scale=SIN_SCALE)
        tt("dve", Er, ex, cs, ALU.mult, [k], [k])
        tt("dve", Ei, ex, sn, ALU.mult, [k], [k])
        E1r = Er[:, 9, :]; E1i = Ei[:, 9, :]
        ts("dve", w1, E1r, -1.0, ALU.add, [k], [k])
        tt("dve", cr, w1, aRe, ALU.mult, [k], [k])
        tt("dve", w2, E1i, aIm, ALU.mult, [k], [k])
        tt("dve", cr, cr, w2, ALU.add, [k], [k])
        tt("dve", ci, E1i, aRe, ALU.mult, [k], [k])
        tt("dve", w2, w1, aIm, ALU.mult, [k], [k])
        tt("dve", ci, ci, w2, ALU.subtract, [k], [k])
        tt("dve", w1, aRe, aRe, ALU.mult, [k], [k])
        tt("dve", w2, aIm, aIm, ALU.mult, [k], [k])
        tt("dve", w1, w1, w2, ALU.add, [k], [k])
        S.op("dve", lambda e: e.reciprocal(out=w1, in_=w1), [k], [k])
        tt("dve", cr, cr, w1, ALU.mult, [k], [k])
        tt("dve", ci, ci, w1, ALU.mult, [k], [k])
        crb = cr.unsqueeze(1).broadcast_to([128, NK, 64])
        cib = ci.unsqueeze(1).broadcast_to([128, NK, 64])
        tt("dve", Fr, Er, crb, ALU.mult, [k], [k])
        tt("dve", t2_, Ei, cib, ALU.mult, [k], [k])
        tt("dve", Fr, Fr, t2_, ALU.subtract, [k], [k])
        tt("dve", Fi, Er, cib, ALU.mult, [k], [k])
        tt("dve", t2_, Ei, crb, ALU.mult, [k], [k])
        tt("dve", Fi, Fi, t2_, ALU.add, [k], [k])

        def kview(v, lo, hi, k0, rev):
            vv = v[lo:hi, :, :].rearrange("p k g -> p g k")
            vv = vv[:, :, k0 - 7:k0 + 1][:, :, ::-1] if rev else vv[:, :, k0:k0 + 8]
            return vv.unsqueeze(3).broadcast_to([hi - lo, 64, 8, 16])

        def bview(v, lo, hi):
            return v[lo:hi].unsqueeze(2).broadcast_to([hi - lo, 64, 8, 16])

        for dstT, k0 in ((WST, 15), (WS2T, 7)):
            tt("dve", u1[0:64], bview(bRe, 0, 64), kview(Fr, 0, 64, k0, True), ALU.mult, [k], [k])
            tt("dve", u2[0:64], bview(bIm, 0, 64), kview(Fi, 0, 64, k0, True), ALU.mult, [k], [k])
            tt("dve", dstT[0:64], u1[0:64], u2[0:64], ALU.subtract, [k], [k])
            tt("dve", u1[64:128], bview(bIm, 64, 128), kview(Fr, 64, 128, k0, True), ALU.mult, [k], [k])
            tt("dve", u2[64:128], bview(bRe, 64, 128), kview(Fi, 64, 128, k0, True), ALU.mult, [k], [k])
            tt("dve", dstT[64:128], u1[64:128], u2[64:128], ALU.add, [k], [k])
        tt("dve", u1[0:64], bview(cRe, 0, 64), kview(Er, 0, 64, 9, False), ALU.mult, [k], [k])
        tt("dve", u2[0:64], bview(cIm, 0, 64), kview(Ei, 0, 64, 9, False), ALU.mult, [k], [k])
        tt("dve", WHf[0:64], u1[0:64], u2[0:64], ALU.subtract, [k], [k])
        tt("dve", u1[64:128], bview(cRe, 64, 128), kview(Ei, 64, 128, 9, False), ALU.mult, [k], [k])
        tt("dve", u2[64:128], bview(cIm, 64, 128), kview(Er, 64, 128, 9, False), ALU.mult, [k], [k])
        tt("dve", u1[64:128], u1[64:128], u2[64:128], ALU.add, [k], [k])
        ts("dve", WHf[64:128], u1[64:128], -1.0, ALU.mult, [k], [k])
        for s2 in range(2):
            copy("dve", CAB[:, si, 0, s2, :], Er[:, 16, :], [k], [("CAB", si)])
        ts("dve", CAB[0:64, si, 1, 0, :], Ei[0:64, 16, :], -1.0, ALU.mult, [k], [("CAB", si)])
        copy("dve", CAB[64:128, si, 1, 0, :], Ei[64:128, 16, :], [k], [("CAB", si)])
        ts("dve", CAB[:, si, 1, 1, :], CAB[:, si, 1, 0, :], -1.0, ALU.mult, [("CAB", si)], [("CAB", si)])
        copy("act", WHKs[:, 0, :, :], WHf.rearrange("p g j c -> p g (j c)"), [k], ["WHKs"])
        for g in range(64):
            pb, pk = bankf()
            pb2, pk2 = bankf()
            tr(pb[:, 0:128], WST[:, g, :, :].rearrange("p j c -> p (j c)"), identf, [k, "identf"], [pk])
            mm(pb2[:, 0:128], WS2T[:, g, :, :].rearrange("p j c -> p (j c)"),
               WHKs[:, 0, g, :], True, True, [k, "WHKs"], [pk2])
            copy("act", WS3s[:, g, 0:128], pb[:, 0:128], [pk], ["WS3s"])
            copy("act", WS3s[:, g, 128:192], pb[:, 0:64], [pk], ["WS3s"])
            tt("dve", k0t, pb2[:, 0:128], tri, ALU.mult, [pk2, k], ["k0t"])
            S.op("dve", lambda e, g=g: e.scalar_tensor_tensor(out=WHKs[:, 1, g, :], in0=identf, scalar=dcol[:, g:g + 1],
                                                               in1=k0t, op0=ALU.mult, op1=ALU.add),
                 ["k0t", k, "identf"], ["WHKs"])
        S.dma("sp", d["WS3"], WS3s, R=["WS3s"], W=[("s5scr", l)], semkey="pst")
        S.dma("sp", d["WHK"], WHKs, R=["WHKs"], W=[("s5scr", l)], semkey="pst")
    S.barrier(skip=[kk_ for kk_ in S.dsem if kk_.startswith("wc_")])
    S.op("pool", lambda e: e.memset(Vpad, 0.0), (), ["Vpad"])
    S.op("pool", lambda e: e.memset(KT, 0.0), (), ["KT"])
    S.op("pool", lambda e: e.memset(KTc, 0.0), (), ["KTc"])
    S.op("pool", lambda e: e.memset(Vc, 0.0), (), ["Vc"])

    HT = T // 2
    NHC = NH

    def tk(h):
        return slice(h * HT, (h + 1) * HT)

    def load_w(slot, src, cols, l, nm):
        ncol = cols[1] - cols[0]
        S.dma("sp", wsl[slot][:, :, 0:ncol], src[:, cols[0]:cols[1]].rearrange("(k p) n -> p k n", p=128),
              R=[("wscr", l, nm, rb) for rb in range(8)], W=[("wsl", slot)], semkey="wsl%d" % slot)

    def load_s5w(l, which):
        d = L[l]
        if which == "WS3":
            S.dma("sp", WS3, d["WS3"], R=[("s5scr", l)], W=["s5w"], semkey="s5w")
        elif which == "WHK":
            S.dma("sp", WHK, d["WHK"], R=[("s5scr", l)], W=["s5w"], semkey="s5w")
        else:
            S.dma("sp", kvw[:, :, 0:512], d["w_kk_b"].rearrange("(k p) n -> p k n", p=128),
                  R=[("wscr", l, "w_kk", rb) for rb in range(8)], W=["s5w"], semkey="s5w")
            S.dma("sp", kvw[:, :, 512:640], d["w_in_b"][:, 1152:1280].rearrange("(k p) n -> p k n", p=128),
                  R=[("wscr", l, "w_in", rb) for rb in range(8)], W=["s5w"], semkey="s5w")

    def layer_loads(l, which):
        d = L[l]
        if l % 2 == 0:
            if which == "slot0":
                load_w(0, d["w_in_b"], (0, 1024), l, "w_in")
            elif which == "slot1":
                load_w(1, d["w_in_b"], (1024, 2048), l, "w_in")
            elif which == "slot2":
                load_w(2, d["w_glu_b"], (0, 1024), l, "w_glu")
            else:
                load_s5w(l, "WS3")
        else:
            if which == "slot0":
                load_w(0, d["w_in_b"], (0, 1024), l, "w_in")
            elif which == "slot1":
                load_w(1, d["w_qp_b"], (0, 1024), l, "w_qp")
            elif which == "slot2":
                load_w(2, d["w_in_b"], (1280, 2304), l, "w_in")
            else:
                load_s5w(l, "KV")

    def rmsnorm(h):
        t = tk(h)
        actf(zB[:, :, t], xB[:, :, t], AF.Square, [("xB", h)], [("zB", h)])
        pb, pk = bankf()
        for kc in range(8):
            mm(pb[:, 0:HT], onesb, zB[:, kc, t], kc == 0, kc == 7, ["onesb", ("zB", h)], [pk])
        actf(T2(h), pb[:, 0:HT], AF.Ln, [pk], [("t2", h)], scale=1.0 / DM, bias=EPS)
        actf(RSTD(h), T2(h), AF.Exp, [("t2", h)], [("rstd", h)], scale=-0.5)

    def norm_apply(h, li, dst, dkey):
        t = tk(h)
        for kc in range(8):
            S.op("dve", lambda e, kc=kc: e.scalar_tensor_tensor(out=dst[:, kc, :], in0=xB[:, kc, t],
                                                                 scalar=normc[:, li, kc:kc + 1], in1=RSTD(h),
                                                                 op0=ALU.mult, op1=ALU.mult),
                 [("xB", h), ("rstd", h), ("normc", li)], [dkey])

    def mm_fm(h, slot, act, akey, mts, evac):
        t = tk(h)
        for mt in mts:
            if mt % 2:
                continue
            pb, pk = bankf()
            for i in range(2):
                for kc in range(8):
                    mm(pb[:, i * HT:(i + 1) * HT], wsl[slot][:, kc, (mt + i) * 128:(mt + i + 1) * 128], act[:, kc, t], kc == 0, kc == 7,
                       [("wsl", slot), akey], [pk])
            evac(mt, pb, pk)

    def out_proj(h, slot, mts=8):
        t = tk(h)

        def ev(mt, pb, pk):
            tt("dve", xB[:, mt:mt + 2, t], xB[:, mt:mt + 2, t], pb[:, :].rearrange("p (a b) -> p a b", a=2), ALU.add,
               [pk, ("xB", h)], [("xB", h)])
        mm_fm(h, slot, gB, ("gB", h), mts, ev)

    NL = len(layers)
    wtag = {}
    locks = {}

    def roles(ti, li):
        gl = ti * NL + li
        X, Z = (0, 2) if gl % 2 == 0 else (2, 0)
        return X, 1, Z

    def load_role(ti, li, role):
        l = layers[li]
        d = L[l]
        X, Y, Z = roles(ti, li)
        if l % 2 == 0:
            spec = {"E1": (Y, d["w_in_b"], (1024, 2048), "w_in"), "E2": (X, d["w_in_b"], (0, 1024), "w_in"),
                    "M": (Z, d["w_glu_b"], (0, 1024), "w_glu"), "OUT": (X, d["w_out_b"], (0, 1024), "w_out")}[role]
        else:
            spec = {"E1": (Y, d["w_qp_b"], (0, 1024), "w_qp"), "E2": (X, d["w_in_b"], (0, 1024), "w_in"),
                    "M": (Z, d["w_in_b"], (1280, 2304), "w_in"), "OUT": (X, d["w_out_b"], (0, 1024), "w_out")}[role]
        slot, src, cols, nm = spec
        load_w(slot, src, cols, l, nm)
        wtag[slot] = (ti, li, role)

    def use(ti, li, role):
        X, Y, Z = roles(ti, li)
        slot = {"E1": Y, "E2": X, "M": Z, "OUT": X}[role]
        assert wtag.get(slot) == (ti, li, role), ("weight slot content mismatch", slot, wtag.get(slot), (ti, li, role))
        return slot

    def nxt_pos(ti, li):
        if li + 1 < NL:
            return (ti, li + 1)
        if ti + 1 < n_tiles:
            return (ti + 1, 0)
        return None

    def s5w_load(ti, li, which):
        load_s5w(layers[li], which)
        wtag["s5w"] = (ti, li, which)

    def s5w_use(ti, li, which):
        assert wtag.get("s5w") == (ti, li, which), ("s5w content mismatch", wtag.get("s5w"), (ti, li, which))

    def acquire(name, me):
        while locks.get(name) not in (None, me):
            yield "BLOCK"
        locks[name] = me

    def release(name, me):
        assert locks.get(name) == me
        locks[name] = None

    def nxt(ti, li):
        if li + 1 < len(layers):
            return layers[li + 1]
        if ti + 1 < n_tiles:
            return layers[0]
        return None

    def ssm_gen(ti, h, li, l):
        d = L[l]
        si = l // 2
        t = tk(h)
        n0 = h * NHC
        r0, r1 = h * 32, h * 32 + 32
        me = (ti, li, h)
        np_ = nxt_pos(ti, li)
        rmsnorm(h)
        norm_apply(h, li, hB[:, :, t], ("hB", h))
        yield
        sl = use(ti, li, "E1")
        ev_g = lambda mt, pb, pk: actf(gB[:, mt:mt + 2, t], pb[:, :].rearrange("p (a b) -> p a b", a=2), AF.Silu, [pk], [("gB", h)])
        mm_fm(h, sl, hB, ("hB", h), range(0, 4), ev_g)
        yield
        mm_fm(h, sl, hB, ("hB", h), range(4, 8), ev_g)
        if h == 1 and np_ is not None:
            load_role(np_[0], np_[1], "E1")
        yield
        sl = use(ti, li, "E2")
        uA = tokA.rearrange("p a b -> p (a b)").rearrange("p (g j c) -> p g j c", g=64, j=8, c=16)
        for j in range(8):
            for nt in range(2):
                pb, pk = bankf()
                for kc in range(8):
                    mm(pb[0:32, :], hB[:, kc, h * HT + j:(h + 1) * HT:8], wsl[sl][:, kc, nt * 512:(nt + 1) * 512], kc == 0, kc == 7,
                       [("hB", h), ("wsl", sl)], [pk])
                copy(ev_eng(), uA[r0:r1, nt * 32:(nt + 1) * 32, j, :],
                     pb[0:32, :].rearrange("p (g c) -> p g c", g=32), [pk], [("tokA", h)])
            if j % 2 == 1 and j < 7:
                yield
        if h == 1:
            load_role(ti, li, "OUT")
        yield
        for gq in range(2):
            pb, pk = bankb()
            for g32 in range(32):
                g = gq * 32 + g32
                tr(pb[:, g32 * 32:(g32 + 1) * 32], uA[r0:r1, g, :, :].rearrange("p j c -> p (j c)"), identb[r0:r1, r0:r1],
                   [("tokA", h), "identb"], [pk])
            copy(ev_eng(), XC[:, gq * 32:(gq + 1) * 32, n0:n0 + NHC], pb[:, :].rearrange("p (g n) -> p g n", g=32), [pk], [("XC", h)])
        yield
        for _ in acquire("SS", me):
            yield _
        s5w_use(ti, li, "WS3")
        for gq in range(8):
            pb, pk = bankf()
            for g8 in range(8):
                g = gq * 8 + g8
                for s_ in range(2):
                    c0 = (g8 * 2 + s_) * NHC
                    mm(pb[:, c0:c0 + NHC], WS3[:, g, s_ * 64:s_ * 64 + 128], XC[:, g, n0:n0 + NHC], True, True,
                       ["s5w", ("XC", h)], [pk])
            copy(ev_eng(), SS[:, :, :, gq * 8:(gq + 1) * 8],
                 pb[:, :].rearrange("p (g s n) -> p n s g", g=8, s=2), [pk], ["SS"])
        if h == 1:
            s5w_load(ti, li, "WHK")
        yield
        for _ in acquire("HP", me):
            yield _
        HPv = HP.rearrange("p g n -> p n g")
        t1 = T1(h)[:, 0:128].rearrange("p (s g) -> p s g", s=2)
        t2 = T2(h)[:, 0:128].rearrange("p (s g) -> p s g", s=2)
        CA = CAB[:, si, 0, :, :]
        CB = CAB[:, si, 1, :, :]
        kq = [("CAB", si)]
        copy("act", HPv[:, n0, :], HHc[:, si, 0, :], ["HHc"], ["HP"])
        for n in range(NHC):
            prev = HHc[:, si, :, :] if n == 0 else SS[:, n - 1, :, :]
            pk_ = "HHc" if n == 0 else "SS"
            tt("dve", t1, prev, CA, ALU.mult, [pk_] + kq, [("t1", h)])
            tt("dve", t2, prev[:, ::-1, :], CB, ALU.mult, [pk_] + kq, [("t2", h)])
            tt("dve", t1, t1, t2, ALU.add, [("t1", h), ("t2", h)], [("t1", h)])
            tt("dve", SS[:, n, :, :], SS[:, n, :, :], t1, ALU.add, [("t1", h), "SS"], ["SS"])
            if n % 4 == 3 and n < NHC - 1:
                yield
        copy("act", HPv[:, n0 + 1:n0 + NHC, :], SS[:, 0:NHC - 1, 0, :], ["SS"], ["HP"])
        copy("pool", HHc[:, si, :, :], SS[:, NHC - 1, :, :], ["SS"], ["HHc"])
        release("SS", me)
        yield
        while wtag.get("s5w") != (ti, li, "WHK"):
            yield "BLOCK"
        zA = tokA
        for gq in range(16):
            pb, pk = bankf()
            for g4 in range(4):
                g = gq * 4 + g4
                mm(pb[0:32, g4 * 128:(g4 + 1) * 128], HP[:, g, n0:n0 + NHC], WHK[:, 0, g, :], True, False, ["HP", "s5w"], [pk])
                mm(pb[0:32, g4 * 128:(g4 + 1) * 128], XC[:, g, n0:n0 + NHC], WHK[:, 1, g, :], False, True, [("XC", h), "s5w"], [pk])
            pv = pb[0:32, :].rearrange("p (g j c) -> p j g c", g=4, j=8)
            actf(zA[r0:r1, :, gq * 64:(gq + 1) * 64].rearrange("p j (g c) -> p j g c", g=4), pv,
                 AF.Gelu_apprx_tanh, [pk], [("tokA", h)])
            if gq % 4 == 3 and gq < 15:
                yield
        release("HP", me)
        if h == 1 and np_ is not None:
            s5w_load(np_[0], np_[1], "WS3" if layers[np_[1]] % 2 == 0 else "KV")
        yield
        for jp in range(2):
            pb, pk = bankb()
            for jj in range(4):
                j = jp * 4 + jj
                for kc in range(8):
                    c0 = (jj * 8 + kc) * 32
                    tr(pb[:, c0:c0 + 32], zA[r0:r1, j, kc * 128:(kc + 1) * 128], identb[r0:r1, r0:r1],
                       [("tokA", h), "identb"], [pk])
            for jj in range(4):
                j = jp * 4 + jj
                copy(ev_eng(), zB[:, :, h * HT + j:(h + 1) * HT:8], pb[:, jj * 256:(jj + 1) * 256].rearrange("p (k n) -> p k n", k=8),
                     [pk], [("zB", h)])
        yield
        sl = use(ti, li, "M")

        def ev_glu(mt, pb, pk):
            sb = sigb[:, :, t]
            for i in range(2):
                S.op("act", lambda e, i=i: e.activation(out=sb[:, i, :], in_=pb[:, i * HT:(i + 1) * HT], func=AF.Sigmoid,
                                                        bias=bgluc[:, si, mt + i:mt + i + 1], scale=1.0),
                     [pk, ("bgluc", si)], [("sigb", h)])
            gv = gB[:, mt:mt + 2, t]
            tt("dve", gv, gv, sb, ALU.mult, [("sigb", h), ("gB", h)], [("gB", h)])
            tt("dve", gv, gv, zB[:, mt:mt + 2, t], ALU.mult, [("zB", h), ("gB", h)], [("gB", h)])
        mm_fm(h, sl, zB, ("zB", h), range(0, 4), ev_glu)
        yield
        mm_fm(h, sl, zB, ("zB", h), range(4, 8), ev_glu)
        if h == 1 and np_ is not None:
            load_role(np_[0], np_[1], "E2")
        yield
        sl = use(ti, li, "OUT")
        out_proj(h, sl, range(0, 4))
        yield
        out_proj(h, sl, range(4, 8))
        if h == 1 and np_ is not None:
            load_role(np_[0], np_[1], "M")
        yield

    def wait_slot(ti, li, role):
        X, Y, Z = roles(ti, li)
        slot = {"E1": Y, "E2": X, "M": Z, "OUT": X}[role]
        while wtag.get(slot) != (ti, li, role):
            yield "BLOCK"

    def att_gen(ti, h, li, l):
        d = L[l]
        si = l // 2
        t = tk(h)
        t0 = ti * T + h * HT
        me = (ti, li, h)
        np_ = nxt_pos(ti, li)
        X, Y, Z = roles(ti, li)
        S.dma("sp", cosT[:, t], ropeC[:, t0:t0 + HT], R=[("rope", "c")], W=[("cosT", h)], semkey="rope%d" % h)
        S.dma("sp", sinT[:, t], ropeS[:, t0:t0 + HT], R=[("rope", "s")], W=[("sinT", h)], semkey="rope%d" % h)
        rmsnorm(h)
        norm_apply(h, li, hB[:, :, t], ("hB", h))
        if h == 0:
            copy("pool", KT[:, :, 0:128], KTc[:, si, :, :], ["KTc"], [("KT", 0)])
            copy("pool", Vpad[:, 0, :, :, :], Vc[:, si, :, :, :], ["Vc"], [("Vpad", 0)])
        yield

        def rope_pair2(dst, dkey, wA, wB, wkeys):
            pa, pka = bankf()
            pbb, pkb = bankf()
            for i in range(2):
                for kc in range(8):
                    mm(pa[:, i * HT:(i + 1) * HT], wA(i, kc), hB[:, kc, t], kc == 0, kc == 7, [("hB", h)] + wkeys, [pka])
            for i in range(2):
                for kc in range(8):
                    mm(pbb[:, i * HT:(i + 1) * HT], wB(i, kc), hB[:, kc, t], kc == 0, kc == 7, [("hB", h)] + wkeys, [pkb])
            cb = cosT[:, t].unsqueeze(1).broadcast_to([128, 2, HT])
            sb_ = sinT[:, t].unsqueeze(1).broadcast_to([128, 2, HT])
            sc3 = T12(h).rearrange("p (a b) -> p a b", a=2)
            tkeys = [("t1", h), ("t2", h)]
            tt("dve", sc3, pa[:, :].rearrange("p (a b) -> p a b", a=2), cb, ALU.mult, [pka, ("cosT", h)], tkeys)
            tt("dve", dst, pbb[:, :].rearrange("p (a b) -> p a b", a=2), sb_, ALU.mult, [pkb, ("sinT", h)], dkey)
            tt("pool", dst, dst, sc3, ALU.add, tkeys + dkey, dkey)

        for _ in wait_slot(ti, li, "E2"):
            yield _
        for _ in wait_slot(ti, li, "E1"):
            yield _
        for m2 in range(4):
            rope_pair2(zB[:, 2 * m2:2 * m2 + 2, t], [("zB", h)],
                       lambda i, kc, m2=m2: wsl[X][:, kc, (2 * m2 + i) * 128:(2 * m2 + i + 1) * 128],
                       lambda i, kc, m2=m2: wsl[Y][:, kc, (2 * m2 + i) * 128:(2 * m2 + i + 1) * 128], [("wsl", X), ("wsl", Y)])
            if m2 < 3:
                yield
        if h == 1:
            load_role(ti, li, "OUT")
            if np_ is not None:
                load_role(np_[0], np_[1], "E1")
        yield
        while wtag.get("s5w") != (ti, li, "KV"):
            yield "BLOCK"
        kslots = [("KT", 1 + 2 * h), ("KT", 2 + 2 * h)]
        rope_pair2(KT[:, :, 128 + h * HT:128 + (h + 1) * HT], kslots,
                   lambda i, kc: kvw[:, kc, i * 128:(i + 1) * 128],
                   lambda i, kc: kvw[:, kc, 256 + i * 128:256 + (i + 1) * 128], ["s5w"])
        for b2 in range(2):
            blk = 2 * h + b2
            pb, pk = bankf()
            for kc in range(8):
                mm(pb[:, 0:128], hB[:, kc, blk * 128:(blk + 1) * 128], kvw[:, kc, 512:640], kc == 0, kc == 7, [("hB", h), "s5w"], [pk])
            pv = pb[:, 0:128].rearrange("p (g d) -> p g d", g=2)
            copy("act", Vpad[:, blk + 1, :, 0, 0:64], pv, [pk], [("Vpad", blk + 1)])
            copy("act", Vpad[:, blk + 1, :, 1, 64:128], pv, [pk], [("Vpad", blk + 1)])
        if h == 1 and np_ is not None:
            s5w_load(np_[0], np_[1], "WS3" if layers[np_[1]] % 2 == 0 else "KV")
        yield
        for _ in wait_slot(ti, li, "M"):
            yield _
        ev_g = lambda mt, pb, pk: actf(gB[:, mt:mt + 2, t], pb[:, :].rearrange("p (a b) -> p a b", a=2), AF.Silu, [pk], [("gB", h)])
        mm_fm(h, Z, hB, ("hB", h), range(0, 4), ev_g)
        yield
        mm_fm(h, Z, hB, ("hB", h), range(4, 8), ev_g)
        if h == 1 and np_ is not None:
            load_role(np_[0], np_[1], "E2")
        yield
        for b2 in range(2):
            blk = 2 * h + b2
            first = (ti == 0 and blk == 0)
            for g in range(2):
                combos = [(par, kb) for par in range(2) for kb in ((1,) if first else (0, 1))]
                pts = []
                for ci_, (par, kb) in enumerate(combos):
                    s_ = blk + kb
                    pb, pk = bankf()
                    lo, hi = par * 64, (par + 1) * 64
                    mm(pb[:, :], KT[lo:hi, g, s_ * 128:(s_ + 1) * 128], zB[lo:hi, g * 4:(g + 1) * 4, blk * 128:(blk + 1) * 128],
                       True, False, [("KT", s_), ("zB", h)], [pk])
                    mm(pb[:, :], identb, mask2[:, kb, :].unsqueeze(1).broadcast_to([128, 4, 128]),
                       False, True, ["identb", "mask2"], [pk])
                    slot = h * 4 + ci_
                    actf(PT[:, slot, :], pb[:, :], AF.Exp, [pk], [("PT", slot)], scale=0.125)
                    pts.append((par, s_, slot))
                po, pko = bankf()
                pd, pkd = bankf()
                for i_, (par, s_, slot) in enumerate(pts):
                    mm(po[:, :], Vpad[:, s_, g, par, :], PT[:, slot, :], i_ == 0, i_ == len(pts) - 1, [("Vpad", s_), ("PT", slot)], [pko])
                for i_, (par, s_, slot) in enumerate(pts):
                    mm(pd[:, :], ones2[:, par, :], PT[:, slot, :], i_ == 0, i_ == len(pts) - 1, ["ones2", ("PT", slot)], [pkd])
                sc = T12(h)
                sc4 = sc.rearrange("p (k q) -> p k q", k=4)
                tkeys = [("t1", h), ("t2", h)]
                tt("dve", sc4, pd[:, :].rearrange("p (k q) -> p k q", k=4),
                   sinke[:, si, g * 4:(g + 1) * 4].unsqueeze(2).broadcast_to([128, 4, 128]), ALU.add,
                   [pkd, ("sinke", si)], tkeys)
                actf(sc, sc, AF.Ln, tkeys, tkeys)
                actf(sc, sc, AF.Exp, tkeys, tkeys, scale=-1.0)
                tt("dve", sc, po[:, :], sc, ALU.mult, [pko] + tkeys, tkeys)
                gv = gB[:, g * 4:(g + 1) * 4, blk * 128:(blk + 1) * 128]
                tt("dve", gv, gv, sc4, ALU.mult, tkeys + [("gB", h)], [("gB", h)])
                if not (g == 1 and b2 == 1):
                    yield
            if b2 == 1 and h == 1:
                copy("pool", KTc[:, si, :, :], KT[:, :, 512:640], [("KT", 4)], ["KTc"])
                copy("pool", Vc[:, si, :, :, :], Vpad[:, 4, :, :, :], [("Vpad", 4)], ["Vc"])
        yield
        for _ in wait_slot(ti, li, "OUT"):
            yield _
        out_proj(h, X, range(0, 4))
        yield
        out_proj(h, X, range(4, 8))
        if h == 1 and np_ is not None:
            load_role(np_[0], np_[1], "M")
        yield

    def tile_gen(ti, h):
        S.epoch = ti
        t = tk(h)
        t0 = ti * T + h * HT
        me = (ti, "io", h)
        for _ in acquire("HP", me):
            yield _
        for b2 in range(2):
            blk = 2 * h + b2
            xs = xtok[:, b2, :]
            S.dma("sp", xs, x[t0 + b2 * 128:t0 + (b2 + 1) * 128, :], W=["HP"], semkey="xin")
            for hh in range(2):
                pb, pk = bankf()
                for q in range(4):
                    kc = hh * 4 + q
                    tr(pb[:, q * 128:(q + 1) * 128], xs[:, kc * 128:(kc + 1) * 128], identf, ["HP", "identf"], [pk])
                copy(ev_eng(), xB[:, hh * 4:(hh + 1) * 4, blk * 128:(blk + 1) * 128],
                     pb[:, :].rearrange("p (k t) -> p k t", k=4), [pk], [("xB", h)])
        release("HP", me)
        yield
        for li, l in enumerate(layers):
            g_ = ssm_gen(ti, h, li, l) if l % 2 == 0 else att_gen(ti, h, li, l)
            for v_ in g_:
                yield v_
        for _ in acquire("SS", me):
            yield _
        rmsnorm(h)
        yh = yB[:, :, t]
        norm_apply(h, 4, yh, "SS")
        yield
        for _ in acquire("HP", me):
            yield _
        for b2 in range(2):
            blk = 2 * h + b2
            xs = xtok[:, b2, :]
            for hh in range(2):
                pb, pk = bankf()
                for q in range(4):
                    kc = hh * 4 + q
                    tr(pb[:, q * 128:(q + 1) * 128], yB[:, kc, blk * 128:(blk + 1) * 128], identf, ["SS", "identf"], [pk])
                copy(ev_eng(), xs[:, hh * 512:(hh + 1) * 512], pb[:, :], [pk], ["HP"])
            S.dma("sp", out[t0 + b2 * 128:t0 + (b2 + 1) * 128, :], xs, R=["HP"], W=["out"], semkey="xin")
        release("HP", me)
        release("SS", me)
        yield

    if n_tiles > 0 and NL > 0:
        for r_ in ("E1", "E2", "M"):
            load_role(0, 0, r_)
        s5w_load(0, 0, "WS3" if layers[0] % 2 == 0 else "KV")
    LAG = 4

    def chain(h):
        for ti in range(n_tiles):
            for v_ in tile_gen(ti, h):
                yield v_

    gens = [chain(0), chain(1)]
    alive = [True, True]
    step = 0
    blocked = 0
    while any(alive):
        progressed = False
        for hh_ in range(2):
            if not alive[hh_]:
                continue
            if hh_ == 1 and step < LAG and alive[0]:
                continue
            try:
                v_ = next(gens[hh_])
                if v_ != "BLOCK":
                    progressed = True
            except StopIteration:
                alive[hh_] = False
                progressed = True
        step += 1
        blocked = 0 if progressed else blocked + 1
        assert blocked < 1000, ("interleave deadlock", wtag, locks)
    S.barrier()

    with nc.Block() as block:
        S.emit(block)
    stack.close()
    return nc


def host_inputs(inputs, layers=(0, 1, 2, 3)):
    f32 = np.float32
    m = {}
    m["c_identf"] = np.eye(128, dtype=f32)
    q = np.arange(128)
    kj = np.arange(128)
    mprev = (kj[:, None] > q[None, :]).astype(f32)
    mcur = (kj[:, None] <= q[None, :]).astype(f32)
    m["c_mask2"] = np.ascontiguousarray(np.stack([mprev, mcur], axis=1))
    o2 = np.zeros((128, 2, 128), f32)
    o2[:, 0, 0:64] = 1.0
    o2[:, 1, 64:128] = 1.0
    m["c_ones2"] = o2
    ii = np.arange(128) // 16
    m["c_tri"] = (ii[None, :] >= ii[:, None]).astype(f32)
    m["c_fidx"] = (np.arange(128) % 32).astype(f32).reshape(128, 1)
    m["c_sgn"] = np.where((np.arange(128) % 64) < 32, -1.0, 1.0).astype(f32).reshape(128, 1)
    m["c_pos"] = np.arange(SEQ, dtype=f32)

    def col(v):
        return np.ascontiguousarray(np.asarray(v, f32).reshape(8, 128).T)

    m["fnorm_col"] = col(inputs["final_norm"])
    for l in layers:
        p = "l%d_" % l
        m[p + "norm_col"] = col(inputs[p + "norm"])
        if l % 2 == 0:
            m[p + "w_in"] = np.ascontiguousarray(inputs[p + "w_in"], f32)
            m[p + "w_glu"] = np.ascontiguousarray(inputs[p + "w_glu"], f32)
            m[p + "w_out"] = np.ascontiguousarray(inputs[p + "w_out"], f32)
            m[p + "bglu_col"] = col(inputs[p + "b_glu"])
            m[p + "aRe"] = np.ascontiguousarray(np.tile(np.asarray(inputs[p + "a_re"], f32).T, (2, 1)))
            m[p + "aIm"] = np.ascontiguousarray(np.tile(np.asarray(inputs[p + "a_im"], f32).T, (2, 1)))
            m[p + "ls"] = np.ascontiguousarray(np.tile(np.asarray(inputs[p + "log_step"], f32)[None, :], (128, 1)))
            m[p + "dcol"] = np.ascontiguousarray(np.tile(np.asarray(inputs[p + "d"], f32).reshape(64, 16).T, (8, 1)))
            m[p + "cRe"] = np.ascontiguousarray(np.tile(np.asarray(inputs[p + "c_re"], f32).transpose(2, 0, 1), (2, 1, 1)))
            m[p + "cIm"] = np.ascontiguousarray(np.tile(np.asarray(inputs[p + "c_im"], f32).transpose(2, 0, 1), (2, 1, 1)))
            m[p + "bRe"] = np.ascontiguousarray(np.tile(np.asarray(inputs[p + "b_re"], f32).transpose(1, 0, 2), (2, 1, 1)))
            m[p + "bIm"] = np.ascontiguousarray(np.tile(np.asarray(inputs[p + "b_im"], f32).transpose(1, 0, 2), (2, 1, 1)))
        else:
            w = np.asarray(inputs[p + "w_in"], f32)
            m[p + "w_in"] = np.ascontiguousarray(w)
            dd = np.arange(64)
            partner = np.where(dd < 32, dd + 32, dd - 32)
            qperm = (np.arange(16)[:, None] * 64 + partner[None, :]).reshape(-1)
            m[p + "w_qp"] = np.ascontiguousarray(w[:, qperm])
            k0 = w[:, 1024:1088]
            k1 = w[:, 1088:1152]
            m[p + "w_kk"] = np.ascontiguousarray(np.concatenate(
                [k0, k0, k1, k1, k0[:, partner], k0[:, partner], k1[:, partner], k1[:, partner]], axis=1))
            m[p + "w_out"] = np.ascontiguousarray(inputs[p + "w_out"], f32)
            s = np.asarray(inputs[p + "sinks"], f32)
            sc = np.zeros((128, 8), f32)
            for kc in range(8):
                sc[0:64, kc] = s[2 * kc]
                sc[64:128, kc] = s[2 * kc + 1]
            m[p + "sink_col"] = sc
    return m


INPUT_NAMES = (
    "x",
    "l0_norm", "l0_w_in", "l0_a_re", "l0_a_im", "l0_log_step", "l0_b_re", "l0_b_im", "l0_c_re", "l0_c_im", "l0_d",
    "l0_w_glu", "l0_b_glu", "l0_w_out",
    "l1_norm", "l1_w_in", "l1_sinks", "l1_w_out",
    "l2_norm", "l2_w_in", "l2_a_re", "l2_a_im", "l2_log_step", "l2_b_re", "l2_b_im", "l2_c_re", "l2_c_im", "l2_d",
    "l2_w_glu", "l2_b_glu", "l2_w_out",
    "l3_norm", "l3_w_in", "l3_sinks", "l3_w_out",
    "final_norm",
)

_NC_CACHE = {}


def kernel(**inputs):
    layers = (0, 1, 2, 3)
    missing = [n for n in INPUT_NAMES if n not in inputs]
    assert not missing, missing
    if "full" not in _NC_CACHE:
        _NC_CACHE["full"] = build(16, layers)
    nc = _NC_CACHE["full"]
    shared = host_inputs(inputs, layers)
    xs = np.asarray(inputs["x"], np.float32)
    in_maps = []
    for c in range(8):
        mc = dict(shared)
        mc["x"] = np.ascontiguousarray(xs[c])
        in_maps.append(mc)
    res = run_bass_kernel_spmd(nc, in_maps, core_ids=list(range(8)))
    return np.stack([np.asarray(r["out"], np.float32) for r in res.results], axis=0)
```

```python
import math
import os
from contextlib import ExitStack

import numpy as np
import concourse.bass as bass
import concourse.mybir as mybir
from concourse.bass_utils import run_bass_kernel_spmd

F32 = mybir.dt.float32
BF16 = mybir.dt.bfloat16
I32 = mybir.dt.int32
AF = mybir.ActivationFunctionType
ALU = mybir.AluOpType

T = 512
NCH = 64
NH = 32
SEQ = 8192
DM = 1024
EPS = 1e-5
TWO_PI = 2.0 * math.pi
SIN_SCALE = 6.28318


class Sched:
    CE = ("pe", "act", "dve", "pool")

    def __init__(self, nc, stack):
        self.nc = nc
        self.stack = stack
        self.q = {e: [] for e in ("pe", "act", "dve", "pool", "sp")}
        self.esem = {e: [stack.enter_context(nc.semaphore("c_%s_%d" % (e, i))) for i in range(4)] for e in self.CE}
        self.ecnt = {(e, i): 0 for e in self.CE for i in range(4)}
        self.epoch = 0
        self.last_w = {}
        self.readers = {}
        self.known = {e: {} for e in self.q}
        self.dsem = {}
        self.dcnt = {}
        self.semkey_of = {}
        self.all_tokens = {}

    def _deps(self, eng, R, W):
        deps = []
        for k in R:
            t = self.last_w.get(k)
            if t is not None:
                deps.append(t)
        for k in W:
            t = self.last_w.get(k)
            if t is not None:
                deps.append(t)
            deps.extend(self.readers.get(k, ()))
        waits = {}
        for (sem, val, e2) in deps:
            if e2 == eng and eng == "pe":
                continue
            if e2 == "dma":
                val = self.dcnt[self.semkey_of[sem]]
            if self.known[eng].get(sem, 0) >= val:
                continue
            if waits.get(sem, 0) < val:
                waits[sem] = val
        for sem, val in waits.items():
            self.q[eng].append(lambda e, s=sem, v=val: e.wait_ge(s, v))
            self.known[eng][sem] = val

    def _record(self, tok, R, W):
        for k in R:
            self.readers.setdefault(k, []).append(tok)
        for k in W:
            self.last_w[k] = tok
            self.readers[k] = []
        self.all_tokens[tok[0]] = max(self.all_tokens.get(tok[0], 0), tok[1])

    def op(self, eng, fn, R=(), W=()):
        self._deps(eng, R, W)
        i = self.epoch % 4
        sem = self.esem[eng][i]
        self.ecnt[(eng, i)] += 1
        val = self.ecnt[(eng, i)]
        self.q[eng].append(lambda e, f=fn, s=sem: f(e).then_inc(s, 1))
        self._record((sem, val, eng), R, W)

    def dma(self, queue, out, in_, R=(), W=(), semkey=None):
        self._deps(queue, R, W)
        if semkey not in self.dsem:
            self.dsem[semkey] = self.stack.enter_context(self.nc.semaphore("d_%s" % semkey))
            self.dcnt[semkey] = 0
            self.semkey_of[self.dsem[semkey]] = semkey
        sem = self.dsem[semkey]
        self.dcnt[semkey] += 16
        val = self.dcnt[semkey]
        self.q[queue].append(lambda e, o=out, i=in_, s=sem: e.dma_start(out=o, in_=i).then_inc(s, 16))
        self._record((sem, val, "dma"), R, W)

    def barrier(self, skip=()):
        for eng in self.q:
            for sem, val in self.all_tokens.items():
                if self.semkey_of.get(sem) in skip:
                    continue
                if self.known[eng].get(sem, 0) < val:
                    self.q[eng].append(lambda e, s=sem, v=val: e.wait_ge(s, v))
                    self.known[eng][sem] = val

    def emit(self, block):
        q = self.q

        @block.tensor
        def _(e):
            for f in q["pe"]:
                f(e)

        @block.scalar
        def _(e):
            for f in q["act"]:
                f(e)

        @block.vector
        def _(e):
            for f in q["dve"]:
                f(e)

        @block.gpsimd
        def _(e):
            for f in q["pool"]:
                f(e)

        @block.sync
        def _(e):
            for f in q["sp"]:
                f(e)


SSM_HOST = ("norm", "bglu", "aRe", "aIm", "ls", "cRe", "cIm", "bRe", "bIm", "dcol")


def build(n_tiles=16, layers=(0, 1, 2, 3)):
    nc = bass.Bass("TRN2", target_bir_lowering=False)
    stack = ExitStack()
    S = Sched(nc, stack)

    def din(name, shape, dt=F32):
        return nc.dram_tensor(name, list(shape), dt, kind="ExternalInput").ap()

    def dscr(name, shape, dt):
        return nc.dram_tensor(name, list(shape), dt, kind="Internal").ap()

    x = din("x", [SEQ, DM])
    out = nc.dram_tensor("out", [SEQ, DM], F32, kind="ExternalOutput").ap()
    c_identf = din("c_identf", [128, 128])
    c_mask2 = din("c_mask2", [128, 2, 128])
    c_ones2 = din("c_ones2", [128, 2, 128])
    c_tri = din("c_tri", [128, 128])
    c_fidx = din("c_fidx", [128, 1])
    c_sgn = din("c_sgn", [128, 1])
    c_pos = din("c_pos", [SEQ])
    fnorm = din("fnorm_col", [128, 8])
    L = {}
    for l in layers:
        p = "l%d_" % l
        d = {}
        d["norm"] = din(p + "norm_col", [128, 8])
        if l % 2 == 0:
            d["w_in"] = din(p + "w_in", [DM, 2048])
            d["w_glu"] = din(p + "w_glu", [DM, DM])
            d["w_out"] = din(p + "w_out", [DM, DM])
            d["bglu"] = din(p + "bglu_col", [128, 8])
            for nm in ("aRe", "aIm", "ls", "dcol"):
                d[nm] = din(p + nm, [128, 64])
            for nm in ("cRe", "cIm", "bRe", "bIm"):
                d[nm] = din(p + nm, [128, 64, 16])
            d["w_in_b"] = dscr(p + "w_in_b", [DM, 2048], BF16)
            d["w_glu_b"] = dscr(p + "w_glu_b", [DM, DM], BF16)
            d["w_out_b"] = dscr(p + "w_out_b", [DM, DM], BF16)
            d["WS3"] = dscr(p + "WS3", [128, 64, 192], BF16)
            d["WHK"] = dscr(p + "WHK", [128, 2, 64, 128], BF16)
        else:
            d["w_in"] = din(p + "w_in", [DM, 2304])
            d["w_qp"] = din(p + "w_qp", [DM, DM])
            d["w_kk"] = din(p + "w_kk", [DM, 512])
            d["w_out"] = din(p + "w_out", [DM, DM])
            d["sink"] = din(p + "sink_col", [128, 8])
            d["w_in_b"] = dscr(p + "w_in_b", [DM, 2304], BF16)
            d["w_qp_b"] = dscr(p + "w_qp_b", [DM, DM], BF16)
            d["w_kk_b"] = dscr(p + "w_kk_b", [DM, 512], BF16)
            d["w_out_b"] = dscr(p + "w_out_b", [DM, DM], BF16)
        L[l] = d
    has_att = any(l % 2 == 1 for l in layers)
    ropeC = dscr("ropeC", [128, SEQ], F32)
    ropeS = dscr("ropeS", [128, SEQ], F32)

    ARENA_W = 53000
    arena = stack.enter_context(nc.sbuf_tensor("arena", [128, ARENA_W], F32))
    cur = [0]

    def carve(shape, dt, at=None):
        n = int(np.prod(shape[1:]))
        words = n if dt in (F32, I32) else (n + 1) // 2
        if at is None:
            off = cur[0]
            cur[0] += words
            assert cur[0] <= ARENA_W, ("sbuf overflow", cur[0])
        else:
            off = at
            assert off + words <= ARENA_W
        v = arena[:, off:off + words]
        if dt != F32:
            v = v.bitcast(dt)
        if len(shape) > 2:
            names = " ".join("d%d" % i for i in range(1, len(shape)))
            kw = {"d%d" % i: shape[i] for i in range(1, len(shape))}
            v = v.rearrange("p (%s) -> p %s" % (names, names), **kw)
        return v

    identf = carve([128, 128], F32)
    identb = carve([128, 128], BF16)
    onesb = carve([128, 128], BF16)
    mask2 = carve([128, 2, 128], BF16)
    ones2 = carve([128, 2, 128], BF16)
    stage_c = carve([128, 2, 128], F32)
    normc = carve([128, 5, 8], F32)
    bgluc = carve([128, 2, 8], F32)
    sinke = carve([128, 2, 8], F32)
    CAB = carve([128, 2, 2, 2, 64], F32)
    HHc = carve([128, 2, 2, 64], F32)
    pro0 = cur[0]
    scr = carve([128, 2, 3, T // 2], F32)

    def RSTD(h):
        return scr[:, h, 0, :]

    def T1(h):
        return scr[:, h, 1, :]

    def T2(h):
        return scr[:, h, 2, :]

    def T12(h):
        return scr[:, h, 1:3, :].rearrange("p a b -> p (a b)")
    sigb = carve([128, 2, T], BF16)
    cosT = carve([128, T], F32)
    sinT = carve([128, T], F32)
    KT = carve([128, 2, 640], BF16)
    Vpad = carve([128, 5, 2, 2, 128], BF16)
    KTc = carve([128, 2, 2, 128], BF16)
    Vc = carve([128, 2, 2, 2, 128], BF16)
    xB = carve([128, 8, T], F32)
    hB = carve([128, 8, T], BF16)
    gB = carve([128, 8, T], BF16)
    zB = carve([128, 8, T], BF16)
    main0 = cur[0]
    tokA_off = cur[0]
    tokA = carve([128, 8, DM], BF16)
    XC = carve([128, 64, 64], BF16)
    HP_off = cur[0]
    HP = carve([128, 64, 64], BF16)
    xtok = carve([128, 2, DM], F32, at=HP_off)
    SS = carve([128, NH, 2, 64], F32)
    s5w = carve([128, 64 * 256], BF16)
    wsl = [carve([128, 8, DM], BF16) for _ in range(3)]
    PT = tokA.rearrange("p a b -> p (a b)")[:, 0:8 * T].rearrange("p (s t) -> p s t", s=8)
    yB = SS.rearrange("p a b c -> p (a b c)").rearrange("p (k t) -> p k t", k=8)
    WS3 = s5w[:, 0:64 * 192].rearrange("p (g r) -> p g r", g=64)
    WHK = s5w.rearrange("p (s g r) -> p s g r", s=2, g=64)
    kvw = s5w[:, 0:8 * 640].rearrange("p (k n) -> p k n", k=8)

    psf = [stack.enter_context(nc.psum_tensor("psf%d" % i, [128, 512], F32)) for i in range(6)]
    psb = [stack.enter_context(nc.psum_tensor("psb%d" % i, [128, 1024], BF16)) for i in range(2)]
    rr = {"f": 0, "b": 0, "e": 0}

    def bankf():
        i = rr["f"] % 6
        rr["f"] += 1
        return psf[i], ("psf", i)

    def bankb():
        i = rr["b"] % 2
        rr["b"] += 1
        return psb[i], ("psb", i)

    def ev_eng():
        rr["e"] += 1
        return "act" if rr["e"] % 2 else "dve"

    def copy(eng, o, i, R, W):
        if eng != "act" and o.dtype == i.dtype:
            S.op(eng, lambda e: e.tensor_single_scalar(out=o, in_=i, scalar=1.0, op=ALU.mult), R, W)
            return
        if eng == "act":
            S.op("act", lambda e: e.activation(out=o, in_=i, func=AF.Copy), R, W)
        else:
            S.op(eng, lambda e: e.tensor_copy(out=o, in_=i), R, W)

    def tt(eng, o, a, b, op, R, W):
        S.op(eng, lambda e: e.tensor_tensor(out=o, in0=a, in1=b, op=op), R, W)

    def ts(eng, o, a, s1, op, R, W):
        S.op(eng, lambda e: e.tensor_single_scalar(out=o, in_=a, scalar=s1, op=op), R, W)

    def actf(o, i, func, R, W, scale=1.0, bias=0.0):
        S.op("act", lambda e: e.activation(out=o, in_=i, func=func, scale=scale, bias=bias), R, W)

    def mm(o, lhsT, rhs, start, stop, R, W):
        S.op("pe", lambda e: e.matmul(o, lhsT=lhsT, rhs=rhs, start=start, stop=stop), R, W)

    def tr(o, i, ident, R, W):
        S.op("pe", lambda e: e.transpose(o, i, ident), R, W)

    S.dma("sp", identf, c_identf, W=["identf"], semkey="c0")
    copy("dve", identb, identf, ["identf"], ["identb"])
    S.op("dve", lambda e: e.memset(onesb, 1.0), (), ["onesb"])
    S.dma("sp", stage_c, c_mask2, W=["stage_c"], semkey="c1")
    S.op("dve", lambda e: e.tensor_scalar(out=mask2, in0=stage_c, scalar1=-1.0, scalar2=30000.0, op0=ALU.add, op1=ALU.mult),
         ["stage_c"], ["mask2"])
    S.dma("sp", stage_c, c_ones2, R=(), W=["stage_c"], semkey="c1")
    copy("dve", ones2, stage_c, ["stage_c"], ["ones2"])
    S.op("dve", lambda e: e.memset(HHc, 0.0), (), ["HHc"])
    for li, l in enumerate(layers):
        S.dma("sp", normc[:, li, :], L[l]["norm"], W=[("normc", li)], semkey="c2")
    S.dma("sp", normc[:, 4, :], fnorm, W=[("normc", 4)], semkey="c2")

    for l in layers:
        d = L[l]
        names = ("w_in", "w_glu", "w_out") if l % 2 == 0 else ("w_in", "w_qp", "w_kk", "w_out")
        for nm in names:
            for rb in range(8):
                S.dma("pool", d[nm + "_b"][rb * 128:(rb + 1) * 128, :], d[nm][rb * 128:(rb + 1) * 128, :], W=[("wscr", l, nm, rb)], semkey="wc_%d_%s" % (l, nm))

    if has_att:
        pw = 2048
        base = pro0
        posb = carve([128, pw], F32, at=base)
        tq = carve([128, pw], F32, at=base + pw)
        tqi = carve([128, pw], I32, at=base + 2 * pw)
        tqf = carve([128, pw], F32, at=base + 3 * pw)
        res = carve([128, pw], F32, at=base + 4 * pw)
        fidx = carve([128, 1], F32, at=base + 5 * pw)
        invf = carve([128, 1], F32, at=base + 5 * pw + 1)
        sgn = carve([128, 1], F32, at=base + 5 * pw + 2)
        S.dma("sp", fidx, c_fidx, W=["pp"], semkey="c3")
        S.dma("sp", sgn, c_sgn, W=["pp"], semkey="c3")
        actf(invf, fidx, AF.Exp, ["pp"], ["pp"], scale=-math.log(10000.0) / 32.0)
        for ch in range(SEQ // pw):
            S.dma("sp", posb, c_pos[ch * pw:(ch + 1) * pw].partition_broadcast(128), W=["pp"], semkey="c4")
            S.op("dve", lambda e: e.tensor_scalar_mul(out=posb, in0=posb, scalar1=invf[:, 0:1]), ["pp", "pp"], ["pp"])
            ts("dve", posb, posb, 1.0 / TWO_PI, ALU.mult, ["pp"], ["pp"])
            for which, shift, dst in (("s", 0.0, ropeS), ("c", 0.25, ropeC)):
                ts("dve", tq, posb, shift, ALU.add, ["pp"], ["pp"])
                copy("dve", tqi, tq, ["pp"], ["pp"])
                copy("dve", tqf, tqi, ["pp"], ["pp"])
                tt("dve", tq, tq, tqf, ALU.subtract, ["pp", "pp"], ["pp"])
                actf(res, tq, AF.Sin, ["pp"], ["pp"], scale=SIN_SCALE)
                if which == "s":
                    S.op("dve", lambda e: e.tensor_scalar_mul(out=res, in0=res, scalar1=sgn[:, 0:1]), ["pp", "pp"], ["pp"])
                S.dma("sp", dst[:, ch * pw:(ch + 1) * pw], res, R=["pp"], W=[("rope", which)], semkey="c5")
        for li, l in enumerate(layers):
            if l % 2 == 1:
                si = (l // 2)
                S.dma("sp", sinke[:, si, :], L[l]["sink"], W=[("sinke", si)], semkey="c6_%d" % si)
                actf(sinke[:, si, :], sinke[:, si, :], AF.Exp, [("sinke", si)], [("sinke", si)])

    for l in layers:
        if l % 2:
            continue
        d = L[l]
        si = l // 2
        o = [pro0]

        def pc(shape, dt=F32):
            n = int(np.prod(shape[1:]))
            w = n if dt in (F32, I32) else (n + 1) // 2
            v = carve(shape, dt, at=o[0])
            o[0] += w
            return v
        NK = 17
        aRe = pc([128, 64]); aIm = pc([128, 64]); ls = pc([128, 64]); dcol = pc([128, 64])
        cRe = pc([128, 64, 16]); cIm = pc([128, 64, 16]); bRe = pc([128, 64, 16]); bIm = pc([128, 64, 16])
        lr = pc([128, 64]); lim = pc([128, 64])
        Er = pc([128, NK, 64]); Ei = pc([128, NK, 64]); Fr = pc([128, NK, 64]); Fi = pc([128, NK, 64])
        cr = pc([128, 64]); ci = pc([128, 64]); w1 = pc([128, 64]); w2 = pc([128, 64])
        WST = pc([128, 64, 8, 16]); WS2T = pc([128, 64, 8, 16], BF16); WHf = pc([128, 64, 8, 16])
        tri = pc([128, 128]); k0t = pc([128, 128])
        U1 = o[0]
        u1 = pc([128, 64, 8, 16])
        U2 = o[0]
        u2 = pc([128, 64, 8, 16])
        o2 = [U1]
        def pc2(shape, dt=F32):
            n = int(np.prod(shape[1:]))
            v = carve(shape, dt, at=o2[0])
            o2[0] += n
            return v
        ex = pc2([128, NK, 64]); tq_ = pc2([128, NK, 64]); t2_ = pc2([128, NK, 64]); ti_ = pc2([128, NK, 64], I32)
        tf_ = pc2([128, NK, 64]); sn = pc2([128, NK, 64]); cs = pc2([128, NK, 64])
        assert o2[0] <= U2
        WS3s = carve([128, 64, 192], BF16, at=U1)
        WHKs = carve([128, 2, 64, 128], BF16, at=U2)
        assert o[0] <= ARENA_W, o[0]
        k = "pp"
        for nm, tl in (("aRe", aRe), ("aIm", aIm), ("ls", ls), ("dcol", dcol), ("cRe", cRe), ("cIm", cIm), ("bRe", bRe), ("bIm", bIm)):
            S.dma("sp", tl, d[nm], W=[k + nm], semkey="pp")
        S.dma("sp", tri, c_tri, W=[k + "tri"], semkey="pp")
        S.dma("sp", bgluc[:, si, :], d["bglu"], W=[("bgluc", si)], semkey="pp")
        lk = [k] + [k + nm for nm in ("aRe", "aIm", "ls", "dcol", "cRe", "cIm", "bRe", "bIm", "tri")]
        actf(ls, ls, AF.Exp, lk, [k, "WS3s", "WHKs"])
        tt("dve", lr, aRe, ls, ALU.mult, [k], [k])
        tt("dve", lim, aIm, ls, ALU.mult, [k], [k])
        for kk in range(-8, 9):
            idx = kk + 8
            actf(ex[:, idx, :], lr, AF.Exp, [k], [k], scale=float(kk))
            ts("dve", tq_[:, idx, :], lim, kk / TWO_PI, ALU.mult, [k], [k])
        for dst, shift in ((sn, 0.0), (cs, 0.25)):
            ts("dve", t2_, tq_, shift, ALU.add, [k], [k])
            copy("dve", ti_, t2_, [k], [k])
            copy("dve", tf_, ti_, [k], [k])
            tt("dve", t2_, t2_, tf_, ALU.subtract, [k], [k])
            actf(dst, t2_, AF.Sin, [k], [k], scale=SIN_SCALE)
        tt("dve", Er, ex, cs, ALU.mult, [k], [k])
        tt("dve", Ei, ex, sn, ALU.mult, [k], [k])
        E1r = Er[:, 9, :]; E1i = Ei[:, 9, :]
        ts("dve", w1, E1r, -1.0, ALU.add, [k], [k])
        tt("dve", cr, w1, aRe, ALU.mult, [k], [k])
        tt("dve", w2, E1i, aIm, ALU.mult, [k], [k])
        tt("dve", cr, cr, w2, ALU.add, [k], [k])
        tt("dve", ci, E1i, aRe, ALU.mult, [k], [k])
        tt("dve", w2, w1, aIm, ALU.mult, [k], [k])
        tt("dve", ci, ci, w2, ALU.subtract, [k], [k])
        tt("dve", w1, aRe, aRe, ALU.mult, [k], [k])
        tt("dve", w2, aIm, aIm, ALU.mult, [k], [k])
        tt("dve", w1, w1, w2, ALU.add, [k], [k])
        S.op("dve", lambda e: e.reciprocal(out=w1, in_=w1), [k], [k])
        tt("dve", cr, cr, w1, ALU.mult, [k], [k])
        tt("dve", ci, ci, w1, ALU.mult, [k], [k])
        crb = cr.unsqueeze(1).broadcast_to([128, NK, 64])
        cib = ci.unsqueeze(1).broadcast_to([128, NK, 64])
        tt("dve", Fr, Er, crb, ALU.mult, [k], [k])
        tt("dve", t2_, Ei, cib, ALU.mult, [k], [k])
        tt("dve", Fr, Fr, t2_, ALU.subtract, [k], [k])
        tt("dve", Fi, Er, cib, ALU.mult, [k], [k])
        tt("dve", t2_, Ei, crb, ALU.mult, [k], [k])
        tt("dve", Fi, Fi, t2_, ALU.add, [k], [k])

        def kview(v, lo, hi, k0, rev):
            vv = v[lo:hi, :, :].rearrange("p k g -> p g k")
            vv = vv[:, :, k0 - 7:k0 + 1][:, :, ::-1] if rev else vv[:, :, k0:k0 + 8]
            return vv.unsqueeze(3).broadcast_to([hi - lo, 64, 8, 16])

        def bview(v, lo, hi):
            return v[lo:hi].unsqueeze(2).broadcast_to([hi - lo, 64, 8, 16])

        for dstT, k0 in ((WST, 15), (WS2T, 7)):
            tt("dve", u1[0:64], bview(bRe, 0, 64), kview(Fr, 0, 64, k0, True), ALU.mult, [k], [k])
            tt("dve", u2[0:64], bview(bIm, 0, 64), kview(Fi, 0, 64, k0, True), ALU.mult, [k], [k])
            tt("dve", dstT[0:64], u1[0:64], u2[0:64], ALU.subtract, [k], [k])
            tt("dve", u1[64:128], bview(bIm, 64, 128), kview(Fr, 64, 128, k0, True), ALU.mult, [k], [k])
            tt("dve", u2[64:128], bview(bRe, 64, 128), kview(Fi, 64, 128, k0, True), ALU.mult, [k], [k])
            tt("dve", dstT[64:128], u1[64:128], u2[64:128], ALU.add, [k], [k])
        tt("dve", u1[0:64], bview(cRe, 0, 64), kview(Er, 0, 64, 9, False), ALU.mult, [k], [k])
        tt("dve", u2[0:64], bview(cIm, 0, 64), kview(Ei, 0, 64, 9, False), ALU.mult, [k], [k])
        tt("dve", WHf[0:64], u1[0:64], u2[0:64], ALU.subtract, [k], [k])
        tt("dve", u1[64:128], bview(cRe, 64, 128), kview(Ei, 64, 128, 9, False), ALU.mult, [k], [k])
        tt("dve", u2[64:128], bview(cIm, 64, 128), kview(Er, 64, 128, 9, False), ALU.mult, [k], [k])
        tt("dve", u1[64:128], u1[64:128], u2[64:128], ALU.add, [k], [k])
        ts("dve", WHf[64:128], u1[64:128], -1.0, ALU.mult, [k], [k])
        for s2 in range(2):
            copy("dve", CAB[:, si, 0, s2, :], Er[:, 16, :], [k], [("CAB", si)])
        ts("dve", CAB[0:64, si, 1, 0, :], Ei[0:64, 16, :], -1.0, ALU.mult, [k], [("CAB", si)])
        copy("dve", CAB[64:128, si, 1, 0, :], Ei[64:128, 16, :], [k], [("CAB", si)])
        ts("dve", CAB[:, si, 1, 1, :], CAB[:, si, 1, 0, :], -1.0, ALU.mult, [("CAB", si)], [("CAB", si)])
        copy("act", WHKs[:, 0, :, :], WHf.rearrange("p g j c -> p g (j c)"), [k], ["WHKs"])
        for g in range(64):
            pb, pk = bankf()
            pb2, pk2 = bankf()
            tr(pb[:, 0:128], WST[:, g, :, :].rearrange("p j c -> p (j c)"), identf, [k, "identf"], [pk])
            mm(pb2[:, 0:128], WS2T[:, g, :, :].rearrange("p j c -> p (j c)"),
               WHKs[:, 0, g, :], True, True, [k, "WHKs"], [pk2])
            copy("act", WS3s[:, g, 0:128], pb[:, 0:128], [pk], ["WS3s"])
            copy("act", WS3s[:, g, 128:192], pb[:, 0:64], [pk], ["WS3s"])
            tt("dve", k0t, pb2[:, 0:128], tri, ALU.mult, [pk2, k], ["k0t"])
            S.op("dve", lambda e, g=g: e.scalar_tensor_tensor(out=WHKs[:, 1, g, :], in0=identf, scalar=dcol[:, g:g + 1],
                                                               in1=k0t, op0=ALU.mult, op1=ALU.add),
                 ["k0t", k, "identf"], ["WHKs"])
        S.dma("sp", d["WS3"], WS3s, R=["WS3s"], W=[("s5scr", l)], semkey="pst")
        S.dma("sp", d["WHK"], WHKs, R=["WHKs"], W=[("s5scr", l)], semkey="pst")
    S.barrier(skip=[kk_ for kk_ in S.dsem if kk_.startswith("wc_")])
    S.op("pool", lambda e: e.memset(Vpad, 0.0), (), ["Vpad"])
    S.op("pool", lambda e: e.memset(KT, 0.0), (), ["KT"])
    S.op("pool", lambda e: e.memset(KTc, 0.0), (), ["KTc"])
    S.op("pool", lambda e: e.memset(Vc, 0.0), (), ["Vc"])

    HT = T // 2
    NHC = NH

    def tk(h):
        return slice(h * HT, (h + 1) * HT)

    def load_w(slot, src, cols, l, nm):
        ncol = cols[1] - cols[0]
        S.dma("sp", wsl[slot][:, :, 0:ncol], src[:, cols[0]:cols[1]].rearrange("(k p) n -> p k n", p=128),
              R=[("wscr", l, nm, rb) for rb in range(8)], W=[("wsl", slot)], semkey="wsl%d" % slot)

    def load_s5w(l, which):
        d = L[l]
        if which == "WS3":
            S.dma("sp", WS3, d["WS3"], R=[("s5scr", l)], W=["s5w"], semkey="s5w")
        elif which == "WHK":
            S.dma("sp", WHK, d["WHK"], R=[("s5scr", l)], W=["s5w"], semkey="s5w")
        else:
            S.dma("sp", kvw[:, :, 0:512], d["w_kk_b"].rearrange("(k p) n -> p k n", p=128),
                  R=[("wscr", l, "w_kk", rb) for rb in range(8)], W=["s5w"], semkey="s5w")
            S.dma("sp", kvw[:, :, 512:640], d["w_in_b"][:, 1152:1280].rearrange("(k p) n -> p k n", p=128),
                  R=[("wscr", l, "w_in", rb) for rb in range(8)], W=["s5w"], semkey="s5w")

    def layer_loads(l, which):
        d = L[l]
        if l % 2 == 0:
            if which == "slot0":
                load_w(0, d["w_in_b"], (0, 1024), l, "w_in")
            elif which == "slot1":
                load_w(1, d["w_in_b"], (1024, 2048), l, "w_in")
            elif which == "slot2":
                load_w(2, d["w_glu_b"], (0, 1024), l, "w_glu")
            else:
                load_s5w(l, "WS3")
        else:
            if which == "slot0":
                load_w(0, d["w_in_b"], (0, 1024), l, "w_in")
            elif which == "slot1":
                load_w(1, d["w_qp_b"], (0, 1024), l, "w_qp")
            elif which == "slot2":
                load_w(2, d["w_in_b"], (1280, 2304), l, "w_in")
            else:
                load_s5w(l, "KV")

    def rmsnorm(h):
        t = tk(h)
        actf(zB[:, :, t], xB[:, :, t], AF.Square, [("xB", h)], [("zB", h)])
        pb, pk = bankf()
        for kc in range(8):
            mm(pb[:, 0:HT], onesb, zB[:, kc, t], kc == 0, kc == 7, ["onesb", ("zB", h)], [pk])
        actf(T2(h), pb[:, 0:HT], AF.Ln, [pk], [("t2", h)], scale=1.0 / DM, bias=EPS)
        actf(RSTD(h), T2(h), AF.Exp, [("t2", h)], [("rstd", h)], scale=-0.5)

    def norm_apply(h, li, dst, dkey):
        t = tk(h)
        for kc in range(8):
            S.op("dve", lambda e, kc=kc: e.scalar_tensor_tensor(out=dst[:, kc, :], in0=xB[:, kc, t],
                                                                 scalar=normc[:, li, kc:kc + 1], in1=RSTD(h),
                                                                 op0=ALU.mult, op1=ALU.mult),
                 [("xB", h), ("rstd", h), ("normc", li)], [dkey])

    def mm_fm(h, slot, act, akey, mts, evac):
        t = tk(h)
        for mt in mts:
            if mt % 2:
                continue
            pb, pk = bankf()
            for i in range(2):
                for kc in range(8):
                    mm(pb[:, i * HT:(i + 1) * HT], wsl[slot][:, kc, (mt + i) * 128:(mt + i + 1) * 128], act[:, kc, t], kc == 0, kc == 7,
                       [("wsl", slot), akey], [pk])
            evac(mt, pb, pk)

    def out_proj(h, slot, mts=8):
        t = tk(h)

        def ev(mt, pb, pk):
            tt("dve", xB[:, mt:mt + 2, t], xB[:, mt:mt + 2, t], pb[:, :].rearrange("p (a b) -> p a b", a=2), ALU.add,
               [pk, ("xB", h)], [("xB", h)])
        mm_fm(h, slot, gB, ("gB", h), mts, ev)

    NL = len(layers)
    wtag = {}
    locks = {}
    udone = {}

    def roles(ti, li):
        gl = ti * NL + li
        X, Z = (0, 2) if gl % 2 == 0 else (2, 0)
        return X, 1, Z

    def load_role(ti, li, role):
        l = layers[li]
        d = L[l]
        X, Y, Z = roles(ti, li)
        if l % 2 == 0:
            spec = {"E1": (Y, d["w_in_b"], (1024, 2048), "w_in"), "E2": (X, d["w_in_b"], (0, 1024), "w_in"),
                    "M": (Z, d["w_glu_b"], (0, 1024), "w_glu"), "OUT": (X, d["w_out_b"], (0, 1024), "w_out")}[role]
        else:
            spec = {"E1": (Y, d["w_qp_b"], (0, 1024), "w_qp"), "E2": (X, d["w_in_b"], (0, 1024), "w_in"),
                    "M": (Z, d["w_in_b"], (1280, 2304), "w_in"), "OUT": (X, d["w_out_b"], (0, 1024), "w_out")}[role]
        slot, src, cols, nm = spec
        load_w(slot, src, cols, l, nm)
        wtag[slot] = (ti, li, role)

    def use(ti, li, role):
        X, Y, Z = roles(ti, li)
        slot = {"E1": Y, "E2": X, "M": Z, "OUT": X}[role]
        assert wtag.get(slot) == (ti, li, role), ("weight slot content mismatch", slot, wtag.get(slot), (ti, li, role))
        return slot

    def nxt_pos(ti, li):
        if li + 1 < NL:
            return (ti, li + 1)
        if ti + 1 < n_tiles:
            return (ti + 1, 0)
        return None

    def s5w_load(ti, li, which):
        load_s5w(layers[li], which)
        wtag["s5w"] = (ti, li, which)

    def s5w_use(ti, li, which):
        assert wtag.get("s5w") == (ti, li, which), ("s5w content mismatch", wtag.get("s5w"), (ti, li, which))

    def acquire(name, me):
        while locks.get(name) not in (None, me):
            yield "BLOCK"
        locks[name] = me

    def release(name, me):
        assert locks.get(name) == me
        locks[name] = None

    def nxt(ti, li):
        if li + 1 < len(layers):
            return layers[li + 1]
        if ti + 1 < n_tiles:
            return layers[0]
        return None

    def ssm_gen(ti, h, li, l):
        d = L[l]
        si = l // 2
        t = tk(h)
        n0 = h * NHC
        r0, r1 = h * 32, h * 32 + 32
        me = (ti, li, h)
        np_ = nxt_pos(ti, li)
        rmsnorm(h)
        norm_apply(h, li, hB[:, :, t], ("hB", h))
        yield
        sl = use(ti, li, "E1")
        ev_g = lambda mt, pb, pk: actf(gB[:, mt:mt + 2, t], pb[:, :].rearrange("p (a b) -> p a b", a=2), AF.Silu, [pk], [("gB", h)])
        mm_fm(h, sl, hB, ("hB", h), range(0, 4), ev_g)
        yield
        mm_fm(h, sl, hB, ("hB", h), range(4, 8), ev_g)
        if h == 1 and np_ is not None:
            load_role(np_[0], np_[1], "E1")
        yield
        uA = tokA.rearrange("p a b -> p (a b)").rearrange("p (g j c) -> p g j c", g=64, j=8, c=16)
        if h == 0:
            while not udone.get((ti, li)):
                yield "BLOCK"
        else:
            sl = use(ti, li, "E2")
            for j in range(8):
                for nt in range(2):
                    pb, pk = bankf()
                    for kc in range(8):
                        mm(pb[0:64, :], hB[:, kc, j:T:8], wsl[sl][:, kc, nt * 512:(nt + 1) * 512], kc == 0, kc == 7,
                           [("hB", 0), ("hB", 1), ("wsl", sl)], [pk])
                    copy(ev_eng(), uA[0:64, nt * 32:(nt + 1) * 32, j, :],
                         pb[0:64, :].rearrange("p (g c) -> p g c", g=32), [pk], [("tokA", 0), ("tokA", 1)])
                if j % 2 == 1 and j < 7:
                    yield
            udone[(ti, li)] = True
            load_role(ti, li, "OUT")
        yield
        for gq in range(2):
            pb, pk = bankb()
            for g32 in range(32):
                g = gq * 32 + g32
                tr(pb[:, g32 * 32:(g32 + 1) * 32], uA[r0:r1, g, :, :].rearrange("p j c -> p (j c)"), identb[r0:r1, r0:r1],
                   [("tokA", h), "identb"], [pk])
            copy(ev_eng(), XC[:, gq * 32:(gq + 1) * 32, n0:n0 + NHC], pb[:, :].rearrange("p (g n) -> p g n", g=32), [pk], [("XC", h)])
        yield
        if h == 1:
            while not udone.get((ti, li, "s5")):
                yield "BLOCK"
        for _ in acquire("SS", me):
            yield _
        s5w_use(ti, li, "WS3")
        for gq in range(8):
            pb, pk = bankf()
            for g8 in range(8):
                g = gq * 8 + g8
                for s_ in range(2):
                    c0 = (g8 * 2 + s_) * NHC
                    mm(pb[:, c0:c0 + NHC], WS3[:, g, s_ * 64:s_ * 64 + 128], XC[:, g, n0:n0 + NHC], True, True,
                       ["s5w", ("XC", h)], [pk])
            copy(ev_eng(), SS[:, :, :, gq * 8:(gq + 1) * 8],
                 pb[:, :].rearrange("p (g s n) -> p n s g", g=8, s=2), [pk], ["SS"])
        if h == 1:
            s5w_load(ti, li, "WHK")
        else:
            udone[(ti, li, "s5")] = True
        yield
        for _ in acquire("HP", me):
            yield _
        HPv = HP.rearrange("p g n -> p n g")
        t1 = T1(h)[:, 0:128].rearrange("p (s g) -> p s g", s=2)
        t2 = T2(h)[:, 0:128].rearrange("p (s g) -> p s g", s=2)
        CA = CAB[:, si, 0, :, :]
        CB = CAB[:, si, 1, :, :]
        kq = [("CAB", si)]
        copy("act", HPv[:, n0, :], HHc[:, si, 0, :], ["HHc"], ["HP"])
        for n in range(NHC):
            prev = HHc[:, si, :, :] if n == 0 else SS[:, n - 1, :, :]
            pk_ = "HHc" if n == 0 else "SS"
            tt("dve", t1, prev, CA, ALU.mult, [pk_] + kq, [("t1", h)])
            tt("dve", t2, prev[:, ::-1, :], CB, ALU.mult, [pk_] + kq, [("t2", h)])
            tt("dve", t1, t1, t2, ALU.add, [("t1", h), ("t2", h)], [("t1", h)])
            tt("dve", SS[:, n, :, :], SS[:, n, :, :], t1, ALU.add, [("t1", h), "SS"], ["SS"])
            if n % 4 == 3 and n < NHC - 1:
                yield
        copy("act", HPv[:, n0 + 1:n0 + NHC, :], SS[:, 0:NHC - 1, 0, :], ["SS"], ["HP"])
        copy("pool", HHc[:, si, :, :], SS[:, NHC - 1, :, :], ["SS"], ["HHc"])
        release("SS", me)
        yield
        while wtag.get("s5w") != (ti, li, "WHK"):
            yield "BLOCK"
        zA = tokA
        for gq in range(16):
            pb, pk = bankf()
            for g4 in range(4):
                g = gq * 4 + g4
                mm(pb[0:32, g4 * 128:(g4 + 1) * 128], HP[:, g, n0:n0 + NHC], WHK[:, 0, g, :], True, False, ["HP", "s5w"], [pk])
                mm(pb[0:32, g4 * 128:(g4 + 1) * 128], XC[:, g, n0:n0 + NHC], WHK[:, 1, g, :], False, True, [("XC", h), "s5w"], [pk])
            pv = pb[0:32, :].rearrange("p (g j c) -> p j g c", g=4, j=8)
            actf(zA[r0:r1, :, gq * 64:(gq + 1) * 64].rearrange("p j (g c) -> p j g c", g=4), pv,
                 AF.Gelu_apprx_tanh, [pk], [("tokA", h)])
            if gq % 4 == 3 and gq < 15:
                yield
        release("HP", me)
        if h == 1 and np_ is not None:
            s5w_load(np_[0], np_[1], "WS3" if layers[np_[1]] % 2 == 0 else "KV")
        yield
        for jp in range(2):
            pb, pk = bankb()
            for jj in range(4):
                j = jp * 4 + jj
                for kc in range(8):
                    c0 = (jj * 8 + kc) * 32
                    tr(pb[:, c0:c0 + 32], zA[r0:r1, j, kc * 128:(kc + 1) * 128], identb[r0:r1, r0:r1],
                       [("tokA", h), "identb"], [pk])
            for jj in range(4):
                j = jp * 4 + jj
                copy(ev_eng(), zB[:, :, h * HT + j:(h + 1) * HT:8], pb[:, jj * 256:(jj + 1) * 256].rearrange("p (k n) -> p k n", k=8),
                     [pk], [("zB", h)])
        yield
        sl = use(ti, li, "M")

        def ev_glu(mt, pb, pk):
            sb = sigb[:, :, t]
            for i in range(2):
                S.op("act", lambda e, i=i: e.activation(out=sb[:, i, :], in_=pb[:, i * HT:(i + 1) * HT], func=AF.Sigmoid,
                                                        bias=bgluc[:, si, mt + i:mt + i + 1], scale=1.0),
                     [pk, ("bgluc", si)], [("sigb", h)])
            gv = gB[:, mt:mt + 2, t]
            tt("dve", gv, gv, sb, ALU.mult, [("sigb", h), ("gB", h)], [("gB", h)])
            tt("dve", gv, gv, zB[:, mt:mt + 2, t], ALU.mult, [("zB", h), ("gB", h)], [("gB", h)])
        mm_fm(h, sl, zB, ("zB", h), range(0, 4), ev_glu)
        yield
        mm_fm(h, sl, zB, ("zB", h), range(4, 8), ev_glu)
        if h == 1 and np_ is not None:
            load_role(np_[0], np_[1], "E2")
        yield
        sl = use(ti, li, "OUT")
        out_proj(h, sl, range(0, 4))
        yield
        out_proj(h, sl, range(4, 8))
        if h == 1 and np_ is not None:
            load_role(np_[0], np_[1], "M")
        yield

    def wait_slot(ti, li, role):
        X, Y, Z = roles(ti, li)
        slot = {"E1": Y, "E2": X, "M": Z, "OUT": X}[role]
        while wtag.get(slot) != (ti, li, role):
            yield "BLOCK"

    def att_gen(ti, h, li, l):
        d = L[l]
        si = l // 2
        t = tk(h)
        t0 = ti * T + h * HT
        me = (ti, li, h)
        np_ = nxt_pos(ti, li)
        X, Y, Z = roles(ti, li)
        S.dma("sp", cosT[:, t], ropeC[:, t0:t0 + HT], R=[("rope", "c")], W=[("cosT", h)], semkey="rope%d" % h)
        S.dma("sp", sinT[:, t], ropeS[:, t0:t0 + HT], R=[("rope", "s")], W=[("sinT", h)], semkey="rope%d" % h)
        rmsnorm(h)
        norm_apply(h, li, hB[:, :, t], ("hB", h))
        if h == 0:
            copy("pool", KT[:, :, 0:128], KTc[:, si, :, :], ["KTc"], [("KT", 0)])
            copy("pool", Vpad[:, 0, :, :, :], Vc[:, si, :, :, :], ["Vc"], [("Vpad", 0)])
        yield

        def rope_pair2(dst, dkey, wA, wB, wkeys):
            pa, pka = bankf()
            pbb, pkb = bankf()
            for i in range(2):
                for kc in range(8):
                    mm(pa[:, i * HT:(i + 1) * HT], wA(i, kc), hB[:, kc, t], kc == 0, kc == 7, [("hB", h)] + wkeys, [pka])
            for i in range(2):
                for kc in range(8):
                    mm(pbb[:, i * HT:(i + 1) * HT], wB(i, kc), hB[:, kc, t], kc == 0, kc == 7, [("hB", h)] + wkeys, [pkb])
            cb = cosT[:, t].unsqueeze(1).broadcast_to([128, 2, HT])
            sb_ = sinT[:, t].unsqueeze(1).broadcast_to([128, 2, HT])
            sc3 = T12(h).rearrange("p (a b) -> p a b", a=2)
            tkeys = [("t1", h), ("t2", h)]
            tt("dve", sc3, pa[:, :].rearrange("p (a b) -> p a b", a=2), cb, ALU.mult, [pka, ("cosT", h)], tkeys)
            tt("dve", dst, pbb[:, :].rearrange("p (a b) -> p a b", a=2), sb_, ALU.mult, [pkb, ("sinT", h)], dkey)
            tt("pool", dst, dst, sc3, ALU.add, tkeys + dkey, dkey)

        for _ in wait_slot(ti, li, "E2"):
            yield _
        for _ in wait_slot(ti, li, "E1"):
            yield _
        for m2 in range(4):
            rope_pair2(zB[:, 2 * m2:2 * m2 + 2, t], [("zB", h)],
                       lambda i, kc, m2=m2: wsl[X][:, kc, (2 * m2 + i) * 128:(2 * m2 + i + 1) * 128],
                       lambda i, kc, m2=m2: wsl[Y][:, kc, (2 * m2 + i) * 128:(2 * m2 + i + 1) * 128], [("wsl", X), ("wsl", Y)])
            if m2 < 3:
                yield
        if h == 1:
            load_role(ti, li, "OUT")
            if np_ is not None:
                load_role(np_[0], np_[1], "E1")
        yield
        while wtag.get("s5w") != (ti, li, "KV"):
            yield "BLOCK"
        kslots = [("KT", 1 + 2 * h), ("KT", 2 + 2 * h)]
        rope_pair2(KT[:, :, 128 + h * HT:128 + (h + 1) * HT], kslots,
                   lambda i, kc: kvw[:, kc, i * 128:(i + 1) * 128],
                   lambda i, kc: kvw[:, kc, 256 + i * 128:256 + (i + 1) * 128], ["s5w"])
        for b2 in range(2):
            blk = 2 * h + b2
            pb, pk = bankf()
            for kc in range(8):
                mm(pb[:, 0:128], hB[:, kc, blk * 128:(blk + 1) * 128], kvw[:, kc, 512:640], kc == 0, kc == 7, [("hB", h), "s5w"], [pk])
            pv = pb[:, 0:128].rearrange("p (g d) -> p g d", g=2)
            copy("act", Vpad[:, blk + 1, :, 0, 0:64], pv, [pk], [("Vpad", blk + 1)])
            copy("act", Vpad[:, blk + 1, :, 1, 64:128], pv, [pk], [("Vpad", blk + 1)])
        if h == 1 and np_ is not None:
            s5w_load(np_[0], np_[1], "WS3" if layers[np_[1]] % 2 == 0 else "KV")
        yield
        for _ in wait_slot(ti, li, "M"):
            yield _
        ev_g = lambda mt, pb, pk: actf(gB[:, mt:mt + 2, t], pb[:, :].rearrange("p (a b) -> p a b", a=2), AF.Silu, [pk], [("gB", h)])
        mm_fm(h, Z, hB, ("hB", h), range(0, 4), ev_g)
        yield
        mm_fm(h, Z, hB, ("hB", h), range(4, 8), ev_g)
        if h == 1 and np_ is not None:
            load_role(np_[0], np_[1], "E2")
        yield
        for b2 in range(2):
            blk = 2 * h + b2
            first = (ti == 0 and blk == 0)
            for g in range(2):
                combos = [(par, kb) for par in range(2) for kb in ((1,) if first else (0, 1))]
                pts = []
                for ci_, (par, kb) in enumerate(combos):
                    s_ = blk + kb
                    pb, pk = bankf()
                    lo, hi = par * 64, (par + 1) * 64
                    mm(pb[:, :], KT[lo:hi, g, s_ * 128:(s_ + 1) * 128], zB[lo:hi, g * 4:(g + 1) * 4, blk * 128:(blk + 1) * 128],
                       True, False, [("KT", s_), ("zB", h)], [pk])
                    mm(pb[:, :], identb, mask2[:, kb, :].unsqueeze(1).broadcast_to([128, 4, 128]),
                       False, True, ["identb", "mask2"], [pk])
                    slot = h * 4 + ci_
                    actf(PT[:, slot, :], pb[:, :], AF.Exp, [pk], [("PT", slot)], scale=0.125)
                    pts.append((par, s_, slot))
                po, pko = bankf()
                pd, pkd = bankf()
                for i_, (par, s_, slot) in enumerate(pts):
                    mm(po[:, :], Vpad[:, s_, g, par, :], PT[:, slot, :], i_ == 0, i_ == len(pts) - 1, [("Vpad", s_), ("PT", slot)], [pko])
                for i_, (par, s_, slot) in enumerate(pts):
                    mm(pd[:, :], ones2[:, par, :], PT[:, slot, :], i_ == 0, i_ == len(pts) - 1, ["ones2", ("PT", slot)], [pkd])
                sc = T12(h)
                sc4 = sc.rearrange("p (k q) -> p k q", k=4)
                tkeys = [("t1", h), ("t2", h)]
                tt("dve", sc4, pd[:, :].rearrange("p (k q) -> p k q", k=4),
                   sinke[:, si, g * 4:(g + 1) * 4].unsqueeze(2).broadcast_to([128, 4, 128]), ALU.add,
                   [pkd, ("sinke", si)], tkeys)
                actf(sc, sc, AF.Ln, tkeys, tkeys)
                actf(sc, sc, AF.Exp, tkeys, tkeys, scale=-1.0)
                tt("dve", sc, po[:, :], sc, ALU.mult, [pko] + tkeys, tkeys)
                gv = gB[:, g * 4:(g + 1) * 4, blk * 128:(blk + 1) * 128]
                tt("dve", gv, gv, sc4, ALU.mult, tkeys + [("gB", h)], [("gB", h)])
                if not (g == 1 and b2 == 1):
                    yield
            if b2 == 1 and h == 1:
                copy("pool", KTc[:, si, :, :], KT[:, :, 512:640], [("KT", 4)], ["KTc"])
                copy("pool", Vc[:, si, :, :, :], Vpad[:, 4, :, :, :], [("Vpad", 4)], ["Vc"])
        yield
        for _ in wait_slot(ti, li, "OUT"):
            yield _
        out_proj(h, X, range(0, 4))
        yield
        out_proj(h, X, range(4, 8))
        if h == 1 and np_ is not None:
            load_role(np_[0], np_[1], "M")
        yield

    def tile_gen(ti, h):
        S.epoch = ti
        t = tk(h)
        t0 = ti * T + h * HT
        me = (ti, "io", h)
        for _ in acquire("HP", me):
            yield _
        for b2 in range(2):
            blk = 2 * h + b2
            xs = xtok[:, b2, :]
            S.dma("sp", xs, x[t0 + b2 * 128:t0 + (b2 + 1) * 128, :], W=["HP"], semkey="xin")
            for hh in range(2):
                pb, pk = bankf()
                for q in range(4):
                    kc = hh * 4 + q
                    tr(pb[:, q * 128:(q + 1) * 128], xs[:, kc * 128:(kc + 1) * 128], identf, ["HP", "identf"], [pk])
                copy(ev_eng(), xB[:, hh * 4:(hh + 1) * 4, blk * 128:(blk + 1) * 128],
                     pb[:, :].rearrange("p (k t) -> p k t", k=4), [pk], [("xB", h)])
        release("HP", me)
        yield
        for li, l in enumerate(layers):
            g_ = ssm_gen(ti, h, li, l) if l % 2 == 0 else att_gen(ti, h, li, l)
            for v_ in g_:
                yield v_
        for _ in acquire("SS", me):
            yield _
        rmsnorm(h)
        yh = yB[:, :, t]
        norm_apply(h, 4, yh, "SS")
        yield
        for _ in acquire("HP", me):
            yield _
        for b2 in range(2):
            blk = 2 * h + b2
            xs = xtok[:, b2, :]
            for hh in range(2):
                pb, pk = bankf()
                for q in range(4):
                    kc = hh * 4 + q
                    tr(pb[:, q * 128:(q + 1) * 128], yB[:, kc, blk * 128:(blk + 1) * 128], identf, ["SS", "identf"], [pk])
                copy(ev_eng(), xs[:, hh * 512:(hh + 1) * 512], pb[:, :], [pk], ["HP"])
            S.dma("sp", out[t0 + b2 * 128:t0 + (b2 + 1) * 128, :], xs, R=["HP"], W=["out"], semkey="xin")
        release("HP", me)
        release("SS", me)
        yield

    if n_tiles > 0 and NL > 0:
        for r_ in ("E1", "E2", "M"):
            load_role(0, 0, r_)
        s5w_load(0, 0, "WS3" if layers[0] % 2 == 0 else "KV")
    LAG = 4

    def chain(h):
        for ti in range(n_tiles):
            for v_ in tile_gen(ti, h):
                yield v_

    gens = [chain(0), chain(1)]
    alive = [True, True]
    step = 0
    blocked = 0
    while any(alive):
        progressed = False
        for hh_ in range(2):
            if not alive[hh_]:
                continue
            if hh_ == 1 and step < LAG and alive[0]:
                continue
            try:
                v_ = next(gens[hh_])
                if v_ != "BLOCK":
                    progressed = True
            except StopIteration:
                alive[hh_] = False
                progressed = True
        step += 1
        blocked = 0 if progressed else blocked + 1
        assert blocked < 1000, ("interleave deadlock", wtag, locks)
    S.barrier()

    with nc.Block() as block:
        S.emit(block)
    stack.close()
    return nc


def host_inputs(inputs, layers=(0, 1, 2, 3)):
    f32 = np.float32
    m = {}
    m["c_identf"] = np.eye(128, dtype=f32)
    q = np.arange(128)
    kj = np.arange(128)
    mprev = (kj[:, None] > q[None, :]).astype(f32)
    mcur = (kj[:, None] <= q[None, :]).astype(f32)
    m["c_mask2"] = np.ascontiguousarray(np.stack([mprev, mcur], axis=1))
    o2 = np.zeros((128, 2, 128), f32)
    o2[:, 0, 0:64] = 1.0
    o2[:, 1, 64:128] = 1.0
    m["c_ones2"] = o2
    ii = np.arange(128) // 16
    m["c_tri"] = (ii[None, :] >= ii[:, None]).astype(f32)
    m["c_fidx"] = (np.arange(128) % 32).astype(f32).reshape(128, 1)
    m["c_sgn"] = np.where((np.arange(128) % 64) < 32, -1.0, 1.0).astype(f32).reshape(128, 1)
    m["c_pos"] = np.arange(SEQ, dtype=f32)

    def col(v):
        return np.ascontiguousarray(np.asarray(v, f32).reshape(8, 128).T)

    m["fnorm_col"] = col(inputs["final_norm"])
    for l in layers:
        p = "l%d_" % l
        m[p + "norm_col"] = col(inputs[p + "norm"])
        if l % 2 == 0:
            m[p + "w_in"] = np.ascontiguousarray(inputs[p + "w_in"], f32)
            m[p + "w_glu"] = np.ascontiguousarray(inputs[p + "w_glu"], f32)
            m[p + "w_out"] = np.ascontiguousarray(inputs[p + "w_out"], f32)
            m[p + "bglu_col"] = col(inputs[p + "b_glu"])
            m[p + "aRe"] = np.ascontiguousarray(np.tile(np.asarray(inputs[p + "a_re"], f32).T, (2, 1)))
            m[p + "aIm"] = np.ascontiguousarray(np.tile(np.asarray(inputs[p + "a_im"], f32).T, (2, 1)))
            m[p + "ls"] = np.ascontiguousarray(np.tile(np.asarray(inputs[p + "log_step"], f32)[None, :], (128, 1)))
            m[p + "dcol"] = np.ascontiguousarray(np.tile(np.asarray(inputs[p + "d"], f32).reshape(64, 16).T, (8, 1)))
            m[p + "cRe"] = np.ascontiguousarray(np.tile(np.asarray(inputs[p + "c_re"], f32).transpose(2, 0, 1), (2, 1, 1)))
            m[p + "cIm"] = np.ascontiguousarray(np.tile(np.asarray(inputs[p + "c_im"], f32).transpose(2, 0, 1), (2, 1, 1)))
            m[p + "bRe"] = np.ascontiguousarray(np.tile(np.asarray(inputs[p + "b_re"], f32).transpose(1, 0, 2), (2, 1, 1)))
            m[p + "bIm"] = np.ascontiguousarray(np.tile(np.asarray(inputs[p + "b_im"], f32).transpose(1, 0, 2), (2, 1, 1)))
        else:
            w = np.asarray(inputs[p + "w_in"], f32)
            m[p + "w_in"] = np.ascontiguousarray(w)
            dd = np.arange(64)
            partner = np.where(dd < 32, dd + 32, dd - 32)
            qperm = (np.arange(16)[:, None] * 64 + partner[None, :]).reshape(-1)
            m[p + "w_qp"] = np.ascontiguousarray(w[:, qperm])
            k0 = w[:, 1024:1088]
            k1 = w[:, 1088:1152]
            m[p + "w_kk"] = np.ascontiguousarray(np.concatenate(
                [k0, k0, k1, k1, k0[:, partner], k0[:, partner], k1[:, partner], k1[:, partner]], axis=1))
            m[p + "w_out"] = np.ascontiguousarray(inputs[p + "w_out"], f32)
            s = np.asarray(inputs[p + "sinks"], f32)
            sc = np.zeros((128, 8), f32)
            for kc in range(8):
                sc[0:64, kc] = s[2 * kc]
                sc[64:128, kc] = s[2 * kc + 1]
            m[p + "sink_col"] = sc
    return m


INPUT_NAMES = (
    "x",
    "l0_norm", "l0_w_in", "l0_a_re", "l0_a_im", "l0_log_step", "l0_b_re", "l0_b_im", "l0_c_re", "l0_c_im", "l0_d",
    "l0_w_glu", "l0_b_glu", "l0_w_out",
    "l1_norm", "l1_w_in", "l1_sinks", "l1_w_out",
    "l2_norm", "l2_w_in", "l2_a_re", "l2_a_im", "l2_log_step", "l2_b_re", "l2_b_im", "l2_c_re", "l2_c_im", "l2_d",
    "l2_w_glu", "l2_b_glu", "l2_w_out",
    "l3_norm", "l3_w_in", "l3_sinks", "l3_w_out",
    "final_norm",
)

_NC_CACHE = {}


def kernel(**inputs):
    layers = (0, 1, 2, 3)
    missing = [n for n in INPUT_NAMES if n not in inputs]
    assert not missing, missing
    if "full" not in _NC_CACHE:
        _NC_CACHE["full"] = build(16, layers)
    nc = _NC_CACHE["full"]
    shared = host_inputs(inputs, layers)
    xs = np.asarray(inputs["x"], np.float32)
    in_maps = []
    for c in range(8):
        mc = dict(shared)
        mc["x"] = np.ascontiguousarray(xs[c])
        in_maps.append(mc)
    res = run_bass_kernel_spmd(nc, in_maps, core_ids=list(range(8)))
    return np.stack([np.asarray(r["out"], np.float32) for r in res.results], axis=0)
```

```python
import math
import os
from contextlib import ExitStack

import numpy as np
import concourse.bass as bass
import concourse.mybir as mybir
from concourse.bass_utils import run_bass_kernel_spmd

F32 = mybir.dt.float32
BF16 = mybir.dt.bfloat16
I32 = mybir.dt.int32
AF = mybir.ActivationFunctionType
ALU = mybir.AluOpType

T = 512
NCH = 64
NH = 32
SEQ = 8192
DM = 1024
EPS = 1e-5
TWO_PI = 2.0 * math.pi
SIN_SCALE = 6.28318


class Sched:
    CE = ("pe", "act", "dve", "pool")

    def __init__(self, nc, stack):
        self.nc = nc
        self.stack = stack
        self.q = {e: [] for e in ("pe", "act", "dve", "pool", "sp")}
        self.esem = {e: [stack.enter_context(nc.semaphore("c_%s_%d" % (e, i))) for i in range(4)] for e in self.CE}
        self.ecnt = {(e, i): 0 for e in self.CE for i in range(4)}
        self.epoch = 0
        self.last_w = {}
        self.readers = {}
        self.known = {e: {} for e in self.q}
        self.dsem = {}
        self.dcnt = {}
        self.semkey_of = {}
        self.all_tokens = {}

    def _deps(self, eng, R, W):
        deps = []
        for k in R:
            t = self.last_w.get(k)
            if t is not None:
                deps.append(t)
        for k in W:
            t = self.last_w.get(k)
            if t is not None:
                deps.append(t)
            deps.extend(self.readers.get(k, ()))
        waits = {}
        for (sem, val, e2) in deps:
            if e2 == eng and eng == "pe":
                continue
            if e2 == "dma":
                val = self.dcnt[self.semkey_of[sem]]
            if self.known[eng].get(sem, 0) >= val:
                continue
            if waits.get(sem, 0) < val:
                waits[sem] = val
        for sem, val in waits.items():
            self.q[eng].append(lambda e, s=sem, v=val: e.wait_ge(s, v))
            self.known[eng][sem] = val

    def _record(self, tok, R, W):
        for k in R:
            self.readers.setdefault(k, []).append(tok)
        for k in W:
            self.last_w[k] = tok
            self.readers[k] = []
        self.all_tokens[tok[0]] = max(self.all_tokens.get(tok[0], 0), tok[1])

    def op(self, eng, fn, R=(), W=()):
        self._deps(eng, R, W)
        i = self.epoch % 4
        sem = self.esem[eng][i]
        self.ecnt[(eng, i)] += 1
        val = self.ecnt[(eng, i)]
        self.q[eng].append(lambda e, f=fn, s=sem: f(e).then_inc(s, 1))
        self._record((sem, val, eng), R, W)

    def dma(self, queue, out, in_, R=(), W=(), semkey=None):
        self._deps(queue, R, W)
        if semkey not in self.dsem:
            self.dsem[semkey] = self.stack.enter_context(self.nc.semaphore("d_%s" % semkey))
            self.dcnt[semkey] = 0
            self.semkey_of[self.dsem[semkey]] = semkey
        sem = self.dsem[semkey]
        self.dcnt[semkey] += 16
        val = self.dcnt[semkey]
        self.q[queue].append(lambda e, o=out, i=in_, s=sem: e.dma_start(out=o, in_=i).then_inc(s, 16))
        self._record((sem, val, "dma"), R, W)

    def barrier(self, skip=()):
        for eng in self.q:
            for sem, val in self.all_tokens.items():
                if self.semkey_of.get(sem) in skip:
                    continue
                if self.known[eng].get(sem, 0) < val:
                    self.q[eng].append(lambda e, s=sem, v=val: e.wait_ge(s, v))
                    self.known[eng][sem] = val

    def emit(self, block):
        q = self.q

        @block.tensor
        def _(e):
            for f in q["pe"]:
                f(e)

        @block.scalar
        def _(e):
            for f in q["act"]:
                f(e)

        @block.vector
        def _(e):
            for f in q["dve"]:
                f(e)

        @block.gpsimd
        def _(e):
            for f in q["pool"]:
                f(e)

        @block.sync
        def _(e):
            for f in q["sp"]:
                f(e)


SSM_HOST = ("norm", "bglu", "aRe", "aIm", "ls", "cRe", "cIm", "bRe", "bIm", "dcol")


def build(n_tiles=16, layers=(0, 1, 2, 3)):
    nc = bass.Bass("TRN2", target_bir_lowering=False)
    stack = ExitStack()
    S = Sched(nc, stack)

    def din(name, shape, dt=F32):
        return nc.dram_tensor(name, list(shape), dt, kind="ExternalInput").ap()

    def dscr(name, shape, dt):
        return nc.dram_tensor(name, list(shape), dt, kind="Internal").ap()

    x = din("x", [SEQ, DM])
    out = nc.dram_tensor("out", [SEQ, DM], F32, kind="ExternalOutput").ap()
    c_identf = din("c_identf", [128, 128])
    c_mask2 = din("c_mask2", [128, 2, 128])
    c_ones2 = din("c_ones2", [128, 2, 128])
    c_tri = din("c_tri", [128, 128])
    c_fidx = din("c_fidx", [128, 1])
    c_sgn = din("c_sgn", [128, 1])
    c_pos = din("c_pos", [SEQ])
    fnorm = din("fnorm_col", [128, 8])
    L = {}
    for l in layers:
        p = "l%d_" % l
        d = {}
        d["norm"] = din(p + "norm_col", [128, 8])
        if l % 2 == 0:
            d["w_in"] = din(p + "w_in", [DM, 2048])
            d["w_glu"] = din(p + "w_glu", [DM, DM])
            d["w_out"] = din(p + "w_out", [DM, DM])
            d["bglu"] = din(p + "bglu_col", [128, 8])
            for nm in ("aRe", "aIm", "ls", "dcol"):
                d[nm] = din(p + nm, [128, 64])
            for nm in ("cRe", "cIm", "bRe", "bIm"):
                d[nm] = din(p + nm, [128, 64, 16])
            d["w_in_b"] = dscr(p + "w_in_b", [DM, 2048], BF16)
            d["w_glu_b"] = dscr(p + "w_glu_b", [DM, DM], BF16)
            d["w_out_b"] = dscr(p + "w_out_b", [DM, DM], BF16)
            d["WS3"] = dscr(p + "WS3", [128, 64, 192], BF16)
            d["WHK"] = dscr(p + "WHK", [128, 2, 64, 128], BF16)
        else:
            d["w_in"] = din(p + "w_in", [DM, 2304])
            d["w_qp"] = din(p + "w_qp", [DM, DM])
            d["w_kk"] = din(p + "w_kk", [DM, 512])
            d["w_out"] = din(p + "w_out", [DM, DM])
            d["sink"] = din(p + "sink_col", [128, 8])
            d["w_in_b"] = dscr(p + "w_in_b", [DM, 2304], BF16)
            d["w_qp_b"] = dscr(p + "w_qp_b", [DM, DM], BF16)
            d["w_kk_b"] = dscr(p + "w_kk_b", [DM, 512], BF16)
            d["w_out_b"] = dscr(p + "w_out_b", [DM, DM], BF16)
        L[l] = d
    has_att = any(l % 2 == 1 for l in layers)
    ropeC = dscr("ropeC", [128, SEQ], F32)
    ropeS = dscr("ropeS", [128, SEQ], F32)

    ARENA_W = 53000
    arena = stack.enter_context(nc.sbuf_tensor("arena", [128, ARENA_W], F32))
    cur = [0]

    def carve(shape, dt, at=None):
        n = int(np.prod(shape[1:]))
        words = n if dt in (F32, I32) else (n + 1) // 2
        if at is None:
            off = cur[0]
            cur[0] += words
            assert cur[0] <= ARENA_W, ("sbuf overflow", cur[0])
        else:
            off = at
            assert off + words <= ARENA_W
        v = arena[:, off:off + words]
        if dt != F32:
            v = v.bitcast(dt)
        if len(shape) > 2:
            names = " ".join("d%d" % i for i in range(1, len(shape)))
            kw = {"d%d" % i: shape[i] for i in range(1, len(shape))}
            v = v.rearrange("p (%s) -> p %s" % (names, names), **kw)
        return v

    identf = carve([128, 128], F32)
    identb = carve([128, 128], BF16)
    onesb = carve([128, 128], BF16)
    mask2 = carve([128, 2, 128], BF16)
    ones2 = carve([128, 2, 128], BF16)
    stage_c = carve([128, 2, 128], F32)
    normc = carve([128, 5, 8], F32)
    bgluc = carve([128, 2, 8], F32)
    sinke = carve([128, 2, 8], F32)
    CAB = carve([128, 2, 2, 2, 64], F32)
    HHc = carve([128, 2, 2, 64], F32)
    pro0 = cur[0]
    scr = carve([128, 2, 3, T // 2], F32)

    def RSTD(h):
        return scr[:, h, 0, :]

    def T1(h):
        return scr[:, h, 1, :]

    def T2(h):
        return scr[:, h, 2, :]

    def T12(h):
        return scr[:, h, 1:3, :].rearrange("p a b -> p (a b)")
    sigb = carve([128, 2, T], BF16)
    cosT = carve([128, T], F32)
    sinT = carve([128, T], F32)
    KT = carve([128, 2, 640], BF16)
    Vpad = carve([128, 5, 2, 2, 128], BF16)
    KTc = carve([128, 2, 2, 128], BF16)
    Vc = carve([128, 2, 2, 2, 128], BF16)
    xB = carve([128, 8, T], F32)
    hB = carve([128, 8, T], BF16)
    gB = carve([128, 8, T], BF16)
    zB = carve([128, 8, T], BF16)
    main0 = cur[0]
    tokA_off = cur[0]
    tokA = carve([128, 8, DM], BF16)
    XC = carve([128, 64, 64], BF16)
    HP_off = cur[0]
    HP = carve([128, 64, 64], BF16)
    xtok = carve([128, 2, DM], F32, at=HP_off)
    SS = carve([128, NH, 2, 64], F32)
    s5w = carve([128, 64 * 256], BF16)
    wsl = [carve([128, 8, DM], BF16) for _ in range(3)]
    PT = tokA.rearrange("p a b -> p (a b)")[:, 0:8 * T].rearrange("p (s t) -> p s t", s=8)
    yB = SS.rearrange("p a b c -> p (a b c)").rearrange("p (k t) -> p k t", k=8)
    WS3 = s5w[:, 0:64 * 192].rearrange("p (g r) -> p g r", g=64)
    WHK = s5w.rearrange("p (s g r) -> p s g r", s=2, g=64)
    kvw = s5w[:, 0:8 * 640].rearrange("p (k n) -> p k n", k=8)

    psf = [stack.enter_context(nc.psum_tensor("psf%d" % i, [128, 512], F32)) for i in range(6)]
    psb = [stack.enter_context(nc.psum_tensor("psb%d" % i, [128, 1024], BF16)) for i in range(2)]
    rr = {"f": 0, "b": 0, "e": 0}

    def bankf():
        i = rr["f"] % 6
        rr["f"] += 1
        return psf[i], ("psf", i)

    def bankb():
        i = rr["b"] % 2
        rr["b"] += 1
        return psb[i], ("psb", i)

    def ev_eng():
        rr["e"] += 1
        return "act" if rr["e"] % 2 else "dve"

    def copy(eng, o, i, R, W):
        if eng != "act" and o.dtype == i.dtype:
            S.op(eng, lambda e: e.tensor_single_scalar(out=o, in_=i, scalar=1.0, op=ALU.mult), R, W)
            return
        if eng == "act":
            S.op("act", lambda e: e.activation(out=o, in_=i, func=AF.Identity, scale=1.0, bias=0.0), R, W)
        else:
            S.op(eng, lambda e: e.tensor_copy(out=o, in_=i), R, W)

    def tt(eng, o, a, b, op, R, W):
        S.op(eng, lambda e: e.tensor_tensor(out=o, in0=a, in1=b, op=op), R, W)

    def ts(eng, o, a, s1, op, R, W):
        S.op(eng, lambda e: e.tensor_single_scalar(out=o, in_=a, scalar=s1, op=op), R, W)

    def actf(o, i, func, R, W, scale=1.0, bias=0.0):
        S.op("act", lambda e: e.activation(out=o, in_=i, func=func, scale=scale, bias=bias), R, W)

    def mm(o, lhsT, rhs, start, stop, R, W):
        S.op("pe", lambda e: e.matmul(o, lhsT=lhsT, rhs=rhs, start=start, stop=stop), R, W)

    def tr(o, i, ident, R, W):
        S.op("pe", lambda e: e.transpose(o, i, ident), R, W)

    S.dma("sp", identf, c_identf, W=["identf"], semkey="c0")
    copy("dve", identb, identf, ["identf"], ["identb"])
    S.op("dve", lambda e: e.memset(onesb, 1.0), (), ["onesb"])
    S.dma("sp", stage_c, c_mask2, W=["stage_c"], semkey="c1")
    S.op("dve", lambda e: e.tensor_scalar(out=mask2, in0=stage_c, scalar1=-1.0, scalar2=30000.0, op0=ALU.add, op1=ALU.mult),
         ["stage_c"], ["mask2"])
    S.dma("sp", stage_c, c_ones2, R=(), W=["stage_c"], semkey="c1")
    copy("dve", ones2, stage_c, ["stage_c"], ["ones2"])
    S.op("dve", lambda e: e.memset(HHc, 0.0), (), ["HHc"])
    for li, l in enumerate(layers):
        S.dma("sp", normc[:, li, :], L[l]["norm"], W=[("normc", li)], semkey="c2")
    S.dma("sp", normc[:, 4, :], fnorm, W=[("normc", 4)], semkey="c2")

    for l in layers:
        d = L[l]
        names = ("w_in", "w_glu", "w_out") if l % 2 == 0 else ("w_in", "w_qp", "w_kk", "w_out")
        for nm in names:
            for rb in range(8):
                S.dma("pool", d[nm + "_b"][rb * 128:(rb + 1) * 128, :], d[nm][rb * 128:(rb + 1) * 128, :], W=[("wscr", l, nm, rb)], semkey="wc_%d_%s" % (l, nm))

    if has_att:
        pw = 2048
        base = pro0
        posb = carve([128, pw], F32, at=base)
        tq = carve([128, pw], F32, at=base + pw)
        tqi = carve([128, pw], I32, at=base + 2 * pw)
        tqf = carve([128, pw], F32, at=base + 3 * pw)
        res = carve([128, pw], F32, at=base + 4 * pw)
        fidx = carve([128, 1], F32, at=base + 5 * pw)
        invf = carve([128, 1], F32, at=base + 5 * pw + 1)
        sgn = carve([128, 1], F32, at=base + 5 * pw + 2)
        S.dma("sp", fidx, c_fidx, W=["pp"], semkey="c3")
        S.dma("sp", sgn, c_sgn, W=["pp"], semkey="c3")
        actf(invf, fidx, AF.Exp, ["pp"], ["pp"], scale=-math.log(10000.0) / 32.0)
        for ch in range(SEQ // pw):
            S.dma("sp", posb, c_pos[ch * pw:(ch + 1) * pw].partition_broadcast(128), W=["pp"], semkey="c4")
            S.op("dve", lambda e: e.tensor_scalar_mul(out=posb, in0=posb, scalar1=invf[:, 0:1]), ["pp", "pp"], ["pp"])
            ts("dve", posb, posb, 1.0 / TWO_PI, ALU.mult, ["pp"], ["pp"])
            for which, shift, dst in (("s", 0.0, ropeS), ("c", 0.25, ropeC)):
                ts("dve", tq, posb, shift, ALU.add, ["pp"], ["pp"])
                copy("dve", tqi, tq, ["pp"], ["pp"])
                copy("dve", tqf, tqi, ["pp"], ["pp"])
                tt("dve", tq, tq, tqf, ALU.subtract, ["pp", "pp"], ["pp"])
                actf(res, tq, AF.Sin, ["pp"], ["pp"], scale=SIN_SCALE)
                if which == "s":
                    S.op("dve", lambda e: e.tensor_scalar_mul(out=res, in0=res, scalar1=sgn[:, 0:1]), ["pp", "pp"], ["pp"])
                S.dma("sp", dst[:, ch * pw:(ch + 1) * pw], res, R=["pp"], W=[("rope", which)], semkey="c5")
        for li, l in enumerate(layers):
            if l % 2 == 1:
                si = (l // 2)
                S.dma("sp", sinke[:, si, :], L[l]["sink"], W=[("sinke", si)], semkey="c6_%d" % si)
                actf(sinke[:, si, :], sinke[:, si, :], AF.Exp, [("sinke", si)], [("sinke", si)])

    for l in layers:
        if l % 2:
            continue
        d = L[l]
        si = l // 2
        o = [pro0]

        def pc(shape, dt=F32):
            n = int(np.prod(shape[1:]))
            w = n if dt in (F32, I32) else (n + 1) // 2
            v = carve(shape, dt, at=o[0])
            o[0] += w
            return v
        NK = 17
        aRe = pc([128, 64]); aIm = pc([128, 64]); ls = pc([128, 64]); dcol = pc([128, 64])
        cRe = pc([128, 64, 16]); cIm = pc([128, 64, 16]); bRe = pc([128, 64, 16]); bIm = pc([128, 64, 16])
        lr = pc([128, 64]); lim = pc([128, 64])
        Er = pc([128, NK, 64]); Ei = pc([128, NK, 64]); Fr = pc([128, NK, 64]); Fi = pc([128, NK, 64])
        cr = pc([128, 64]); ci = pc([128, 64]); w1 = pc([128, 64]); w2 = pc([128, 64])
        WST = pc([128, 64, 8, 16]); WS2T = pc([128, 64, 8, 16], BF16); WHf = pc([128, 64, 8, 16])
        tri = pc([128, 128]); k0t = pc([128, 128])
        U1 = o[0]
        u1 = pc([128, 64, 8, 16])
        U2 = o[0]
        u2 = pc([128, 64, 8, 16])
        o2 = [U1]
        def pc2(shape, dt=F32):
            n = int(np.prod(shape[1:]))
            v = carve(shape, dt, at=o2[0])
            o2[0] += n
            return v
        ex = pc2([128, NK, 64]); tq_ = pc2([128, NK, 64]); t2_ = pc2([128, NK, 64]); ti_ = pc2([128, NK, 64], I32)
        tf_ = pc2([128, NK, 64]); sn = pc2([128, NK, 64]); cs = pc2([128, NK, 64])
        assert o2[0] <= U2
        WS3s = carve([128, 64, 192], BF16, at=U1)
        WHKs = carve([128, 2, 64, 128], BF16, at=U2)
        assert o[0] <= ARENA_W, o[0]
        k = "pp"
        for nm, tl in (("aRe", aRe), ("aIm", aIm), ("ls", ls), ("dcol", dcol), ("cRe", cRe), ("cIm", cIm), ("bRe", bRe), ("bIm", bIm)):
            S.dma("sp", tl, d[nm], W=[k + nm], semkey="pp")
        S.dma("sp", tri, c_tri, W=[k + "tri"], semkey="pp")
        S.dma("sp", bgluc[:, si, :], d["bglu"], W=[("bgluc", si)], semkey="pp")
        lk = [k] + [k + nm for nm in ("aRe", "aIm", "ls", "dcol", "cRe", "cIm", "bRe", "bIm", "tri")]
        actf(ls, ls, AF.Exp, lk, [k, "WS3s", "WHKs"])
        tt("dve", lr, aRe, ls, ALU.mult, [k], [k])
        tt("dve", lim, aIm, ls, ALU.mult, [k], [k])
        for kk in range(-8, 9):
            idx = kk + 8
            actf(ex[:, idx, :], lr, AF.Exp, [k], [k], scale=float(kk))
            ts("dve", tq_[:, idx, :], lim, kk / TWO_PI, ALU.mult, [k], [k])
        for dst, shift in ((sn, 0.0), (cs, 0.25)):
            ts("dve", t2_, tq_, shift, ALU.add, [k], [k])
            copy("dve", ti_, t2_, [k], [k])
            copy("dve", tf_, ti_, [k], [k])
            tt("dve", t2_, t2_, tf_, ALU.subtract, [k], [k])
            actf(dst, t2_, AF.Sin, [k], [k], scale=SIN_SCALE)
        tt("dve", Er, ex, cs, ALU.mult, [k], [k])
        tt("dve", Ei, ex, sn, ALU.mult, [k], [k])
        E1r = Er[:, 9, :]; E1i = Ei[:, 9, :]
        ts("dve", w1, E1r, -1.0, ALU.add, [k], [k])
        tt("dve", cr, w1, aRe, ALU.mult, [k], [k])
        tt("dve", w2, E1i, aIm, ALU.mult, [k], [k])
        tt("dve", cr, cr, w2, ALU.add, [k], [k])
        tt("dve", ci, E1i, aRe, ALU.mult, [k], [k])
        tt("dve", w2, w1, aIm, ALU.mult, [k], [k])
        tt("dve", ci, ci, w2, ALU.subtract, [k], [k])
        tt("dve", w1, aRe, aRe, ALU.mult, [k], [k])
        tt("dve", w2, aIm, aIm, ALU.mult, [k], [k])
        tt("dve", w1, w1, w2, ALU.add, [k], [k])
        S.op("dve", lambda e: e.reciprocal(out=w1, in_=w1), [k], [k])
        tt("dve", cr, cr, w1, ALU.mult, [k], [k])
        tt("dve", ci, ci, w1, ALU.mult, [k], [k])
        crb = cr.unsqueeze(1).broadcast_to([128, NK, 64])
        cib = ci.unsqueeze(1).broadcast_to([128, NK, 64])
        tt("dve", Fr, Er, crb, ALU.mult, [k], [k])
        tt("dve", t2_, Ei, cib, ALU.mult, [k], [k])
        tt("dve", Fr, Fr, t2_, ALU.subtract, [k], [k])
        tt("dve", Fi, Er, cib, ALU.mult, [k], [k])
        tt("dve", t2_, Ei, crb, ALU.mult, [k], [k])
        tt("dve", Fi, Fi, t2_, ALU.add, [k], [k])

        def kview(v, lo, hi, k0, rev):
            vv = v[lo:hi, :, :].rearrange("p k g -> p g k")
            vv = vv[:, :, k0 - 7:k0 + 1][:, :, ::-1] if rev else vv[:, :, k0:k0 + 8]
            return vv.unsqueeze(3).broadcast_to([hi - lo, 64, 8, 16])

        def bview(v, lo, hi):
            return v[lo:hi].unsqueeze(2).broadcast_to([hi - lo, 64, 8, 16])

        for dstT, k0 in ((WST, 15), (WS2T, 7)):
            tt("dve", u1[0:64], bview(bRe, 0, 64), kview(Fr, 0, 64, k0, True), ALU.mult, [k], [k])
            tt("dve", u2[0:64], bview(bIm, 0, 64), kview(Fi, 0, 64, k0, True), ALU.mult, [k], [k])
            tt("dve", dstT[0:64], u1[0:64], u2[0:64], ALU.subtract, [k], [k])
            tt("dve", u1[64:128], bview(bIm, 64, 128), kview(Fr, 64, 128, k0, True), ALU.mult, [k], [k])
            tt("dve", u2[64:128], bview(bRe, 64, 128), kview(Fi, 64, 128, k0, True), ALU.mult, [k], [k])
            tt("dve", dstT[64:128], u1[64:128], u2[64:128], ALU.add, [k], [k])
        tt("dve", u1[0:64], bview(cRe, 0, 64), kview(Er, 0, 64, 9, False), ALU.mult, [k], [k])
        tt("dve", u2[0:64], bview(cIm, 0, 64), kview(Ei, 0, 64, 9, False), ALU.mult, [k], [k])
        tt("dve", WHf[0:64], u1[0:64], u2[0:64], ALU.subtract, [k], [k])
        tt("dve", u1[64:128], bview(cRe, 64, 128), kview(Ei, 64, 128, 9, False), ALU.mult, [k], [k])
        tt("dve", u2[64:128], bview(cIm, 64, 128), kview(Er, 64, 128, 9, False), ALU.mult, [k], [k])
        tt("dve", u1[64:128], u1[64:128], u2[64:128], ALU.add, [k], [k])
        ts("dve", WHf[64:128], u1[64:128], -1.0, ALU.mult, [k], [k])
        for s2 in range(2):
            copy("dve", CAB[:, si, 0, s2, :], Er[:, 16, :], [k], [("CAB", si)])
        ts("dve", CAB[0:64, si, 1, 0, :], Ei[0:64, 16, :], -1.0, ALU.mult, [k], [("CAB", si)])
        copy("dve", CAB[64:128, si, 1, 0, :], Ei[64:128, 16, :], [k], [("CAB", si)])
        ts("dve", CAB[:, si, 1, 1, :], CAB[:, si, 1, 0, :], -1.0, ALU.mult, [("CAB", si)], [("CAB", si)])
        copy("act", WHKs[:, 0, :, :], WHf.rearrange("p g j c -> p g (j c)"), [k], ["WHKs"])
        for g in range(64):
            pb, pk = bankf()
            pb2, pk2 = bankf()
            tr(pb[:, 0:128], WST[:, g, :, :].rearrange("p j c -> p (j c)"), identf, [k, "identf"], [pk])
            mm(pb2[:, 0:128], WS2T[:, g, :, :].rearrange("p j c -> p (j c)"),
               WHKs[:, 0, g, :], True, True, [k, "WHKs"], [pk2])
            copy("act", WS3s[:, g, 0:128], pb[:, 0:128], [pk], ["WS3s"])
            copy("act", WS3s[:, g, 128:192], pb[:, 0:64], [pk], ["WS3s"])
            tt("dve", k0t, pb2[:, 0:128], tri, ALU.mult, [pk2, k], ["k0t"])
            S.op("dve", lambda e, g=g: e.scalar_tensor_tensor(out=WHKs[:, 1, g, :], in0=identf, scalar=dcol[:, g:g + 1],
                                                               in1=k0t, op0=ALU.mult, op1=ALU.add),
                 ["k0t", k, "identf"], ["WHKs"])
        S.dma("sp", d["WS3"], WS3s, R=["WS3s"], W=[("s5scr", l)], semkey="pst")
        S.dma("sp", d["WHK"], WHKs, R=["WHKs"], W=[("s5scr", l)], semkey="pst")
    S.barrier(skip=[kk_ for kk_ in S.dsem if kk_.startswith("wc_")])
    S.op("pool", lambda e: e.memset(Vpad, 0.0), (), ["Vpad"])
    S.op("pool", lambda e: e.memset(KT, 0.0), (), ["KT"])
    S.op("pool", lambda e: e.memset(KTc, 0.0), (), ["KTc"])
    S.op("pool", lambda e: e.memset(Vc, 0.0), (), ["Vc"])

    HT = T // 2
    NHC = NH

    def tk(h):
        return slice(h * HT, (h + 1) * HT)

    def load_w(slot, src, cols, l, nm):
        ncol = cols[1] - cols[0]
        S.dma("sp", wsl[slot][:, :, 0:ncol], src[:, cols[0]:cols[1]].rearrange("(k p) n -> p k n", p=128),
              R=[("wscr", l, nm, rb) for rb in range(8)], W=[("wsl", slot)], semkey="wsl%d" % slot)

    def load_s5w(l, which):
        d = L[l]
        if which == "WS3":
            S.dma("sp", WS3, d["WS3"], R=[("s5scr", l)], W=["s5w"], semkey="s5w")
        elif which == "WHK":
            S.dma("sp", WHK, d["WHK"], R=[("s5scr", l)], W=["s5w"], semkey="s5w")
        else:
            S.dma("sp", kvw[:, :, 0:512], d["w_kk_b"].rearrange("(k p) n -> p k n", p=128),
                  R=[("wscr", l, "w_kk", rb) for rb in range(8)], W=["s5w"], semkey="s5w")
            S.dma("sp", kvw[:, :, 512:640], d["w_in_b"][:, 1152:1280].rearrange("(k p) n -> p k n", p=128),
                  R=[("wscr", l, "w_in", rb) for rb in range(8)], W=["s5w"], semkey="s5w")

    def layer_loads(l, which):
        d = L[l]
        if l % 2 == 0:
            if which == "slot0":
                load_w(0, d["w_in_b"], (0, 1024), l, "w_in")
            elif which == "slot1":
                load_w(1, d["w_in_b"], (1024, 2048), l, "w_in")
            elif which == "slot2":
                load_w(2, d["w_glu_b"], (0, 1024), l, "w_glu")
            else:
                load_s5w(l, "WS3")
        else:
            if which == "slot0":
                load_w(0, d["w_in_b"], (0, 1024), l, "w_in")
            elif which == "slot1":
                load_w(1, d["w_qp_b"], (0, 1024), l, "w_qp")
            elif which == "slot2":
                load_w(2, d["w_in_b"], (1280, 2304), l, "w_in")
            else:
                load_s5w(l, "KV")

    def rmsnorm(h):
        t = tk(h)
        actf(zB[:, :, t], xB[:, :, t], AF.Square, [("xB", h)], [("zB", h)])
        pb, pk = bankf()
        for kc in range(8):
            mm(pb[:, 0:HT], onesb, zB[:, kc, t], kc == 0, kc == 7, ["onesb", ("zB", h)], [pk])
        actf(T2(h), pb[:, 0:HT], AF.Ln, [pk], [("t2", h)], scale=1.0 / DM, bias=EPS)
        actf(RSTD(h), T2(h), AF.Exp, [("t2", h)], [("rstd", h)], scale=-0.5)

    def norm_apply(h, li, dst, dkey):
        t = tk(h)
        for kc in range(8):
            S.op("dve", lambda e, kc=kc: e.scalar_tensor_tensor(out=dst[:, kc, :], in0=xB[:, kc, t],
                                                                 scalar=normc[:, li, kc:kc + 1], in1=RSTD(h),
                                                                 op0=ALU.mult, op1=ALU.mult),
                 [("xB", h), ("rstd", h), ("normc", li)], [dkey])

    def mm_fm(h, slot, act, akey, mts, evac):
        t = tk(h)
        for mt in mts:
            if mt % 2:
                continue
            pb, pk = bankf()
            for i in range(2):
                for kc in range(8):
                    mm(pb[:, i * HT:(i + 1) * HT], wsl[slot][:, kc, (mt + i) * 128:(mt + i + 1) * 128], act[:, kc, t], kc == 0, kc == 7,
                       [("wsl", slot), akey], [pk])
            evac(mt, pb, pk)

    def out_proj(h, slot, mts=8):
        t = tk(h)

        def ev(mt, pb, pk):
            tt("dve", xB[:, mt:mt + 2, t], xB[:, mt:mt + 2, t], pb[:, :].rearrange("p (a b) -> p a b", a=2), ALU.add,
               [pk, ("xB", h)], [("xB", h)])
        mm_fm(h, slot, gB, ("gB", h), mts, ev)

    NL = len(layers)
    wtag = {}
    locks = {}
    udone = {}

    def roles(ti, li):
        gl = ti * NL + li
        X, Z = (0, 2) if gl % 2 == 0 else (2, 0)
        return X, 1, Z

    def load_role(ti, li, role):
        l = layers[li]
        d = L[l]
        X, Y, Z = roles(ti, li)
        if l % 2 == 0:
            spec = {"E1": (Y, d["w_in_b"], (1024, 2048), "w_in"), "E2": (X, d["w_in_b"], (0, 1024), "w_in"),
                    "M": (Z, d["w_glu_b"], (0, 1024), "w_glu"), "OUT": (X, d["w_out_b"], (0, 1024), "w_out")}[role]
        else:
            spec = {"E1": (Y, d["w_qp_b"], (0, 1024), "w_qp"), "E2": (X, d["w_in_b"], (0, 1024), "w_in"),
                    "M": (Z, d["w_in_b"], (1280, 2304), "w_in"), "OUT": (X, d["w_out_b"], (0, 1024), "w_out")}[role]
        slot, src, cols, nm = spec
        load_w(slot, src, cols, l, nm)
        wtag[slot] = (ti, li, role)

    def use(ti, li, role):
        X, Y, Z = roles(ti, li)
        slot = {"E1": Y, "E2": X, "M": Z, "OUT": X}[role]
        assert wtag.get(slot) == (ti, li, role), ("weight slot content mismatch", slot, wtag.get(slot), (ti, li, role))
        return slot

    def nxt_pos(ti, li):
        if li + 1 < NL:
            return (ti, li + 1)
        if ti + 1 < n_tiles:
            return (ti + 1, 0)
        return None

    def s5w_load(ti, li, which):
        load_s5w(layers[li], which)
        wtag["s5w"] = (ti, li, which)

    def s5w_use(ti, li, which):
        assert wtag.get("s5w") == (ti, li, which), ("s5w content mismatch", wtag.get("s5w"), (ti, li, which))

    def acquire(name, me):
        while locks.get(name) not in (None, me):
            yield "BLOCK"
        locks[name] = me

    def release(name, me):
        assert locks.get(name) == me
        locks[name] = None

    def nxt(ti, li):
        if li + 1 < len(layers):
            return layers[li + 1]
        if ti + 1 < n_tiles:
            return layers[0]
        return None

    def ssm_gen(ti, h, li, l):
        d = L[l]
        si = l // 2
        t = tk(h)
        n0 = h * NHC
        r0, r1 = h * 32, h * 32 + 32
        me = (ti, li, h)
        np_ = nxt_pos(ti, li)
        rmsnorm(h)
        norm_apply(h, li, hB[:, :, t], ("hB", h))
        yield
        sl = use(ti, li, "E1")
        ev_g = lambda mt, pb, pk: actf(gB[:, mt:mt + 2, t], pb[:, :].rearrange("p (a b) -> p a b", a=2), AF.Silu, [pk], [("gB", h)])
        mm_fm(h, sl, hB, ("hB", h), range(0, 4), ev_g)
        yield
        mm_fm(h, sl, hB, ("hB", h), range(4, 8), ev_g)
        if h == 1 and np_ is not None:
            load_role(np_[0], np_[1], "E1")
        yield
        uA = tokA.rearrange("p a b -> p (a b)").rearrange("p (g j c) -> p g j c", g=64, j=8, c=16)
        if h == 0:
            while not udone.get((ti, li)):
                yield "BLOCK"
        else:
            sl = use(ti, li, "E2")
            for j in range(8):
                for nt in range(2):
                    pb, pk = bankf()
                    for kc in range(8):
                        mm(pb[0:64, :], hB[:, kc, j:T:8], wsl[sl][:, kc, nt * 512:(nt + 1) * 512], kc == 0, kc == 7,
                           [("hB", 0), ("hB", 1), ("wsl", sl)], [pk])
                    copy(ev_eng(), uA[0:64, nt * 32:(nt + 1) * 32, j, :],
                         pb[0:64, :].rearrange("p (g c) -> p g c", g=32), [pk], [("tokA", 0), ("tokA", 1)])
                if j % 2 == 1 and j < 7:
                    yield
            udone[(ti, li)] = True
            load_role(ti, li, "OUT")
        yield
        for gq in range(2):
            pb, pk = bankb()
            for g32 in range(32):
                g = gq * 32 + g32
                tr(pb[:, g32 * 32:(g32 + 1) * 32], uA[r0:r1, g, :, :].rearrange("p j c -> p (j c)"), identb[r0:r1, r0:r1],
                   [("tokA", h), "identb"], [pk])
            copy(ev_eng(), XC[:, gq * 32:(gq + 1) * 32, n0:n0 + NHC], pb[:, :].rearrange("p (g n) -> p g n", g=32), [pk], [("XC", h)])
        yield
        if h == 1:
            while not udone.get((ti, li, "s5")):
                yield "BLOCK"
        for _ in acquire("SS", me):
            yield _
        s5w_use(ti, li, "WS3")
        for gq in range(8):
            pb, pk = bankf()
            for g8 in range(8):
                g = gq * 8 + g8
                for s_ in range(2):
                    c0 = (g8 * 2 + s_) * NHC
                    mm(pb[:, c0:c0 + NHC], WS3[:, g, s_ * 64:s_ * 64 + 128], XC[:, g, n0:n0 + NHC], True, True,
                       ["s5w", ("XC", h)], [pk])
            copy(ev_eng(), SS[:, :, :, gq * 8:(gq + 1) * 8],
                 pb[:, :].rearrange("p (g s n) -> p n s g", g=8, s=2), [pk], ["SS"])
        if h == 1:
            s5w_load(ti, li, "WHK")
        else:
            udone[(ti, li, "s5")] = True
        yield
        for _ in acquire("HP", me):
            yield _
        HPv = HP.rearrange("p g n -> p n g")
        t1 = T1(h)[:, 0:128].rearrange("p (s g) -> p s g", s=2)
        t2 = T2(h)[:, 0:128].rearrange("p (s g) -> p s g", s=2)
        CA = CAB[:, si, 0, :, :]
        CB = CAB[:, si, 1, :, :]
        kq = [("CAB", si)]
        copy("act", HPv[:, n0, :], HHc[:, si, 0, :], ["HHc"], ["HP"])
        for n in range(NHC):
            prev = HHc[:, si, :, :] if n == 0 else SS[:, n - 1, :, :]
            pk_ = "HHc" if n == 0 else "SS"
            tt("dve", t1, prev, CA, ALU.mult, [pk_] + kq, [("t1", h)])
            tt("dve", t2, prev[:, ::-1, :], CB, ALU.mult, [pk_] + kq, [("t2", h)])
            tt("dve", t1, t1, t2, ALU.add, [("t1", h), ("t2", h)], [("t1", h)])
            tt("dve", SS[:, n, :, :], SS[:, n, :, :], t1, ALU.add, [("t1", h), "SS"], ["SS"])
            if n % 4 == 3 and n < NHC - 1:
                yield
        copy("act", HPv[:, n0 + 1:n0 + NHC, :], SS[:, 0:NHC - 1, 0, :], ["SS"], ["HP"])
        copy("pool", HHc[:, si, :, :], SS[:, NHC - 1, :, :], ["SS"], ["HHc"])
        release("SS", me)
        yield
        while wtag.get("s5w") != (ti, li, "WHK"):
            yield "BLOCK"
        zA = tokA
        for gq in range(16):
            pb, pk = bankf()
            for g4 in range(4):
                g = gq * 4 + g4
                mm(pb[0:32, g4 * 128:(g4 + 1) * 128], HP[:, g, n0:n0 + NHC], WHK[:, 0, g, :], True, False, ["HP", "s5w"], [pk])
                mm(pb[0:32, g4 * 128:(g4 + 1) * 128], XC[:, g, n0:n0 + NHC], WHK[:, 1, g, :], False, True, [("XC", h), "s5w"], [pk])
            pv = pb[0:32, :].rearrange("p (g j c) -> p j g c", g=4, j=8)
            actf(zA[r0:r1, :, gq * 64:(gq + 1) * 64].rearrange("p j (g c) -> p j g c", g=4), pv,
                 AF.Gelu_apprx_tanh, [pk], [("tokA", h)])
            if gq % 4 == 3 and gq < 15:
                yield
        release("HP", me)
        if h == 1 and np_ is not None:
            s5w_load(np_[0], np_[1], "WS3" if layers[np_[1]] % 2 == 0 else "KV")
        yield
        for jp in range(2):
            pb, pk = bankb()
            for jj in range(4):
                j = jp * 4 + jj
                for kc in range(8):
                    c0 = (jj * 8 + kc) * 32
                    tr(pb[:, c0:c0 + 32], zA[r0:r1, j, kc * 128:(kc + 1) * 128], identb[r0:r1, r0:r1],
                       [("tokA", h), "identb"], [pk])
            for jj in range(4):
                j = jp * 4 + jj
                copy(ev_eng(), zB[:, :, h * HT + j:(h + 1) * HT:8], pb[:, jj * 256:(jj + 1) * 256].rearrange("p (k n) -> p k n", k=8),
                     [pk], [("zB", h)])
        yield
        sl = use(ti, li, "M")

        def ev_glu(mt, pb, pk):
            sb = sigb[:, :, t]
            for i in range(2):
                S.op("act", lambda e, i=i: e.activation(out=sb[:, i, :], in_=pb[:, i * HT:(i + 1) * HT], func=AF.Sigmoid,
                                                        bias=bgluc[:, si, mt + i:mt + i + 1], scale=1.0),
                     [pk, ("bgluc", si)], [("sigb", h)])
            gv = gB[:, mt:mt + 2, t]
            tt("dve", gv, gv, sb, ALU.mult, [("sigb", h), ("gB", h)], [("gB", h)])
            tt("dve", gv, gv, zB[:, mt:mt + 2, t], ALU.mult, [("zB", h), ("gB", h)], [("gB", h)])
        mm_fm(h, sl, zB, ("zB", h), range(0, 4), ev_glu)
        yield
        mm_fm(h, sl, zB, ("zB", h), range(4, 8), ev_glu)
        if h == 1 and np_ is not None:
            load_role(np_[0], np_[1], "E2")
        yield
        sl = use(ti, li, "OUT")
        out_proj(h, sl, range(0, 4))
        yield
        out_proj(h, sl, range(4, 8))
        if h == 1 and np_ is not None:
            load_role(np_[0], np_[1], "M")
        yield

    def wait_slot(ti, li, role):
        X, Y, Z = roles(ti, li)
        slot = {"E1": Y, "E2": X, "M": Z, "OUT": X}[role]
        while wtag.get(slot) != (ti, li, role):
            yield "BLOCK"

    def att_gen(ti, h, li, l):
        d = L[l]
        si = l // 2
        t = tk(h)
        t0 = ti * T + h * HT
        me = (ti, li, h)
        np_ = nxt_pos(ti, li)
        X, Y, Z = roles(ti, li)
        S.dma("sp", cosT[:, t], ropeC[:, t0:t0 + HT], R=[("rope", "c")], W=[("cosT", h)], semkey="rope%d" % h)
        S.dma("sp", sinT[:, t], ropeS[:, t0:t0 + HT], R=[("rope", "s")], W=[("sinT", h)], semkey="rope%d" % h)
        rmsnorm(h)
        norm_apply(h, li, hB[:, :, t], ("hB", h))
        if h == 0:
            copy("pool", KT[:, :, 0:128], KTc[:, si, :, :], ["KTc"], [("KT", 0)])
            copy("pool", Vpad[:, 0, :, :, :], Vc[:, si, :, :, :], ["Vc"], [("Vpad", 0)])
        yield

        def rope_pair2(dst, dkey, wA, wB, wkeys):
            pa, pka = bankf()
            pbb, pkb = bankf()
            for i in range(2):
                for kc in range(8):
                    mm(pa[:, i * HT:(i + 1) * HT], wA(i, kc), hB[:, kc, t], kc == 0, kc == 7, [("hB", h)] + wkeys, [pka])
            for i in range(2):
                for kc in range(8):
                    mm(pbb[:, i * HT:(i + 1) * HT], wB(i, kc), hB[:, kc, t], kc == 0, kc == 7, [("hB", h)] + wkeys, [pkb])
            cb = cosT[:, t].unsqueeze(1).broadcast_to([128, 2, HT])
            sb_ = sinT[:, t].unsqueeze(1).broadcast_to([128, 2, HT])
            sc3 = T12(h).rearrange("p (a b) -> p a b", a=2)
            tkeys = [("t1", h), ("t2", h)]
            tt("dve", sc3, pa[:, :].rearrange("p (a b) -> p a b", a=2), cb, ALU.mult, [pka, ("cosT", h)], tkeys)
            tt("dve", dst, pbb[:, :].rearrange("p (a b) -> p a b", a=2), sb_, ALU.mult, [pkb, ("sinT", h)], dkey)
            tt("dve", dst, dst, sc3, ALU.add, tkeys + dkey, dkey)

        for _ in wait_slot(ti, li, "E2"):
            yield _
        for _ in wait_slot(ti, li, "E1"):
            yield _
        for m2 in range(4):
            rope_pair2(zB[:, 2 * m2:2 * m2 + 2, t], [("zB", h)],
                       lambda i, kc, m2=m2: wsl[X][:, kc, (2 * m2 + i) * 128:(2 * m2 + i + 1) * 128],
                       lambda i, kc, m2=m2: wsl[Y][:, kc, (2 * m2 + i) * 128:(2 * m2 + i + 1) * 128], [("wsl", X), ("wsl", Y)])
            if m2 < 3:
                yield
        if h == 1:
            load_role(ti, li, "OUT")
            if np_ is not None:
                load_role(np_[0], np_[1], "E1")
        yield
        while wtag.get("s5w") != (ti, li, "KV"):
            yield "BLOCK"
        kslots = [("KT", 1 + 2 * h), ("KT", 2 + 2 * h)]
        rope_pair2(KT[:, :, 128 + h * HT:128 + (h + 1) * HT], kslots,
                   lambda i, kc: kvw[:, kc, i * 128:(i + 1) * 128],
                   lambda i, kc: kvw[:, kc, 256 + i * 128:256 + (i + 1) * 128], ["s5w"])
        for b2 in range(2):
            blk = 2 * h + b2
            pb, pk = bankf()
            for kc in range(8):
                mm(pb[:, 0:128], hB[:, kc, blk * 128:(blk + 1) * 128], kvw[:, kc, 512:640], kc == 0, kc == 7, [("hB", h), "s5w"], [pk])
            pv = pb[:, 0:128].rearrange("p (g d) -> p g d", g=2)
            copy("act", Vpad[:, blk + 1, :, 0, 0:64], pv, [pk], [("Vpad", blk + 1)])
            copy("act", Vpad[:, blk + 1, :, 1, 64:128], pv, [pk], [("Vpad", blk + 1)])
        if h == 1 and np_ is not None:
            s5w_load(np_[0], np_[1], "WS3" if layers[np_[1]] % 2 == 0 else "KV")
        yield
        for _ in wait_slot(ti, li, "M"):
            yield _
        ev_g = lambda mt, pb, pk: actf(gB[:, mt:mt + 2, t], pb[:, :].rearrange("p (a b) -> p a b", a=2), AF.Silu, [pk], [("gB", h)])
        mm_fm(h, Z, hB, ("hB", h), range(0, 4), ev_g)
        yield
        mm_fm(h, Z, hB, ("hB", h), range(4, 8), ev_g)
        if h == 1 and np_ is not None:
            load_role(np_[0], np_[1], "E2")
        yield
        for b2 in range(2):
            blk = 2 * h + b2
            first = (ti == 0 and blk == 0)
            for g in range(2):
                combos = [(par, kb) for par in range(2) for kb in ((1,) if first else (0, 1))]
                pts = []
                for ci_, (par, kb) in enumerate(combos):
                    s_ = blk + kb
                    pb, pk = bankf()
                    lo, hi = par * 64, (par + 1) * 64
                    mm(pb[:, :], KT[lo:hi, g, s_ * 128:(s_ + 1) * 128], zB[lo:hi, g * 4:(g + 1) * 4, blk * 128:(blk + 1) * 128],
                       True, False, [("KT", s_), ("zB", h)], [pk])
                    mm(pb[:, :], identb, mask2[:, kb, :].unsqueeze(1).broadcast_to([128, 4, 128]),
                       False, True, ["identb", "mask2"], [pk])
                    slot = h * 4 + ci_
                    actf(PT[:, slot, :], pb[:, :], AF.Exp, [pk], [("PT", slot)], scale=0.125)
                    pts.append((par, s_, slot))
                po, pko = bankf()
                pd, pkd = bankf()
                for i_, (par, s_, slot) in enumerate(pts):
                    mm(po[:, :], Vpad[:, s_, g, par, :], PT[:, slot, :], i_ == 0, i_ == len(pts) - 1, [("Vpad", s_), ("PT", slot)], [pko])
                for i_, (par, s_, slot) in enumerate(pts):
                    mm(pd[:, :], ones2[:, par, :], PT[:, slot, :], i_ == 0, i_ == len(pts) - 1, ["ones2", ("PT", slot)], [pkd])
                sc = T12(h)
                sc4 = sc.rearrange("p (k q) -> p k q", k=4)
                tkeys = [("t1", h), ("t2", h)]
                tt("dve", sc4, pd[:, :].rearrange("p (k q) -> p k q", k=4),
                   sinke[:, si, g * 4:(g + 1) * 4].unsqueeze(2).broadcast_to([128, 4, 128]), ALU.add,
                   [pkd, ("sinke", si)], tkeys)
                actf(sc, sc, AF.Ln, tkeys, tkeys)
                actf(sc, sc, AF.Exp, tkeys, tkeys, scale=-1.0)
                tt("dve", sc, po[:, :], sc, ALU.mult, [pko] + tkeys, tkeys)
                gv = gB[:, g * 4:(g + 1) * 4, blk * 128:(blk + 1) * 128]
                tt("dve", gv, gv, sc4, ALU.mult, tkeys + [("gB", h)], [("gB", h)])
                if not (g == 1 and b2 == 1):
                    yield
            if b2 == 1 and h == 1:
                copy("pool", KTc[:, si, :, :], KT[:, :, 512:640], [("KT", 4)], ["KTc"])
                copy("pool", Vc[:, si, :, :, :], Vpad[:, 4, :, :, :], [("Vpad", 4)], ["Vc"])
        yield
        for _ in wait_slot(ti, li, "OUT"):
            yield _
        out_proj(h, X, range(0, 4))
        yield
        out_proj(h, X, range(4, 8))
        if h == 1 and np_ is not None:
            load_role(np_[0], np_[1], "M")
        yield

    def tile_gen(ti, h):
        S.epoch = ti
        t = tk(h)
        t0 = ti * T + h * HT
        me = (ti, "io", h)
        for _ in acquire("HP", me):
            yield _
        for b2 in range(2):
            blk = 2 * h + b2
            xs = xtok[:, b2, :]
            S.dma("sp", xs, x[t0 + b2 * 128:t0 + (b2 + 1) * 128, :], W=["HP"], semkey="xin")
            for hh in range(2):
                pb, pk = bankf()
                for q in range(4):
                    kc = hh * 4 + q
                    tr(pb[:, q * 128:(q + 1) * 128], xs[:, kc * 128:(kc + 1) * 128], identf, ["HP", "identf"], [pk])
                copy(ev_eng(), xB[:, hh * 4:(hh + 1) * 4, blk * 128:(blk + 1) * 128],
                     pb[:, :].rearrange("p (k t) -> p k t", k=4), [pk], [("xB", h)])
        release("HP", me)
        yield
        for li, l in enumerate(layers):
            g_ = ssm_gen(ti, h, li, l) if l % 2 == 0 else att_gen(ti, h, li, l)
            for v_ in g_:
                yield v_
        for _ in acquire("SS", me):
            yield _
        rmsnorm(h)
        yh = yB[:, :, t]
        norm_apply(h, 4, yh, "SS")
        yield
        for _ in acquire("HP", me):
            yield _
        for b2 in range(2):
            blk = 2 * h + b2
            xs = xtok[:, b2, :]
            for hh in range(2):
                pb, pk = bankf()
                for q in range(4):
                    kc = hh * 4 + q
                    tr(pb[:, q * 128:(q + 1) * 128], yB[:, kc, blk * 128:(blk + 1) * 128], identf, ["SS", "identf"], [pk])
                copy(ev_eng(), xs[:, hh * 512:(hh + 1) * 512], pb[:, :], [pk], ["HP"])
            S.dma("sp", out[t0 + b2 * 128:t0 + (b2 + 1) * 128, :], xs, R=["HP"], W=["out"], semkey="xin")
        release("HP", me)
        release("SS", me)
        yield

    if n_tiles > 0 and NL > 0:
        for r_ in ("E1", "E2", "M"):
            load_role(0, 0, r_)
        s5w_load(0, 0, "WS3" if layers[0] % 2 == 0 else "KV")
    LAG = 4

    def chain(h):
        for ti in range(n_tiles):
            for v_ in tile_gen(ti, h):
                yield v_

    gens = [chain(0), chain(1)]
    alive = [True, True]
    step = 0
    blocked = 0
    while any(alive):
        progressed = False
        for hh_ in range(2):
            if not alive[hh_]:
                continue
            if hh_ == 1 and step < LAG and alive[0]:
                continue
            try:
                v_ = next(gens[hh_])
                if v_ != "BLOCK":
                    progressed = True
            except StopIteration:
                alive[hh_] = False
                progressed = True
        step += 1
        blocked = 0 if progressed else blocked + 1
        assert blocked < 1000, ("interleave deadlock", wtag, locks)
    S.barrier()

    with nc.Block() as block:
        S.emit(block)
    stack.close()
    return nc


def host_inputs(inputs, layers=(0, 1, 2, 3)):
    f32 = np.float32
    m = {}
    m["c_identf"] = np.eye(128, dtype=f32)
    q = np.arange(128)
    kj = np.arange(128)
    mprev = (kj[:, None] > q[None, :]).astype(f32)
    mcur = (kj[:, None] <= q[None, :]).astype(f32)
    m["c_mask2"] = np.ascontiguousarray(np.stack([mprev, mcur], axis=1))
    o2 = np.zeros((128, 2, 128), f32)
    o2[:, 0, 0:64] = 1.0
    o2[:, 1, 64:128] = 1.0
    m["c_ones2"] = o2
    ii = np.arange(128) // 16
    m["c_tri"] = (ii[None, :] >= ii[:, None]).astype(f32)
    m["c_fidx"] = (np.arange(128) % 32).astype(f32).reshape(128, 1)
    m["c_sgn"] = np.where((np.arange(128) % 64) < 32, -1.0, 1.0).astype(f32).reshape(128, 1)
    m["c_pos"] = np.arange(SEQ, dtype=f32)

    def col(v):
        return np.ascontiguousarray(np.asarray(v, f32).reshape(8, 128).T)

    m["fnorm_col"] = col(inputs["final_norm"])
    for l in layers:
        p = "l%d_" % l
        m[p + "norm_col"] = col(inputs[p + "norm"])
        if l % 2 == 0:
            m[p + "w_in"] = np.ascontiguousarray(inputs[p + "w_in"], f32)
            m[p + "w_glu"] = np.ascontiguousarray(inputs[p + "w_glu"], f32)
            m[p + "w_out"] = np.ascontiguousarray(inputs[p + "w_out"], f32)
            m[p + "bglu_col"] = col(inputs[p + "b_glu"])
            m[p + "aRe"] = np.ascontiguousarray(np.tile(np.asarray(inputs[p + "a_re"], f32).T, (2, 1)))
            m[p + "aIm"] = np.ascontiguousarray(np.tile(np.asarray(inputs[p + "a_im"], f32).T, (2, 1)))
            m[p + "ls"] = np.ascontiguousarray(np.tile(np.asarray(inputs[p + "log_step"], f32)[None, :], (128, 1)))
            m[p + "dcol"] = np.ascontiguousarray(np.tile(np.asarray(inputs[p + "d"], f32).reshape(64, 16).T, (8, 1)))
            m[p + "cRe"] = np.ascontiguousarray(np.tile(np.asarray(inputs[p + "c_re"], f32).transpose(2, 0, 1), (2, 1, 1)))
            m[p + "cIm"] = np.ascontiguousarray(np.tile(np.asarray(inputs[p + "c_im"], f32).transpose(2, 0, 1), (2, 1, 1)))
            m[p + "bRe"] = np.ascontiguousarray(np.tile(np.asarray(inputs[p + "b_re"], f32).transpose(1, 0, 2), (2, 1, 1)))
            m[p + "bIm"] = np.ascontiguousarray(np.tile(np.asarray(inputs[p + "b_im"], f32).transpose(1, 0, 2), (2, 1, 1)))
        else:
            w = np.asarray(inputs[p + "w_in"], f32)
            m[p + "w_in"] = np.ascontiguousarray(w)
            dd = np.arange(64)
            partner = np.where(dd < 32, dd + 32, dd - 32)
            qperm = (np.arange(16)[:, None] * 64 + partner[None, :]).reshape(-1)
            m[p + "w_qp"] = np.ascontiguousarray(w[:, qperm])
            k0 = w[:, 1024:1088]
            k1 = w[:, 1088:1152]
            m[p + "w_kk"] = np.ascontiguousarray(np.concatenate(
                [k0, k0, k1, k1, k0[:, partner], k0[:, partner], k1[:, partner], k1[:, partner]], axis=1))
            m[p + "w_out"] = np.ascontiguousarray(inputs[p + "w_out"], f32)
            s = np.asarray(inputs[p + "sinks"], f32)
            sc = np.zeros((128, 8), f32)
            for kc in range(8):
                sc[0:64, kc] = s[2 * kc]
                sc[64:128, kc] = s[2 * kc + 1]
            m[p + "sink_col"] = sc
    return m


INPUT_NAMES = (
    "x",
    "l0_norm", "l0_w_in", "l0_a_re", "l0_a_im", "l0_log_step", "l0_b_re", "l0_b_im", "l0_c_re", "l0_c_im", "l0_d",
    "l0_w_glu", "l0_b_glu", "l0_w_out",
    "l1_norm", "l1_w_in", "l1_sinks", "l1_w_out",
    "l2_norm", "l2_w_in", "l2_a_re", "l2_a_im", "l2_log_step", "l2_b_re", "l2_b_im", "l2_c_re", "l2_c_im", "l2_d",
    "l2_w_glu", "l2_b_glu", "l2_w_out",
    "l3_norm", "l3_w_in", "l3_sinks", "l3_w_out",
    "final_norm",
)

_NC_CACHE = {}


def kernel(**inputs):
    layers = (0, 1, 2, 3)
    missing = [n for n in INPUT_NAMES if n not in inputs]
    assert not missing, missing
    if "full" not in _NC_CACHE:
        _NC_CACHE["full"] = build(16, layers)
    nc = _NC_CACHE["full"]
    shared = host_inputs(inputs, layers)
    xs = np.asarray(inputs["x"], np.float32)
    in_maps = []
    for c in range(8):
        mc = dict(shared)
        mc["x"] = np.ascontiguousarray(xs[c])
        in_maps.append(mc)
    res = run_bass_kernel_spmd(nc, in_maps, core_ids=list(range(8)))
    return np.stack([np.asarray(r["out"], np.float32) for r in res.results], axis=0)
```
